# Optimizing a Trainium2 kernel written in Bass

```python
import jax, jax.numpy as jnp
from jax import lax
import numpy as np

D_MODEL = 2048
BATCH = 8
SEQ = 2048
DEPTH = 2

HEAD_DIM = 128
ATTN_WIDTH = D_MODEL // 2
N_Q_HEADS = ATTN_WIDTH // HEAD_DIM
N_KV_HEADS = max(1, N_Q_HEADS // 4)
GQA_GROUP = N_Q_HEADS // N_KV_HEADS
KV_WIDTH = N_KV_HEADS * HEAD_DIM
WINDOW = 128
BLOCK = 128
GMLP_WIDTH = D_MODEL - ATTN_WIDTH
GMLP_HEAD_DIM = 128
N_GMLP_HEADS = GMLP_WIDTH // GMLP_HEAD_DIM
CHUNK = 128
IN_WIDTH = ATTN_WIDTH + 2 * KV_WIDTH + 2 * GMLP_WIDTH
D_FF = 5632
CONV_WIDTH = 3
ROPE_THETA = 10000.0
EPS = 1e-6
MASK_VALUE = -1e30

kernel_name = "hybrid_window_gqa_sgu_convffn_encoder"


def rms_norm(x, g):
    xf = x.astype(jnp.float32)
    y = xf * lax.rsqrt(jnp.mean(xf * xf, axis=-1, keepdims=True) + EPS)
    return (y * g.astype(jnp.float32)).astype(x.dtype)


def layer_norm(x, g, b):
    xf = x.astype(jnp.float32)
    mu = jnp.mean(xf, axis=-1, keepdims=True)
    xc = xf - mu
    y = xc * lax.rsqrt(jnp.mean(xc * xc, axis=-1, keepdims=True) + EPS)
    return (y * g.astype(jnp.float32) + b.astype(jnp.float32)).astype(x.dtype)


def rope_tables(seq):
    inv_freq = ROPE_THETA ** (-jnp.arange(0, HEAD_DIM, 2, dtype=jnp.float32) / HEAD_DIM)
    ang = jnp.arange(seq, dtype=jnp.float32)[:, None] * inv_freq[None, :]
    return jnp.cos(ang), jnp.sin(ang)


def apply_rope(x, cos, sin):
    xf = x.astype(jnp.float32)
    x1, x2 = jnp.split(xf, 2, axis=-1)
    c = cos[None, :, None, :]
    s = sin[None, :, None, :]
    return jnp.concatenate([x1 * c - x2 * s, x2 * c + x1 * s], axis=-1).astype(x.dtype)


def banded_window_attention(q, k, v, sink):
    B, S, _, D = q.shape
    nb = S // BLOCK
    qb = q.reshape(B, nb, BLOCK, N_KV_HEADS, GQA_GROUP, D)

    def band(t):
        tp = jnp.pad(t, ((0, 0), (BLOCK, BLOCK), (0, 0), (0, 0)))
        tp = tp.reshape(B, nb + 2, BLOCK, N_KV_HEADS, D)
        return jnp.concatenate([tp[:, :-2], tp[:, 1:-1], tp[:, 2:]], axis=2)

    kb, vb = band(k), band(v)
    s = jnp.einsum('bnqhgd,bnkhd->bnhgqk', qb, kb).astype(jnp.float32) * (D ** -0.5)
    blk = jnp.arange(nb)[:, None, None]
    q_pos = blk * BLOCK + jnp.arange(BLOCK)[None, :, None]
    k_pos = blk * BLOCK - BLOCK + jnp.arange(3 * BLOCK)[None, None, :]
    valid = (jnp.abs(k_pos - q_pos) <= WINDOW) & (k_pos >= 0) & (k_pos < S)
    s = jnp.where(valid[None, :, None, None], s, MASK_VALUE)
    sk = sink.astype(jnp.float32).reshape(N_KV_HEADS, GQA_GROUP)[None, None, :, :, None, None]
    m = jnp.maximum(jnp.max(s, axis=-1, keepdims=True), sk)
    p = jnp.exp(s - m)
    probs = p / (jnp.sum(p, axis=-1, keepdims=True) + jnp.exp(sk - m))
    out = jnp.einsum('bnhgqk,bnkhd->bnqhgd', probs.astype(v.dtype), vb)
    return out.reshape(B, S, N_Q_HEADS * D)


def chunked_spatial_gating(u, v, ln_g, ln_b, w_s, b_s):
    B, S, _ = u.shape
    nc = S // CHUNK
    vn = layer_norm(v, ln_g, ln_b).reshape(B, nc, CHUNK, N_GMLP_HEADS, GMLP_HEAD_DIM)
    f = jnp.einsum('hpq,bcqhd->bcphd', w_s, vn) + b_s.T[None, None, :, :, None]
    return u * f.reshape(B, S, GMLP_WIDTH)


def depthwise_conv_centred(h, w, b):
    S = h.shape[1]
    half = CONV_WIDTH // 2
    hp = jnp.pad(h, ((0, 0), (half, half), (0, 0)))
    out = b
    for t in range(CONV_WIDTH):
        out = out + hp[:, t:t + S] * w[t]
    return out


def conv_gated_ffn(h, w_up, conv_w, conv_b, w_down):
    a = depthwise_conv_centred(h @ w_up, conv_w, conv_b)
    g, u = jnp.split(a, 2, axis=-1)
    return (jax.nn.silu(g) * u) @ w_down


def setup_inputs(seed: int = 0) -> dict:
    key = jax.random.key(seed)
    ks = jax.random.split(key, 20)
    f32 = jnp.float32
    nrm = lambda k, shape, scale: jax.random.normal(k, shape, f32) * scale
    res_scale = (2.0 * DEPTH) ** -0.5
    return {
        "x": jax.random.normal(ks[0], (BATCH, SEQ, D_MODEL), f32),
        "norm1_g": 1.0 + nrm(ks[1], (DEPTH, D_MODEL), 0.02),
        "w_in": nrm(ks[2], (DEPTH, D_MODEL, IN_WIDTH), D_MODEL ** -0.5),
        "q_norm_g": 1.0 + nrm(ks[3], (DEPTH, HEAD_DIM), 0.02),
        "k_norm_g": 1.0 + nrm(ks[4], (DEPTH, HEAD_DIM), 0.02),
        "sink": nrm(ks[5], (DEPTH, N_Q_HEADS), 0.5),
        "sgu_ln_g": 1.0 + nrm(ks[6], (DEPTH, GMLP_WIDTH), 0.02),
        "sgu_ln_b": nrm(ks[7], (DEPTH, GMLP_WIDTH), 0.02),
        "w_s": nrm(ks[8], (DEPTH, N_GMLP_HEADS, CHUNK, CHUNK), 0.5 * CHUNK ** -0.5),
        "b_s": 1.0 + nrm(ks[9], (DEPTH, N_GMLP_HEADS, CHUNK), 0.02),
        "attn_out_g": 1.0 + nrm(ks[10], (DEPTH, ATTN_WIDTH), 0.02),
        "sgu_out_g": 1.0 + nrm(ks[11], (DEPTH, GMLP_WIDTH), 0.02),
        "w_o": nrm(ks[12], (DEPTH, D_MODEL, D_MODEL), D_MODEL ** -0.5 * res_scale),
        "norm2_g": 1.0 + nrm(ks[13], (DEPTH, D_MODEL), 0.02),
        "w_up": nrm(ks[14], (DEPTH, D_MODEL, 2 * D_FF), D_MODEL ** -0.5),
        "conv_w": nrm(ks[15], (DEPTH, CONV_WIDTH, 2 * D_FF), CONV_WIDTH ** -0.5),
        "conv_b": nrm(ks[16], (DEPTH, 2 * D_FF), 0.01),
        "w_down": nrm(ks[17], (DEPTH, D_FF, D_MODEL), D_FF ** -0.5 * res_scale),
    }


def reference(x, norm1_g, w_in, q_norm_g, k_norm_g, sink, sgu_ln_g, sgu_ln_b, w_s, b_s,
              attn_out_g, sgu_out_g, w_o, norm2_g, w_up, conv_w, conv_b, w_down):
    B, S, _ = x.shape
    cos, sin = rope_tables(S)
    splits = [ATTN_WIDTH, ATTN_WIDTH + KV_WIDTH, ATTN_WIDTH + 2 * KV_WIDTH,
              ATTN_WIDTH + 2 * KV_WIDTH + GMLP_WIDTH]
    for l in range(DEPTH):
        h = rms_norm(x, norm1_g[l])
        q, k, v, gu, gv = jnp.split(h @ w_in[l], splits, axis=-1)
        q = apply_rope(rms_norm(q.reshape(B, S, N_Q_HEADS, HEAD_DIM), q_norm_g[l]), cos, sin)
        k = apply_rope(rms_norm(k.reshape(B, S, N_KV_HEADS, HEAD_DIM), k_norm_g[l]), cos, sin)
        v = v.reshape(B, S, N_KV_HEADS, HEAD_DIM)
        attn = banded_window_attention(q, k, v, sink[l])
        sgu = chunked_spatial_gating(jax.nn.gelu(gu), jax.nn.gelu(gv),
                                     sgu_ln_g[l], sgu_ln_b[l], w_s[l], b_s[l])
        mixed = jnp.concatenate([rms_norm(attn, attn_out_g[l]), rms_norm(sgu, sgu_out_g[l])], axis=-1)
        x = x + mixed @ w_o[l]
        x = x + conv_gated_ffn(rms_norm(x, norm2_g[l]), w_up[l], conv_w[l], conv_b[l], w_down[l])
    return x
```

```python
import contextlib
import numpy as np
import ml_dtypes
import concourse.bass as bass
import concourse.mybir as mybir
from concourse.bass_utils import run_bass_kernel_spmd

F32 = mybir.dt.float32
BF16 = mybir.dt.bfloat16
ALU = mybir.AluOpType
AF = mybir.ActivationFunctionType

D = 2048
NCH = 16
DFF = 5632
NPAIR = 44
HD = 128
NQ = 8
NKV = 2
INW = 3584
EPS = 1e-6
SB_BASE = 16512
SB_CAP = 212863

ENGS = ("pe", "act", "dve", "pool", "sp")
SAME_ENG_DIST = 3
import os
DBGJ = int(os.environ.get("DBGJ", "0"))


class Res:
    __slots__ = ("name", "w", "r")

    def __init__(self, name):
        self.name = name
        self.w = None
        self.r = []


class Chan:
    __slots__ = ("name", "sem", "count")

    def __init__(self, name):
        self.name = name
        self.sem = None
        self.count = 0


class Op:
    __slots__ = ("eng", "fn", "deps", "raw", "pos", "sig", "signo", "chan", "chan_val", "gidx")


class Prog:
    def __init__(self, nc):
        self.nc = nc
        self.ops = []
        self.streams = {e: [] for e in ENGS}
        self.chans = []

    def chan(self, name):
        c = Chan(name)
        self.chans.append(c)
        return c

    def op(self, eng, fn, reads=(), writes=(), chan=None):
        o = Op()
        o.eng = eng
        o.fn = fn
        o.chan = chan
        o.sig = False
        o.signo = None
        o.chan_val = None
        deps = set()
        for r in reads:
            if r.w is not None:
                deps.add(r.w)
        o.raw = set(deps)
        for r in writes:
            if r.w is not None:
                deps.add(r.w)
            deps.update(r.r)
        deps.discard(o)
        while any(d.fn is None for d in deps):
            nd = set()
            for d in deps:
                if d.fn is None:
                    nd.update(d.deps)
                else:
                    nd.add(d)
            deps = nd
        o.deps = deps
        for r in reads:
            r.r.append(o)
        for r in writes:
            r.w = o
            r.r = []
        o.pos = len(self.streams[eng])
        o.gidx = len(self.ops)
        self.streams[eng].append(o)
        self.ops.append(o)
        if chan is not None:
            chan.count += 16
            o.chan_val = chan.count
        return o

    def dma(self, eng, out, in_, reads=(), writes=(), chan=None):
        assert chan is not None
        return self.op(eng, lambda e: e.dma_start(out=out, in_=in_), reads, writes, chan)

    def _needs_wait(self, x, c):
        if c.chan is not None:
            return True
        if c.eng != x.eng:
            return True
        if c.eng == "pe":
            return False
        return True

    def emit(self):
        nc = self.nc
        for x in self.ops:
            for c in x.deps:
                if c.chan is None and self._needs_wait(x, c):
                    c.sig = True
        for e in ENGS:
            n = 0
            for o in self.streams[e]:
                if o.sig:
                    n += 1
                    o.signo = n
        with contextlib.ExitStack() as st:
            esem = {e: st.enter_context(nc.semaphore("s_" + e)) for e in ENGS}
            for ci, c in enumerate(self.chans):
                if c.count:
                    c.sem = st.enter_context(nc.semaphore(f"c{ci}_" + c.name))
            block = st.enter_context(nc.Block())

            def run(ename):
                def body(eng):
                    seen_e = {e: 0 for e in ENGS}
                    seen_c = {}
                    for o in self.streams[ename]:
                        need_c = {}
                        need_e = {}
                        for c in o.deps:
                            if not self._needs_wait(o, c):
                                continue
                            if c.chan is not None:
                                if c.chan_val > need_c.get(c.chan, 0):
                                    need_c[c.chan] = c.chan_val
                            else:
                                if c.signo > need_e.get(c.eng, 0):
                                    need_e[c.eng] = c.signo
                        for ch, v in need_c.items():
                            if seen_c.get(ch, 0) >= v:
                                continue
                            eng.wait_ge(ch.sem, v)
                            seen_c[ch] = v
                        for en, v in need_e.items():
                            if seen_e[en] >= v:
                                continue
                            eng.wait_ge(esem[en], v)
                            seen_e[en] = v
                        if o.fn is None:
                            continue
                        ins = o.fn(eng)
                        if o.chan is not None:
                            ins.then_inc(o.chan.sem, 16)
                        elif o.sig:
                            ins.then_inc(esem[ename], 1)

                return body

            block.tensor(run("pe"))
            block.scalar(run("act"))
            block.vector(run("dve"))
            block.gpsimd(run("pool"))
            block.sync(run("sp"))


class Slots:
    def __init__(self, P, alloc, name, n, shape, dtype):
        self.t = [alloc(f"{name}{i}", shape, dtype) for i in range(n)]
        self.r = [Res(f"{name}{i}") for i in range(n)]
        self.c = [P.chan(f"{name}{i}") for i in range(n)]
        self.c2 = [P.chan(f"{name}s{i}") for i in range(n)]
        self.i = 0
        self.n = n
        self.free = list(range(n))

    def next(self):
        k = self.i % self.n
        self.i += 1
        return self.t[k], self.r[k], self.c[k]

    def can(self, m):
        return len(self.free) >= m

    def take(self, m):
        ks = [self.free.pop(0) for _ in range(m)]
        return [(k, self.t[k], self.r[k], self.c[k]) for k in ks]

    def give(self, items):
        for it in items:
            self.free.append(it[0])


def build(S, NL, dbg=False):
    assert S % 512 == 0
    NT = S // 512
    NB = S // 128
    nc = bass.Bass("TRN2", target_bir_lowering=False)
    P = Prog(nc)

    def din(name, shape, dt=F32):
        return nc.dram_tensor(name, list(shape), dt, kind="ExternalInput").ap()

    xT_in = din("xT", [S // 512, 128, NCH, 512])
    w_in_d = din("w_in", [NL, 128, NCH, INW])
    w_o_d = din("w_o", [NL, NCH, 128, NCH, 128])
    w_up_d = din("w_up", [NL, NPAIR, 128, NCH, 2, 128])
    w_dn_d = din("w_dn", [NL, 8, 128, NPAIR, 256])
    n1g_d = din("n1g", [NL, 128, NCH])
    n2g_d = din("n2g", [NL, 128, NCH])
    aog_d = din("aog", [NL, 128, 8])
    sog_d = din("sog", [NL, 128, 8])
    cw_d = din("cw", [NL, 128, 3, 88])
    cb_d = din("cb", [NL, 128, 88])
    qg_d = din("qg", [NL, 128])
    kg_d = din("kg", [NL, 128])
    sink_d = din("sink", [NL, 8])
    lng_d = din("lng", [NL, 1024])
    lnb_d = din("lnb", [NL, 1024])
    bs_d = din("bs", [NL, 1024])
    ws_d = din("wsT", [NL, 128, 8, 128])
    c_cc = din("c_cc", [NB, 128, 4, 128])
    c_ss = din("c_ss", [NB, 128, 4, 128])
    c_mask = din("c_mask", [2, 128, 512], BF16)
    c_ones = din("c_ones", [128, 128], BF16)
    c_idb = din("c_idb", [128, 128], BF16)
    outT = nc.dram_tensor("outT", [S // 512, 128, NCH, 512], F32, kind="ExternalOutput").ap()
    xa = nc.dram_tensor("xa_scr", [S // 512, 128, NCH, 512], F32).ap()
    xb = nc.dram_tensor("xb_scr", [S // 512, 128, NCH, 512], F32).ap()
    xn_scr = nc.dram_tensor("xn_scr", [128, S // 512, NCH, 512], BF16).ap()
    wsc_in = nc.dram_tensor("wsc_in", [128, NCH, INW], BF16).ap()
    wsc_o = nc.dram_tensor("wsc_o", [NCH, 128, NCH, 128], BF16).ap()
    dbg_out = {}
    if dbg:
        for nm, shp in (("d_kt", [128, NKV, S]), ("d_v", [128, NB, 256]),
                        ("d_mix", [128, NCH, 512]), ("d_xb", [S // 512, 128, NCH, 512])):
            dbg_out[nm] = nc.dram_tensor(nm, shp, F32, kind="ExternalOutput").ap()

    def fm(ap):
        return ap.rearrange("(c p) s -> p c s", p=128)

    class Arena:
        def __init__(self, base):
            self.off = base
            self.hi = base

        def __call__(self, name, shape, dtype):
            nb = int(np.prod(shape[1:])) * (2 if dtype == BF16 else 4)
            nb = (nb + 63) // 64 * 64
            t = nc.alloc_sbuf_tensor_at(name, list(shape), dtype, offset=self.off)
            self.off += nb
            self.hi = max(self.hi, self.off)
            assert self.off <= SB_BASE + SB_CAP, (name, self.off - SB_BASE)
            return t

    A = Arena(SB_BASE)
    fence_t = [None]
    ones_bf = A("ones_bf", [128, 128], BF16)
    id_bf = A("id_bf", [128, 128], BF16)
    mask4 = A("mask4", [128, 2, 512], BF16)
    eps_t = A("eps_t", [128, 1], F32)
    fence_t[0] = A("fence_t", [128, 1], F32)
    n1g = A("n1g", [128, NCH], F32)
    n2g = A("n2g", [128, NCH], F32)
    aosog = A("aosog", [128, 16], F32)
    cw = A("cw", [128, 3, 88], F32)
    cb = A("cb", [128, 88], F32)
    qg_bc = A("qg_bc", [128, 128], F32)
    kg_bc = A("kg_bc", [128, 128], F32)
    esink = A("esink", [128, 8], F32)
    lng_bc = A("lng_bc", [128, 1024], F32)
    lnb_bc = A("lnb_bc", [128, 1024], F32)
    bs_bc = A("bs_bc", [128, 1024], F32)
    wsT = A("wsT", [128, 8, 128], BF16)
    r_const = Res("const")
    c_const = P.chan("const")
    c_ws = P.chan("ws")
    rstd = A("rstd", [128, 512], F32)
    r_rstd = Res("rstd")
    tmp512 = A("tmp512", [128, 512], F32)
    r_tmp512 = Res("tmp512")
    sq = Slots(P, A, "sq", 5, [128, 512], BF16)
    xo = Slots(P, A, "xo", 2, [128, 512], F32)
    halo = A("halo", [128, S // 512, NCH, 2], F32)
    r_halo = Res("halo")
    phase_base = A.off

    pb = [nc.alloc_psum_tensor(f"pb{i}", [128, 512], F32) for i in range(8)]
    pbb = [t.bitcast(BF16) for t in pb]
    r_pb = [Res(f"pb{i}") for i in range(8)]

    class Rot:
        def __init__(self, ids):
            self.ids = ids
            self.i = 0

        def next(self):
            k = self.ids[self.i % len(self.ids)]
            self.i += 1
            return k

    def mm(out, lhsT, rhs, start, stop, reads, writes, skip=False):
        if skip:
            P.op("pe", lambda e: e.matmul(out, lhsT, rhs, start=start, stop=stop, skip_group_check=True), reads, writes)
        else:
            P.op("pe", lambda e: e.matmul(out, lhsT, rhs, start=start, stop=stop), reads, writes)

    def fence(reads, writes):
        P.op("dve", lambda e: e.memset(fence_t[0][:], 0.0), reads, writes)

    def act(out, in_, func, reads, writes, **kw):
        P.op("act", lambda e: e.activation(out, in_, func, **kw), reads, writes)

    def tt(out, a, b, op, reads, writes, eng="dve"):
        P.op(eng, lambda e: e.tensor_tensor(out, a, b, op), reads, writes)

    def stt(out, in0, scalar, in1, op0, op1, reads, writes):
        P.op("dve", lambda e: e.scalar_tensor_tensor(out, in0, scalar, in1, op0, op1), reads, writes)

    def ts(out, in0, s1, s2, op0, op1, reads, writes):
        P.op("dve", lambda e: e.tensor_scalar(out, in0, s1, s2, op0, op1), reads, writes)

    def recip(out, in_, reads, writes):
        P.op("dve", lambda e: e.reciprocal(out, in_), reads, writes)

    def rsqrt_chain(out, in_, scale, tmp, reads, writes, r_tmp):
        act(tmp, in_, AF.Sqrt, reads + [r_const], [r_tmp], bias=eps_t[:, 0:1], scale=scale)
        recip(out, tmp, [r_tmp], writes)

    P.dma("sp", ones_bf[:], c_ones, writes=[r_const], chan=c_const)
    P.dma("sp", id_bf[:], c_idb, writes=[r_const], chan=c_const)
    P.dma("sp", mask4[:, 0, :], c_mask[0], writes=[r_const], chan=c_const)
    P.dma("sp", mask4[:, 1, :], c_mask[1], writes=[r_const], chan=c_const)
    P.op("dve", lambda e: e.memset(eps_t[:], EPS), [], [r_const])

    def norm_tile(src_t, r_src, ncols, gains, dst_fn, r_dst, rot, src_fn=None):
        if src_fn is None:
            src_fn = lambda c: src_t[:, c, 0:ncols]
        bk = rot.next()
        for c in range(NCH):
            s_t, s_r, _ = sq.next()
            act(s_t[:, 0:ncols], src_fn(c), AF.Square, [r_src], [s_r])
            mm(pb[bk][:, 0:ncols], ones_bf[:], s_t[:, 0:ncols], c == 0, c == NCH - 1,
               [s_r, r_const], [r_pb[bk]])
        rsqrt_chain(rstd[:, 0:ncols], pb[bk][:, 0:ncols], 1.0 / D, tmp512[:, 0:ncols],
                    [r_pb[bk]], [r_rstd], r_tmp512)
        for c in range(NCH):
            stt(dst_fn(c), src_fn(c), gains[:, c:c + 1], rstd[:, 0:ncols], ALU.mult, ALU.mult,
                [r_src, r_rstd, r_const], [r_dst])


    def run_chains(gens, width):
        active = []
        it = iter(gens)
        while True:
            while len(active) < width:
                g = next(it, None)
                if g is None:
                    break
                active.append(g)
            if not active:
                break
            for g in list(active):
                try:
                    next(g)
                except StopIteration:
                    active.remove(g)

    A.off = phase_base
    xn1 = A("xn1", [128, NCH, 512], BF16)
    r_xn1 = Res("xn1")
    KT = A("KT", [128, NKV, S], BF16)
    r_KT = Res("KT")
    Vt = A("Vt", [128, NB, 256], BF16)
    r_V = Res("V")
    _o = A.off
    xt_m = A("xt_m", [128, NCH, 512], F32)
    r_xtm = Res("xt_m")
    c_xtm = P.chan("xt_m")
    c_xns2 = [P.chan("xns0"), P.chan("xns1")]
    c_xnl = P.chan("xnl")
    A.off = _o
    QT = A("QT", [128, 4, 1024], BF16)
    r_QT = [Res(f"QT{i}") for i in range(4)]
    vn = A("vn", [128, 4, 1024], BF16)
    r_vn = [Res(f"vn{i}") for i in range(4)]
    mixT = A("mixT", [128, NCH, 512], BF16)
    r_mix = [Res(f"mix{i}") for i in range(NCH)]
    wbig = Slots(P, A, "wbig", 2, [128, NCH, 512], BF16)
    wsm = Slots(P, A, "wsm", 3, [128, NCH, 128], BF16)
    xch = Slots(P, A, "xch", 3, [128, 512], F32)
    ropec = Slots(P, A, "ropec", 4, [128, 2, 128], F32)
    qn = Slots(P, A, "qn", 4, [128, 512], F32)
    rbs = Slots(P, A, "rb", 4, [128, 512], F32)
    qr = Slots(P, A, "qr", 4, [128, 512], BF16)
    st8 = Slots(P, A, "st8", 4, [128, 16], F32)
    junk = Slots(P, A, "junk", 2, [128, 1024], BF16)
    _o = A.off
    xn1b = A("xn1b", [128, NCH, 512], BF16)
    r_xn1b = Res("xn1b")
    A.off = _o
    pt = Slots(P, A, "pt", 6, [128, 512], BF16)
    den = Slots(P, A, "den", 2, [128, 512], F32)
    usb = Slots(P, A, "usb", 3, [128, 512], F32)
    gel = Slots(P, A, "gel", 2, [128, 1024], F32)
    fsb = Slots(P, A, "fsb", 2, [128, 512], F32)
    statA = A("statA", [128, 512], F32)
    r_statA = Res("statA")
    statS = A("statS", [128, 512], F32)
    r_statS = Res("statS")
    mixer_res = [r_xn1, r_xn1b, r_KT, r_V, r_statA, r_statS, r_xtm] + r_QT + r_vn + r_mix
    for sl in (wbig, wsm, xch, ropec, qn, rbs, qr, st8, junk, pt, den, usb, gel, fsb):
        mixer_res += sl.r
    if dbg:
        dtmp = Slots(P, A, "dtmp", 1, [128, 2048], F32)
        d_t, d_r, d_c = dtmp.t[0], dtmp.r[0], dtmp.c[0]
        mixer_res.append(d_r)
    mixer_hi = A.off


    A.off = phase_base
    xt = A("xt", [128, 2, NCH, 512], F32)
    r_xt = Res("xt")
    c_xt = P.chan("xt")
    xn2 = A("xn2", [128, NCH, 1026], BF16)
    r_xn2 = Res("xn2")
    xh = A("xh", [128, 2, NCH], F32)
    r_xh = Res("xh")
    c_xh = P.chan("xh")
    PARTS = [(0, 15), (15, 30), (30, 44)]
    hT = A("hT", [128, 15, 1024], BF16)
    r_hT = Res("hT")
    wup = Slots(P, A, "wup", 2, [128, NCH, 2, 128], BF16)
    wdn = Slots(P, A, "wdn", 2, [128, 15, 256], BF16)
    acc = Slots(P, A, "acc", 2, [128, 2, 512], F32)
    sg = Slots(P, A, "sg", 2, [128, 512], F32)
    ffn_res = [r_xt, r_xn2, r_xh, r_hT] + wup.r + wdn.r + acc.r + sg.r


    rot = Rot([0, 1, 2, 3, 4, 5, 6, 7])
    rotf = Rot([0, 1, 2, 3, 4, 5, 6, 7])
    cc1 = c_cc[:, :, 0, :]
    ss1 = c_ss[:, :, 0, :]

    class BankPool:
        def __init__(self, ids):
            self.free = list(ids)

        def can(self, m):
            return len(self.free) >= m

        def take(self, m):
            return [self.free.pop(0) for _ in range(m)]

        def give(self, ks):
            self.free.extend(ks)

    banks = BankPool(range(8))

    def acquire(reqs):
        while not all(p.can(m) for p, m in reqs):
            yield None
        yield [p.take(m) for p, m in reqs]

    def with_res(reqs, body):
        def gen():
            got = None
            for got in acquire(reqs):
                if got is None:
                    yield
            try:
                yield from body(*got)
            finally:
                for (p, m), g_ in zip(reqs, got):
                    p.give(g_)
        return gen()

    r_wsc = {}

    def load_w(pool_slots, slot, key, src_f32, scr, first):
        k, w_t, w_r, w_c = slot
        if key not in r_wsc:
            r_wsc[key] = Res("wsc" + str(key))
        if first:
            P.dma("pool", w_t[:], src_f32, writes=[w_r], chan=w_c)
            P.dma("sp", scr, w_t[:], reads=[w_r], writes=[r_wsc[key]], chan=pool_slots.c2[k])
        else:
            P.dma("pool", w_t[:], scr, reads=[r_wsc[key]], writes=[w_r], chan=w_c)

    def norm_stream(src, t0, gains, dst_fn, r_dst):
        bk = rot.next()
        for c in range(NCH):
            x_t, x_r, x_c = xch.next()
            P.dma("sp", x_t[:], fm(src)[:, c, t0:t0 + 512], writes=[x_r], chan=x_c)
            s_t, s_r, _ = sq.next()
            act(s_t[:], x_t[:], AF.Square, [x_r], [s_r])
            mm(pb[bk][:], ones_bf[:], s_t[:], c == 0, c == NCH - 1, [s_r, r_const], [r_pb[bk]])
        rsqrt_chain(rstd[:], pb[bk][:], 1.0 / D, tmp512[:], [r_pb[bk]], [r_rstd], r_tmp512)
        for c in range(NCH):
            x_t, x_r, x_c = xch.next()
            P.dma("sp", x_t[:], fm(src)[:, c, t0:t0 + 512], writes=[x_r], chan=x_c)
            stt(dst_fn(c), x_t[:], gains[:, c:c + 1], rstd[:], ALU.mult, ALU.mult, [x_r, r_rstd, r_const], [r_dst])

    QK_REQS = [(banks, 1), (ropec, 1), (st8, 1), (qn, 1), (rbs, 1), (qr, 1)]

    def qk_body(nheads, g_bc, blk, proj_fn, dst_fn):
        def body(bk_, rc_, s8_, qn_, rb_, qr_):
            bk = bk_[0]
            _, rc_t, rc_r, rc_c = rc_[0]
            _, s8, s8_r, _ = s8_[0]
            _, q_n, qn_r, _ = qn_[0]
            jk, jk_r = q_n, qn_r
            _, rb, rb_r, _ = rb_[0]
            _, q_t, q_r, _ = qr_[0]
            W = nheads * 128
            P.dma("sp", rc_t[:, 0, :], cc1[blk], writes=[rc_r], chan=rc_c)
            P.dma("sp", rc_t[:, 1, :], ss1[blk], writes=[rc_r], chan=rc_c)
            proj_fn(bk)
            yield
            for h in range(nheads):
                act(jk[:, h * 128:(h + 1) * 128], pb[bk][:, h * 128:(h + 1) * 128], AF.Square,
                    [r_pb[bk]], [jk_r, s8_r], accum_out=s8[:, h:h + 1])
            yield
            act(s8[:, 4:4 + nheads], s8[:, 0:nheads], AF.Sqrt, [s8_r, r_const], [s8_r], bias=eps_t[:, 0:1], scale=1.0 / HD)
            yield
            recip(s8[:, 8:8 + nheads], s8[:, 4:4 + nheads], [s8_r], [s8_r])
            yield
            for h in range(nheads):
                stt(q_n[:, h * 128:(h + 1) * 128], pb[bk][:, h * 128:(h + 1) * 128], s8[:, 8 + h:9 + h], g_bc[:],
                    ALU.mult, ALU.mult, [r_pb[bk], s8_r, r_const], [qn_r])
            yield
            qn3 = q_n[:, 0:W].rearrange("p (h d) -> p h d", d=128)
            rb3 = rb[:, 0:W].rearrange("p (h d) -> p h d", d=128)
            cc3 = rc_t[:, 0:1, :].broadcast_to([128, nheads, 128])
            ssa = rc_t[:, 1:2, 0:64].broadcast_to([128, nheads, 64])
            ssb = rc_t[:, 1:2, 64:128].broadcast_to([128, nheads, 64])
            tt(rb3[:, :, 0:64], qn3[:, :, 64:128], ssa, ALU.mult, [qn_r, rc_r], [rb_r])
            tt(rb3[:, :, 64:128], qn3[:, :, 0:64], ssb, ALU.mult, [qn_r, rc_r], [rb_r])
            yield
            tt(qn3, qn3, cc3, ALU.mult, [qn_r, rc_r], [qn_r])
            yield
            tt(q_t[:, 0:W], q_n[:, 0:W], rb[:, 0:W], ALU.add, [qn_r, rb_r], [q_r])
            yield
            for h in range(nheads):
                P.op("pe", lambda e, h=h: e.transpose(pbb[bk][:, h * 128:(h + 1) * 128], q_t[:, h * 128:(h + 1) * 128], id_bf[:]),
                     [q_r, r_const], [r_pb[bk]])
            yield
            dst_fn(bk)
        return body


    cur, nxt = xa, xb
    first_src = xT_in
    prev_res = []

    for l in range(NL):
        for dst, src in ((n1g[:], n1g_d[l]), (n2g[:], n2g_d[l]), (aosog[:, 0:8], aog_d[l]), (aosog[:, 8:16], sog_d[l]),
                         (cw[:], cw_d[l]), (cb[:], cb_d[l]),
                         (qg_bc[:], qg_d[l:l + 1, :].partition_broadcast(128)),
                         (kg_bc[:], kg_d[l:l + 1, :].partition_broadcast(128)),
                         (esink[:], sink_d[l:l + 1, :].partition_broadcast(128)),
                         (lng_bc[:], lng_d[l:l + 1, :].partition_broadcast(128)),
                         (lnb_bc[:], lnb_d[l:l + 1, :].partition_broadcast(128)),
                         (bs_bc[:], bs_d[l:l + 1, :].partition_broadcast(128))):
            P.dma("sp", dst, src, writes=[r_const], chan=c_const)
        P.dma("pool", wsT[:], ws_d[l], writes=[r_const], chan=c_ws)
        act(esink[:], esink[:], AF.Exp, [r_const], [r_const])

        src_x = first_src if l == 0 else cur

        fence([], prev_res + mixer_res)

        wkv_t, wkv_r, wkv_c = wbig.next()
        P.dma("pool", wkv_t[:], w_in_d[l][:, :, 1024:1536], writes=[wkv_r], chan=wkv_c)

        xnb = [xn1, xn1b]
        r_xnb = [r_xn1, r_xn1b]

        def kv_chain(j, bi):
            blk = j * 4 + bi
            xb_, rxb_ = xnb[j % 2], r_xnb[j % 2]

            def proj(bk):
                for kc in range(NCH):
                    mm(pb[bk][:], xb_[:, kc, bi * 128:(bi + 1) * 128], wkv_t[:, kc, :], kc == 0, kc == NCH - 1,
                       [rxb_, wkv_r], [r_pb[bk]])
                P.op("act", lambda e: e.activation(Vt[:, blk, :], pb[bk][:, 256:512], AF.Copy), [r_pb[bk]], [r_V])

            def kdst(tb):
                for h in range(NKV):
                    P.op("act", lambda e, h=h: e.activation(KT[:, h, blk * 128:(blk + 1) * 128],
                                                            pbb[tb][:, h * 128:(h + 1) * 128], AF.Copy),
                         [r_pb[tb]], [r_KT])
            return with_res(QK_REQS, qk_body(NKV, kg_bc, blk, proj, kdst))

        r_xns = [Res(f"xns{j}") for j in range(NT)]

        def norm_gen(j):
            buf, rbuf = xnb[j % 2], r_xnb[j % 2]

            def body(bk_, sq_):
                bk = bk_[0]
                for c in range(NCH):
                    _, s_t, s_r, _ = sq_[c % 2]
                    act(s_t[:], xt_m[:, c, :], AF.Square, [r_xtm], [s_r])
                    mm(pb[bk][:], ones_bf[:], s_t[:], c == 0, c == NCH - 1, [s_r, r_const], [r_pb[bk]])
                    if c % 4 == 3:
                        yield
                rsqrt_chain(rstd[:], pb[bk][:], 1.0 / D, tmp512[:], [r_pb[bk]], [r_rstd], r_tmp512)
                yield
                for c in range(NCH):
                    stt(buf[:, c, :], xt_m[:, c, :], n1g[:, c:c + 1], rstd[:], ALU.mult, ALU.mult,
                        [r_xtm, r_rstd, r_const], [rbuf])
                    if c % 4 == 3:
                        yield
                P.dma("sp", xn_scr[:, j], buf[:], reads=[rbuf], writes=[r_xns[j]], chan=c_xns2[j % 2])
                if j + 1 < NT:
                    P.dma("sp", xt_m[:], src_x[j + 1], writes=[r_xtm], chan=c_xtm)
            return with_res([(banks, 1), (sq, 2)], body)

        P.dma("sp", xt_m[:], src_x[0], writes=[r_xtm], chan=c_xtm)
        run_chains([norm_gen(0)], 1)
        for j in range(NT):
            ch = [kv_chain(j, bi) for bi in range(4)]
            if j + 1 < NT:
                ch = [norm_gen(j + 1)] + ch
            run_chains(ch, 5)
        fence([], [r_xtm, r_xn1b] + r_QT + r_vn + r_mix + pt.r + den.r + usb.r + gel.r)
        P.dma("sp", xn1[:], xn_scr[:, 0], reads=r_xns, writes=[r_xn1], chan=c_xnl)

        if dbg and l == 0:
            for h in range(NKV):
                P.op("dve", lambda e, h=h: e.tensor_copy(d_t[:, 0:S], KT[:, h, :]), [r_KT], [d_r])
                P.dma("sp", dbg_out["d_kt"][:, h, :], d_t[:, 0:S], reads=[d_r], chan=d_c)
            for blk in range(NB):
                P.op("dve", lambda e, blk=blk: e.tensor_copy(d_t[:, 0:256], Vt[:, blk, :]), [r_V], [d_r])
                P.dma("sp", dbg_out["d_v"][:, blk, :], d_t[:, 0:256], reads=[d_r], chan=d_c)

        for j in range(NT):
            t0 = j * 512
            P.op("dve", lambda e: e.memset(statA[:], 0.0), [], [r_statA])
            P.op("dve", lambda e: e.memset(statS[:], 0.0), [], [r_statS])

            def q_chain(g, bi, wq_t, wq_r, j=j):
                blk = j * 4 + bi

                def proj(bk):
                    for kc in range(NCH):
                        mm(pb[bk][:], xn1[:, kc, bi * 128:(bi + 1) * 128], wq_t[:, kc, :], kc == 0, kc == NCH - 1,
                           [r_xn1, wq_r], [r_pb[bk]])

                def qdst(tb):
                    P.op("act", lambda e: e.activation(QT[:, bi, g * 512:(g + 1) * 512], pbb[tb][:, 0:512], AF.Copy),
                         [r_pb[tb]], [r_QT[bi]])
                return with_res(QK_REQS, qk_body(4, qg_bc, blk, proj, qdst))

            def att_chain(g, bi, j=j):
                blk = j * 4 + bi
                kbs = [kb for kb in (blk - 1, blk, blk + 1) if 0 <= kb < NB]

                def body(bk_, pt_, dn_, us_, sq_):
                    sbs = bk_[0:3]
                    ob, db = bk_[0], bk_[1]
                    _, d_n, dn_r, _ = dn_[0]
                    _, u_t, u_r, _ = us_[0]
                    _, s_t, s_r, _ = sq_[0]
                    for kb, sb_ in zip(kbs, sbs):
                        mm(pb[sb_][:], KT[:, g, kb * 128:(kb + 1) * 128], QT[:, bi, g * 512:(g + 1) * 512], True, True,
                           [r_KT, r_QT[bi]], [r_pb[sb_]])
                    yield
                    pts = []
                    for i, (kb, sb_) in enumerate(zip(kbs, sbs)):
                        _, p_t, p_r, _ = pt_[i]
                        act(p_t[:], pb[sb_][:], AF.Exp, [r_pb[sb_]], [p_r], scale=float(HD) ** -0.5)
                        pts.append((p_t, p_r, kb))
                    yield
                    for p_t, p_r, kb in pts:
                        if kb != blk:
                            mi = 0 if kb < blk else 1
                            tt(p_t[:], p_t[:], mask4[:, mi, :], ALU.mult, [p_r, r_const], [p_r])
                    yield
                    for i, (p_t, p_r, kb) in enumerate(pts):
                        mm(pb[ob][:], Vt[:, kb, g * 128:(g + 1) * 128], p_t[:], i == 0, i == len(pts) - 1,
                           [r_V, p_r], [r_pb[ob]])
                    for i, (p_t, p_r, kb) in enumerate(pts):
                        mm(pb[db][:], ones_bf[:], p_t[:], i == 0, i == len(pts) - 1, [r_const, p_r], [r_pb[db]])
                    yield
                    for h in range(4):
                        ts(d_n[:, h * 128:(h + 1) * 128], pb[db][:, h * 128:(h + 1) * 128],
                           esink[:, g * 4 + h:g * 4 + h + 1], None, ALU.add, ALU.bypass, [r_pb[db], r_const], [dn_r])
                    yield
                    recip(d_n[:], d_n[:], [dn_r], [dn_r])
                    yield
                    tt(u_t[:], pb[ob][:], d_n[:], ALU.mult, [r_pb[ob], dn_r], [u_r])
                    yield
                    u3 = u_t[:].rearrange("p (h t) -> p h t", t=128)
                    P.op("act", lambda e: e.activation(mixT[:, g * 4:(g + 1) * 4, bi * 128:(bi + 1) * 128], u3, AF.Copy),
                         [u_r], [r_mix[g * 4 + h] for h in range(4)])
                    act(s_t[:], u_t[:], AF.Square, [u_r], [s_r])
                    yield
                    sb0 = sbs[2]
                    for h in range(4):
                        mm(pb[sb0][:, 0:128], ones_bf[:], s_t[:, h * 128:(h + 1) * 128], h == 0, h == 3,
                           [s_r, r_const], [r_pb[sb0]])
                    yield
                    tt(statA[:, bi * 128:(bi + 1) * 128], statA[:, bi * 128:(bi + 1) * 128], pb[sb0][:, 0:128], ALU.add,
                       [r_statA, r_pb[sb0]], [r_statA])
                return with_res([(banks, 3), (pt, 3), (den, 1), (usb, 1), (sq, 1)], body)

            def gv_chain(bi, wgv):
                def body(bk_, gl_, s8_, jk_):
                    bk = bk_[0]
                    _, g_t, g_r, _ = gl_[0]
                    _, s8, s8_r, _ = s8_[0]
                    _, jk, jk_r, _ = jk_[0]
                    for gg in range(2):
                        w_t, w_r = wgv[gg]
                        for kc in range(NCH):
                            mm(pb[bk][:], xn1[:, kc, bi * 128:(bi + 1) * 128], w_t[:, kc, :], kc == 0, kc == NCH - 1,
                               [r_xn1, w_r], [r_pb[bk]])
                        yield
                        act(g_t[:, gg * 512:(gg + 1) * 512], pb[bk][:], AF.Gelu_apprx_tanh, [r_pb[bk]], [g_r, s8_r],
                            accum_out=s8[:, gg:gg + 1])
                        yield
                    act(jk[:], g_t[:], AF.Square, [g_r], [jk_r, s8_r], accum_out=s8[:, 2:3])
                    yield
                    tt(s8[:, 3:4], s8[:, 0:1], s8[:, 1:2], ALU.add, [s8_r], [s8_r])
                    yield
                    ts(s8[:, 4:5], s8[:, 3:4], 1.0 / 1024, None, ALU.mult, ALU.bypass, [s8_r], [s8_r])
                    yield
                    tt(s8[:, 5:6], s8[:, 4:5], s8[:, 4:5], ALU.mult, [s8_r], [s8_r])
                    yield
                    stt(s8[:, 6:7], s8[:, 2:3], 1.0 / 1024, s8[:, 5:6], ALU.mult, ALU.subtract, [s8_r], [s8_r])
                    yield
                    act(s8[:, 9:10], s8[:, 6:7], AF.Sqrt, [s8_r, r_const], [s8_r], bias=eps_t[:, 0:1], scale=1.0)
                    yield
                    recip(s8[:, 7:8], s8[:, 9:10], [s8_r], [s8_r])
                    yield
                    ts(g_t[:], g_t[:], s8[:, 4:5], s8[:, 7:8], ALU.subtract, ALU.mult, [g_r, s8_r], [g_r])
                    yield
                    tt(g_t[:], g_t[:], lng_bc[:], ALU.mult, [g_r, r_const], [g_r])
                    yield
                    tt(vn[:, bi, :], g_t[:], lnb_bc[:], ALU.add, [g_r, r_const], [r_vn[bi]])
                return with_res([(banks, 1), (gel, 1), (st8, 1), (junk, 1)], body)

            def head_chain(h):
                def body(bk_, ws_, us_, fs_, sq_):
                    bk = bk_[0]
                    _, wu_t, wu_r, wu_c = ws_[0]
                    _, u_t, u_r, _ = us_[0]
                    _, f_t, f_r, _ = fs_[0]
                    _, s_t, s_r, _ = sq_[0]
                    load_w(wsm, ws_[0], ("gu", h), w_in_d[l][:, :, 1536 + h * 128:1536 + (h + 1) * 128],
                           wsc_in[:, :, 1536 + h * 128:1536 + (h + 1) * 128], j == 0)
                    for kc in range(NCH):
                        mm(pb[bk][:], wu_t[:, kc, :], xn1[:, kc, :], kc == 0, kc == NCH - 1, [r_xn1, wu_r], [r_pb[bk]])
                    yield
                    act(u_t[:], pb[bk][:], AF.Gelu_apprx_tanh, [r_pb[bk]], [u_r])
                    yield
                    for bi in range(4):
                        mm(pb[bk][:, bi * 128:(bi + 1) * 128], vn[:, bi, h * 128:(h + 1) * 128], wsT[:, h, :], True, True,
                           [r_vn[bi], r_const], [r_pb[bk]])
                    yield
                    f3 = f_t[:].rearrange("p (b t) -> p b t", t=128)
                    p3 = pb[bk][:].rearrange("p (b t) -> p b t", t=128)
                    b3 = bs_bc[:, h * 128:(h + 1) * 128].rearrange("p (o t) -> p o t", o=1).broadcast_to([128, 4, 128])
                    tt(f3, p3, b3, ALU.add, [r_pb[bk], r_const], [f_r])
                    yield
                    tt(f_t[:], f_t[:], u_t[:], ALU.mult, [f_r, u_r], [f_r])
                    yield
                    act(mixT[:, 8 + h, :], f_t[:], AF.Copy, [f_r], [r_mix[8 + h]])
                    act(s_t[:], f_t[:], AF.Square, [f_r], [s_r])
                    yield
                    mm(pb[bk][:], ones_bf[:], s_t[:], True, True, [s_r, r_const], [r_pb[bk]])
                    yield
                    tt(statS[:], statS[:], pb[bk][:], ALU.add, [r_statS, r_pb[bk]], [r_statS])
                return with_res([(banks, 1), (wsm, 1), (usb, 1), (fsb, 1), (sq, 1)], body)

            def wo_chain(f, j=j):
                def body(bk_, ws_, xc_):
                    bk = bk_[0]
                    _, wo_t, wo_r, wo_c = ws_[0]
                    _, x_t, x_r, x_c = xc_[0]
                    load_w(wsm, ws_[0], ("wo", f), w_o_d[l, f], wsc_o[f], j == 0)
                    P.dma("sp", x_t[:], src_x[j][:, f, :], writes=[x_r], chan=x_c)
                    for kc in range(NCH):
                        mm(pb[bk][:], wo_t[:, kc, :], mixT[:, kc, :], kc == 0, kc == NCH - 1, [r_mix[kc], wo_r], [r_pb[bk]])
                    yield
                    tt(x_t[:], pb[bk][:], x_t[:], ALU.add, [r_pb[bk], x_r], [x_r])
                    yield
                    P.op("act", lambda e: e.activation(halo[:, j, f, :], x_t[:, 0:512:511], AF.Copy), [x_r], [r_halo])
                    P.dma("sp", nxt[j][:, f, :], x_t[:], reads=[x_r], chan=x_c)
                    if dbg and l == 0:
                        P.dma("sp", dbg_out["d_xb"][j][:, f, :], x_t[:], reads=[x_r], chan=x_c)
                return with_res([(banks, 1), (wsm, 1), (xch, 1)], body)

            def make_q(jq):
                qch = []
                wqs = []
                for g in range(NKV):
                    k = wbig.i % wbig.n
                    wq_t, wq_r, wq_c = wbig.next()
                    load_w(wbig, (k, wq_t, wq_r, wq_c), ("q", g), w_in_d[l][:, :, g * 512:(g + 1) * 512],
                           wsc_in[:, :, g * 512:(g + 1) * 512], jq == 0)
                    wqs.append((wq_t, wq_r))
                for g in range(NKV):
                    qch += [q_chain(g, bi, *wqs[g], j=jq) for bi in range(4)]
                return qch
            if j == 0:
                run_chains(make_q(0), 4)
            wgv = []
            for gg in range(2):
                k = wbig.i % wbig.n
                w_t, w_r, w_c = wbig.next()
                load_w(wbig, (k, w_t, w_r, w_c), ("gv", gg), w_in_d[l][:, :, 2560 + gg * 512:2560 + (gg + 1) * 512],
                       wsc_in[:, :, 2560 + gg * 512:2560 + (gg + 1) * 512], j == 0)
                wgv.append((w_t, w_r))
            chains = []
            atts = [att_chain(g, bi) for g in range(NKV) for bi in range(4)]
            gvs = [gv_chain(bi, wgv) for bi in range(4)]
            for i in range(4):
                chains += [gvs[i], atts[i]]
            chains += atts[4:]
            run_chains(chains, 3)
            heads = [head_chain(h) for h in range(8)]
            run_chains(heads, 3)
            if j + 1 < NT:
                P.dma("sp", xn1[:], xn_scr[:, j + 1], reads=r_xns, writes=[r_xn1], chan=c_xnl)
            for half, stt_t, stt_r in ((0, statA, r_statA), (1, statS, r_statS)):
                rsqrt_chain(rstd[:], stt_t[:], 1.0 / 1024, tmp512[:], [stt_r], [r_rstd], r_tmp512)
                for c in range(8):
                    cc = half * 8 + c
                    stt(mixT[:, cc, :], mixT[:, cc, :], aosog[:, cc:cc + 1], rstd[:], ALU.mult, ALU.mult,
                        [r_mix[cc], r_rstd, r_const], [r_mix[cc]])
            if dbg and l == 0 and j == DBGJ:
                for c in range(NCH):
                    P.op("dve", lambda e, c=c: e.tensor_copy(d_t[:, 0:512], mixT[:, c, :]), [r_mix[c]], [d_r])
                    P.dma("sp", dbg_out["d_mix"][:, c, :], d_t[:, 0:512], reads=[d_r], chan=d_c)
            wos = [wo_chain(f) for f in range(NCH)]
            if j + 1 < NT:
                nq = make_q(j + 1)
                mix = []
                for i in range(8):
                    mix += [wos[2 * i], nq[i], wos[2 * i + 1]]
                run_chains(mix, 5)
            else:
                run_chains(wos, 3)

        r_scr = Res("scr")
        bar = P.op("sp", None, reads=[], writes=[r_scr])
        bar.deps = set(o for o in P.ops if o.chan is not None and o.chan in xch.c)

        fence([], mixer_res + ffn_res)
        prev_res = ffn_res
        last_layer = (l == NL - 1)
        dst_x = outT if last_layer else cur
        store_ops = []
        for js in range(S // 1024):
            t0 = js * 1024
            for sub in range(2):
                P.dma("sp", xt[:, sub], nxt[2 * js + sub], reads=[r_scr], writes=[r_xt], chan=c_xt)
            P.op("dve", lambda e: e.memset(xh[:], 0.0), [], [r_xh])
            if js > 0:
                P.op("act", lambda e, js=js: e.activation(xh[:, 0, :], halo[:, 2 * js - 1, :, 1], AF.Copy), [r_halo], [r_xh])
            if 2 * js + 2 < S // 512:
                P.op("act", lambda e, js=js: e.activation(xh[:, 1, :], halo[:, 2 * js + 2, :, 0], AF.Copy), [r_halo], [r_xh])
            for sub in range(2):
                norm_tile(xt, r_xt, 512, n2g, lambda c, sub=sub: xn2[:, c, 1 + sub * 512:513 + sub * 512], r_xn2, rotf,
                          src_fn=lambda c, sub=sub: xt[:, sub, c, :])
            hb = rotf.next()
            for c in range(NCH):
                s_t, s_r, _ = sq.next()
                act(s_t[:, 0:2], xh[:, :, c], AF.Square, [r_xh], [s_r])
                mm(pb[hb][:, 0:2], ones_bf[:], s_t[:, 0:2], c == 0, c == NCH - 1, [s_r, r_const], [r_pb[hb]])
            rsqrt_chain(rstd[:, 0:2], pb[hb][:, 0:2], 1.0 / D, tmp512[:, 0:2], [r_pb[hb]], [r_rstd], r_tmp512)
            for c in range(NCH):
                stt(xn2[:, c, 0:1026:1025], xh[:, :, c], n2g[:, c:c + 1], rstd[:, 0:2], ALU.mult, ALU.mult,
                    [r_xh, r_rstd, r_const], [r_xn2])
            for (p0, p1) in PARTS:
                for i in range(p0, p1):
                    w_t, w_r, w_c = wup.next()
                    P.dma("pool", w_t[:], w_up_d[l, i], writes=[w_r], chan=w_c)
                    for sub in range(2):
                        c0 = sub * 512
                        a_t, a_r, _ = acc.next()
                        for gu in range(2):
                            b0, b1 = rotf.next(), rotf.next()
                            for kc in range(NCH):
                                mm(pb[b0][:, 0:258], w_t[:, kc, gu, :], xn2[:, kc, c0:c0 + 258], kc == 0, kc == NCH - 1,
                                   [r_xn2, w_r], [r_pb[b0]])
                                mm(pb[b1][:, 0:258], w_t[:, kc, gu, :], xn2[:, kc, c0 + 256:c0 + 514], kc == 0, kc == NCH - 1,
                                   [r_xn2, w_r], [r_pb[b1]])
                            ch = gu * NPAIR + i
                            for hf, bk in ((0, b0), (1, b1)):
                                dst = a_t[:, gu, hf * 256:(hf + 1) * 256]
                                act(dst, pb[bk][:, 1:257], AF.Identity, [r_pb[bk], r_const], [a_r],
                                    scale=cw[:, 1, ch:ch + 1], bias=cb[:, ch:ch + 1])
                                stt(dst, pb[bk][:, 0:256], cw[:, 0, ch:ch + 1], dst, ALU.mult, ALU.add,
                                    [r_pb[bk], r_const, a_r], [a_r])
                                stt(dst, pb[bk][:, 2:258], cw[:, 2, ch:ch + 1], dst, ALU.mult, ALU.add,
                                    [r_pb[bk], r_const, a_r], [a_r])
                        g_t, g_r, _ = sg.next()
                        act(g_t[:], a_t[:, 0, :], AF.Silu, [a_r], [g_r])
                        tt(hT[:, i - p0, c0:c0 + 512], g_t[:], a_t[:, 1, :], ALU.mult, [g_r, a_r], [r_hT])
                npp = p1 - p0
                for fp in range(8):
                    w_t, w_r, w_c = wdn.next()
                    P.dma("pool", w_t[:, 0:npp, :], w_dn_d[l, fp][:, p0:p1, :], writes=[w_r], chan=w_c)
                    for fi in range(2):
                        f = fp * 2 + fi
                        for sub in range(2):
                            c0 = sub * 512
                            bk = rotf.next()
                            for kc in range(npp):
                                mm(pb[bk][:], w_t[:, kc, fi * 128:(fi + 1) * 128], hT[:, kc, c0:c0 + 512], kc == 0, kc == npp - 1,
                                   [r_hT, w_r], [r_pb[bk]])
                            tt(xt[:, sub, f, :], pb[bk][:], xt[:, sub, f, :], ALU.add, [r_pb[bk], r_xt], [r_xt])
            for sub in range(2):
                store_ops.append(P.dma("sp", dst_x[2 * js + sub], xt[:, sub], reads=[r_xt], chan=c_xt))
        bar2 = P.op("sp", None, reads=[], writes=[])
        bar2.deps = set(store_ops)
        first_src = None

    if dbg:
        fin = P.op("sp", None, reads=[], writes=[])
        fin.deps = set(o for o in P.ops if o.chan is not None and o.eng == "sp")
    P.emit()
    return nc


def _consts(S):
    NB = S // 128
    inv = (10000.0 ** (-np.arange(0, HD, 2, dtype=np.float32) / HD)).astype(np.float32)
    ang = np.arange(S, dtype=np.float32)[:, None] * inv[None, :]
    cos, sin = np.cos(ang).astype(np.float32), np.sin(ang).astype(np.float32)
    cc = np.concatenate([cos, cos], axis=1)
    ss = np.concatenate([-sin, sin], axis=1)
    cc = np.ascontiguousarray(np.broadcast_to(cc.reshape(NB, 128, 1, 128), (NB, 128, 4, 128)))
    ss = np.ascontiguousarray(np.broadcast_to(ss.reshape(NB, 128, 1, 128), (NB, 128, 4, 128)))
    kk = np.arange(128)[:, None]
    qq = np.arange(128)[None, :]
    mprev = (kk >= qq).astype(np.float32)
    mnext = (kk <= qq).astype(np.float32)
    mask = np.stack([np.tile(mprev, (1, 4)), np.tile(mnext, (1, 4))]).astype(ml_dtypes.bfloat16)
    ones = np.ones((128, 128), ml_dtypes.bfloat16)
    idb = np.eye(128, dtype=np.float32).astype(ml_dtypes.bfloat16)
    return dict(c_cc=cc, c_ss=ss, c_mask=mask, c_ones=ones, c_idb=idb)


def _layout_weights(inp, NL):
    f = np.float32
    a = lambda v: np.ascontiguousarray(np.asarray(v, dtype=f))
    w = {}
    w["w_in"] = a(np.asarray(inp["w_in"]).reshape(NL, NCH, 128, INW).transpose(0, 2, 1, 3))
    w["w_o"] = a(np.asarray(inp["w_o"]).reshape(NL, NCH, 128, NCH, 128).transpose(0, 3, 2, 1, 4))
    wu = np.asarray(inp["w_up"]).reshape(NL, NCH, 128, 2, NPAIR, 128)
    w["w_up"] = a(wu.transpose(0, 4, 2, 1, 3, 5))
    wd = np.asarray(inp["w_down"]).reshape(NL, NPAIR, 128, 8, 256)
    w["w_dn"] = a(wd.transpose(0, 3, 2, 1, 4))
    w["n1g"] = a(np.asarray(inp["norm1_g"]).reshape(NL, NCH, 128).transpose(0, 2, 1))
    w["n2g"] = a(np.asarray(inp["norm2_g"]).reshape(NL, NCH, 128).transpose(0, 2, 1))
    w["aog"] = a(np.asarray(inp["attn_out_g"]).reshape(NL, 8, 128).transpose(0, 2, 1))
    w["sog"] = a(np.asarray(inp["sgu_out_g"]).reshape(NL, 8, 128).transpose(0, 2, 1))
    w["cw"] = a(np.asarray(inp["conv_w"]).reshape(NL, 3, 88, 128).transpose(0, 3, 1, 2))
    w["cb"] = a(np.asarray(inp["conv_b"]).reshape(NL, 88, 128).transpose(0, 2, 1))
    w["qg"] = a(inp["q_norm_g"])
    w["kg"] = a(inp["k_norm_g"])
    w["sink"] = a(inp["sink"])
    w["lng"] = a(inp["sgu_ln_g"])
    w["lnb"] = a(inp["sgu_ln_b"])
    w["bs"] = a(np.asarray(inp["b_s"]).reshape(NL, 1024))
    w["wsT"] = a(np.asarray(inp["w_s"]).transpose(0, 3, 1, 2))
    return w


_CACHE = {}


def _get_nc(S, NL, dbg=False):
    key = (S, NL, dbg)
    if key not in _CACHE:
        _CACHE[key] = build(S, NL, dbg)
    return _CACHE[key]


def untile(o):
    nt = o.shape[0]
    return np.ascontiguousarray(np.asarray(o).transpose(0, 3, 2, 1).reshape(nt * 512, D))


def run_layers(xs, inp, NL, dbg=False):
    S = xs[0].shape[0]
    w = _layout_weights(inp, NL)
    w.update(_consts(S))
    nc = build(S, NL, dbg)
    in_maps = []
    for x in xs:
        m = dict(w)
        xx = np.asarray(x, dtype=np.float32)
        m["xT"] = np.ascontiguousarray(xx.reshape(S // 512, 512, NCH, 128).transpose(0, 3, 2, 1))
        in_maps.append(m)
    res = run_bass_kernel_spmd(nc, in_maps, core_ids=list(range(len(xs))))
    return res


def kernel(**inputs):
    x = np.asarray(inputs["x"], dtype=np.float32)
    B = x.shape[0]
    NL = np.asarray(inputs["w_in"]).shape[0]
    res = run_layers([x[b] for b in range(B)], inputs, NL)
    out = np.stack([untile(r["outT"]) for r in res.results]).astype(np.float32)
    return out
```

```python
import contextlib
import numpy as np
import ml_dtypes
import concourse.bass as bass
import concourse.mybir as mybir
from concourse.bass_utils import run_bass_kernel_spmd

F32 = mybir.dt.float32
BF16 = mybir.dt.bfloat16
ALU = mybir.AluOpType
AF = mybir.ActivationFunctionType

D = 2048
NCH = 16
DFF = 5632
NPAIR = 44
HD = 128
NQ = 8
NKV = 2
INW = 3584
EPS = 1e-6
SB_BASE = 16512
SB_CAP = 212863

ENGS = ("pe", "act", "dve", "pool", "sp")
SAME_ENG_DIST = 3
import os
DBGJ = int(os.environ.get("DBGJ", "0"))


class Res:
    __slots__ = ("name", "w", "r")

    def __init__(self, name):
        self.name = name
        self.w = None
        self.r = []


class Chan:
    __slots__ = ("name", "sem", "count")

    def __init__(self, name):
        self.name = name
        self.sem = None
        self.count = 0


class Op:
    __slots__ = ("eng", "fn", "deps", "raw", "pos", "sig", "signo", "chan", "chan_val", "gidx")


class Prog:
    def __init__(self, nc):
        self.nc = nc
        self.ops = []
        self.streams = {e: [] for e in ENGS}
        self.chans = []

    def chan(self, name):
        c = Chan(name)
        self.chans.append(c)
        return c

    def op(self, eng, fn, reads=(), writes=(), chan=None):
        o = Op()
        o.eng = eng
        o.fn = fn
        o.chan = chan
        o.sig = False
        o.signo = None
        o.chan_val = None
        deps = set()
        for r in reads:
            if r.w is not None:
                deps.add(r.w)
        o.raw = set(deps)
        for r in writes:
            if r.w is not None:
                deps.add(r.w)
            deps.update(r.r)
        deps.discard(o)
        while any(d.fn is None for d in deps):
            nd = set()
            for d in deps:
                if d.fn is None:
                    nd.update(d.deps)
                else:
                    nd.add(d)
            deps = nd
        o.deps = deps
        for r in reads:
            r.r.append(o)
        for r in writes:
            r.w = o
            r.r = []
        o.pos = len(self.streams[eng])
        o.gidx = len(self.ops)
        self.streams[eng].append(o)
        self.ops.append(o)
        if chan is not None:
            chan.count += 16
            o.chan_val = chan.count
        return o

    def dma(self, eng, out, in_, reads=(), writes=(), chan=None):
        assert chan is not None
        return self.op(eng, lambda e: e.dma_start(out=out, in_=in_), reads, writes, chan)

    def _needs_wait(self, x, c):
        if c.chan is not None:
            return True
        if c.eng != x.eng:
            return True
        if c.eng == "pe":
            return False
        return True

    def emit(self):
        nc = self.nc
        for x in self.ops:
            for c in x.deps:
                if c.chan is None and self._needs_wait(x, c):
                    c.sig = True
        for e in ENGS:
            n = 0
            for o in self.streams[e]:
                if o.sig:
                    n += 1
                    o.signo = n
        with contextlib.ExitStack() as st:
            esem = {e: st.enter_context(nc.semaphore("s_" + e)) for e in ENGS}
            for ci, c in enumerate(self.chans):
                if c.count:
                    c.sem = st.enter_context(nc.semaphore(f"c{ci}_" + c.name))
            block = st.enter_context(nc.Block())

            def run(ename):
                def body(eng):
                    seen_e = {e: 0 for e in ENGS}
                    seen_c = {}
                    for o in self.streams[ename]:
                        need_c = {}
                        need_e = {}
                        for c in o.deps:
                            if not self._needs_wait(o, c):
                                continue
                            if c.chan is not None:
                                if c.chan_val > need_c.get(c.chan, 0):
                                    need_c[c.chan] = c.chan_val
                            else:
                                if c.signo > need_e.get(c.eng, 0):
                                    need_e[c.eng] = c.signo
                        for ch, v in need_c.items():
                            if seen_c.get(ch, 0) >= v:
                                continue
                            eng.wait_ge(ch.sem, v)
                            seen_c[ch] = v
                        for en, v in need_e.items():
                            if seen_e[en] >= v:
                                continue
                            eng.wait_ge(esem[en], v)
                            seen_e[en] = v
                        if o.fn is None:
                            continue
                        ins = o.fn(eng)
                        if o.chan is not None:
                            ins.then_inc(o.chan.sem, 16)
                        elif o.sig:
                            ins.then_inc(esem[ename], 1)

                return body

            block.tensor(run("pe"))
            block.scalar(run("act"))
            block.vector(run("dve"))
            block.gpsimd(run("pool"))
            block.sync(run("sp"))


class Slots:
    def __init__(self, P, alloc, name, n, shape, dtype):
        self.t = [alloc(f"{name}{i}", shape, dtype) for i in range(n)]
        self.r = [Res(f"{name}{i}") for i in range(n)]
        self.c = [P.chan(f"{name}{i}") for i in range(n)]
        self.c2 = [P.chan(f"{name}s{i}") for i in range(n)]
        self.i = 0
        self.n = n
        self.free = list(range(n))

    def next(self):
        k = self.i % self.n
        self.i += 1
        return self.t[k], self.r[k], self.c[k]

    def can(self, m):
        return len(self.free) >= m

    def take(self, m):
        ks = [self.free.pop(0) for _ in range(m)]
        return [(k, self.t[k], self.r[k], self.c[k]) for k in ks]

    def give(self, items):
        for it in items:
            self.free.append(it[0])


def build(S, NL, dbg=False):
    assert S % 512 == 0
    NT = S // 512
    NB = S // 128
    nc = bass.Bass("TRN2", target_bir_lowering=False)
    P = Prog(nc)

    def din(name, shape, dt=F32):
        return nc.dram_tensor(name, list(shape), dt, kind="ExternalInput").ap()

    xT_in = din("xT", [D, S])
    w_in_d = din("w_in", [NL, 128, NCH, INW])
    w_o_d = din("w_o", [NL, NCH, 128, NCH, 128])
    w_up_d = din("w_up", [NL, NPAIR, 128, NCH, 2, 128])
    w_dn_d = din("w_dn", [NL, 8, 128, NPAIR, 256])
    n1g_d = din("n1g", [NL, 128, NCH])
    n2g_d = din("n2g", [NL, 128, NCH])
    aog_d = din("aog", [NL, 128, 8])
    sog_d = din("sog", [NL, 128, 8])
    cw_d = din("cw", [NL, 128, 3, 88])
    cb_d = din("cb", [NL, 128, 88])
    qg_d = din("qg", [NL, 128])
    kg_d = din("kg", [NL, 128])
    sink_d = din("sink", [NL, 8])
    lng_d = din("lng", [NL, 1024])
    lnb_d = din("lnb", [NL, 1024])
    bs_d = din("bs", [NL, 1024])
    ws_d = din("wsT", [NL, 128, 8, 128])
    c_cc = din("c_cc", [NB, 128, 4, 128])
    c_ss = din("c_ss", [NB, 128, 4, 128])
    c_mask = din("c_mask", [2, 128, 512], BF16)
    c_ones = din("c_ones", [128, 128], BF16)
    c_idb = din("c_idb", [128, 128], BF16)
    outT = nc.dram_tensor("outT", [D, S], F32, kind="ExternalOutput").ap()
    xa = nc.dram_tensor("xa_scr", [D, S], F32).ap()
    xb = nc.dram_tensor("xb_scr", [D, S], F32).ap()
    xn_scr = nc.dram_tensor("xn_scr", [128, S // 512, NCH, 512], BF16).ap()
    wsc_in = nc.dram_tensor("wsc_in", [128, NCH, INW], BF16).ap()
    wsc_o = nc.dram_tensor("wsc_o", [NCH, 128, NCH, 128], BF16).ap()
    dbg_out = {}
    if dbg:
        for nm, shp in (("d_kt", [128, NKV, S]), ("d_v", [128, NB, 256]),
                        ("d_mix", [128, NCH, 512]), ("d_xb", [D, S])):
            dbg_out[nm] = nc.dram_tensor(nm, shp, F32, kind="ExternalOutput").ap()

    def fm(ap):
        return ap.rearrange("(c p) s -> p c s", p=128)

    class Arena:
        def __init__(self, base):
            self.off = base
            self.hi = base

        def __call__(self, name, shape, dtype):
            nb = int(np.prod(shape[1:])) * (2 if dtype == BF16 else 4)
            nb = (nb + 63) // 64 * 64
            t = nc.alloc_sbuf_tensor_at(name, list(shape), dtype, offset=self.off)
            self.off += nb
            self.hi = max(self.hi, self.off)
            assert self.off <= SB_BASE + SB_CAP, (name, self.off - SB_BASE)
            return t

    A = Arena(SB_BASE)
    fence_t = [None]
    ones_bf = A("ones_bf", [128, 128], BF16)
    id_bf = A("id_bf", [128, 128], BF16)
    mask4 = A("mask4", [128, 2, 512], BF16)
    eps_t = A("eps_t", [128, 1], F32)
    fence_t[0] = A("fence_t", [128, 1], F32)
    n1g = A("n1g", [128, NCH], F32)
    n2g = A("n2g", [128, NCH], F32)
    aosog = A("aosog", [128, 16], F32)
    cw = A("cw", [128, 3, 88], F32)
    cb = A("cb", [128, 88], F32)
    qg_bc = A("qg_bc", [128, 128], F32)
    kg_bc = A("kg_bc", [128, 128], F32)
    esink = A("esink", [128, 8], F32)
    lng_bc = A("lng_bc", [128, 1024], F32)
    lnb_bc = A("lnb_bc", [128, 1024], F32)
    bs_bc = A("bs_bc", [128, 1024], F32)
    wsT = A("wsT", [128, 8, 128], BF16)
    r_const = Res("const")
    c_const = P.chan("const")
    c_ws = P.chan("ws")
    rstd = A("rstd", [128, 512], F32)
    r_rstd = Res("rstd")
    tmp512 = A("tmp512", [128, 512], F32)
    r_tmp512 = Res("tmp512")
    sq = Slots(P, A, "sq", 5, [128, 512], BF16)
    xo = Slots(P, A, "xo", 2, [128, 512], F32)
    phase_base = A.off

    pb = [nc.alloc_psum_tensor(f"pb{i}", [128, 512], F32) for i in range(8)]
    pbb = [t.bitcast(BF16) for t in pb]
    r_pb = [Res(f"pb{i}") for i in range(8)]

    class Rot:
        def __init__(self, ids):
            self.ids = ids
            self.i = 0

        def next(self):
            k = self.ids[self.i % len(self.ids)]
            self.i += 1
            return k

    def mm(out, lhsT, rhs, start, stop, reads, writes, skip=False):
        if skip:
            P.op("pe", lambda e: e.matmul(out, lhsT, rhs, start=start, stop=stop, skip_group_check=True), reads, writes)
        else:
            P.op("pe", lambda e: e.matmul(out, lhsT, rhs, start=start, stop=stop), reads, writes)

    r_fence = Res("fence")

    def fence(reads, writes):
        P.op("dve", lambda e: e.memset(fence_t[0][:], 0.0), reads, list(writes) + [r_fence])

    def act(out, in_, func, reads, writes, **kw):
        P.op("act", lambda e: e.activation(out, in_, func, **kw), reads, writes)

    def tt(out, a, b, op, reads, writes, eng="dve"):
        P.op(eng, lambda e: e.tensor_tensor(out, a, b, op), reads, writes)

    def stt(out, in0, scalar, in1, op0, op1, reads, writes):
        P.op("dve", lambda e: e.scalar_tensor_tensor(out, in0, scalar, in1, op0, op1), reads, writes)

    def ts(out, in0, s1, s2, op0, op1, reads, writes):
        P.op("dve", lambda e: e.tensor_scalar(out, in0, s1, s2, op0, op1), reads, writes)

    def recip(out, in_, reads, writes):
        P.op("dve", lambda e: e.reciprocal(out, in_), reads, writes)

    def rsqrt_chain(out, in_, scale, tmp, reads, writes, r_tmp):
        act(tmp, in_, AF.Sqrt, reads + [r_const], [r_tmp], bias=eps_t[:, 0:1], scale=scale)
        recip(out, tmp, [r_tmp], writes)

    P.dma("sp", ones_bf[:], c_ones, writes=[r_const], chan=c_const)
    P.dma("sp", id_bf[:], c_idb, writes=[r_const], chan=c_const)
    P.dma("sp", mask4[:, 0, :], c_mask[0], writes=[r_const], chan=c_const)
    P.dma("sp", mask4[:, 1, :], c_mask[1], writes=[r_const], chan=c_const)
    P.op("dve", lambda e: e.memset(eps_t[:], EPS), [], [r_const])

    def norm_tile(src_t, r_src, ncols, gains, dst_fn, r_dst, rot, src_fn=None):
        if src_fn is None:
            src_fn = lambda c: src_t[:, c, 0:ncols]
        bk = rot.next()
        for c in range(NCH):
            s_t, s_r, _ = sq.next()
            act(s_t[:, 0:ncols], src_fn(c), AF.Square, [r_src], [s_r])
            mm(pb[bk][:, 0:ncols], ones_bf[:], s_t[:, 0:ncols], c == 0, c == NCH - 1,
               [s_r, r_const], [r_pb[bk]])
        rsqrt_chain(rstd[:, 0:ncols], pb[bk][:, 0:ncols], 1.0 / D, tmp512[:, 0:ncols],
                    [r_pb[bk]], [r_rstd], r_tmp512)
        for c in range(NCH):
            stt(dst_fn(c), src_fn(c), gains[:, c:c + 1], rstd[:, 0:ncols], ALU.mult, ALU.mult,
                [r_src, r_rstd, r_const], [r_dst])


    def run_chains(gens, width):
        active = []
        it = iter(gens)
        while True:
            while len(active) < width:
                g = next(it, None)
                if g is None:
                    break
                active.append(g)
            if not active:
                break
            for g in list(active):
                try:
                    next(g)
                except StopIteration:
                    active.remove(g)

    A.off = phase_base
    xn1 = A("xn1", [128, NCH, 512], BF16)
    r_xn1 = Res("xn1")
    KT = A("KT", [128, NKV, S], BF16)
    r_KT = Res("KT")
    Vt = A("Vt", [128, NB, 256], BF16)
    r_V = Res("V")
    _o = A.off
    xt_m = A("xt_m", [128, NCH, 512], F32)
    r_xtm = Res("xt_m")
    c_xtm = P.chan("xt_m")
    c_xns = P.chan("xns")
    A.off = _o
    QT = A("QT", [128, 4, 1024], BF16)
    r_QT = [Res(f"QT{i}") for i in range(4)]
    vn = A("vn", [128, 4, 1024], BF16)
    r_vn = [Res(f"vn{i}") for i in range(4)]
    mixT = A("mixT", [128, NCH, 512], BF16)
    r_mix = [Res(f"mix{i}") for i in range(NCH)]
    wbig = Slots(P, A, "wbig", 2, [128, NCH, 512], BF16)
    wsm = Slots(P, A, "wsm", 3, [128, NCH, 128], BF16)
    xch = Slots(P, A, "xch", 3, [128, 512], F32)
    ropet = A("ropet", [128, 4, 2, 128], F32)
    r_rope = Res("rope")
    c_rope = P.chan("rope")
    qn = Slots(P, A, "qn", 4, [128, 512], F32)
    rbs = Slots(P, A, "rb", 4, [128, 512], F32)
    qr = Slots(P, A, "qr", 4, [128, 512], BF16)
    st8 = Slots(P, A, "st8", 4, [128, 16], F32)
    junk = Slots(P, A, "junk", 2, [128, 1024], BF16)
    pt = Slots(P, A, "pt", 6, [128, 512], BF16)
    den = Slots(P, A, "den", 2, [128, 512], F32)
    usb = Slots(P, A, "usb", 3, [128, 512], F32)
    gel = Slots(P, A, "gel", 2, [128, 1024], F32)
    fsb = Slots(P, A, "fsb", 2, [128, 512], F32)
    statA = A("statA", [128, 512], F32)
    r_statA = Res("statA")
    statS = A("statS", [128, 512], F32)
    r_statS = Res("statS")
    mixer_res = [r_xn1, r_KT, r_V, r_statA, r_statS, r_xtm, r_rope] + r_QT + r_vn + r_mix
    for sl in (wbig, wsm, xch, qn, rbs, qr, st8, junk, pt, den, usb, gel, fsb):
        mixer_res += sl.r
    if dbg:
        dtmp = Slots(P, A, "dtmp", 1, [128, 2048], F32)
        d_t, d_r, d_c = dtmp.t[0], dtmp.r[0], dtmp.c[0]
        mixer_res.append(d_r)
    mixer_hi = A.off


    A.off = phase_base
    xt = A("xt", [128, NCH, 1024], F32)
    r_xt = Res("xt")
    c_xt = P.chan("xt")
    xn2 = A("xn2", [128, NCH, 1026], BF16)
    r_xn2 = Res("xn2")
    xh = A("xh", [128, 2, NCH], F32)
    r_xh = Res("xh")
    c_xh = P.chan("xh")
    PARTS = [(0, 15), (15, 30), (30, 44)]
    hT = A("hT", [128, 15, 1024], BF16)
    r_hT = Res("hT")
    wup = Slots(P, A, "wup", 2, [128, NCH, 2, 128], BF16)
    wdn = Slots(P, A, "wdn", 2, [128, 15, 256], BF16)
    acc = Slots(P, A, "acc", 2, [128, 2, 512], F32)
    sg = Slots(P, A, "sg", 2, [128, 512], F32)
    ffn_res = [r_xt, r_xn2, r_xh, r_hT] + wup.r + wdn.r + acc.r + sg.r


    rot = Rot([0, 1, 2, 3, 4, 5, 6, 7])
    rotf = Rot([0, 1, 2, 3, 4, 5, 6, 7])
    cc1 = c_cc[:, :, 0, :]
    ss1 = c_ss[:, :, 0, :]

    class BankPool:
        def __init__(self, ids):
            self.free = list(ids)

        def can(self, m):
            return len(self.free) >= m

        def take(self, m):
            return [self.free.pop(0) for _ in range(m)]

        def give(self, ks):
            self.free.extend(ks)

    banks = BankPool(range(8))

    def acquire(reqs):
        while not all(p.can(m) for p, m in reqs):
            yield None
        yield [p.take(m) for p, m in reqs]

    def with_res(reqs, body):
        def gen():
            got = None
            for got in acquire(reqs):
                if got is None:
                    yield
            try:
                yield from body(*got)
            finally:
                for (p, m), g_ in zip(reqs, got):
                    p.give(g_)
        return gen()

    r_wsc = {}

    def load_w(pool_slots, slot, key, src_f32, scr, first):
        k, w_t, w_r, w_c = slot
        if key not in r_wsc:
            r_wsc[key] = Res("wsc" + str(key))
        if first:
            P.dma("pool", w_t[:], src_f32, writes=[w_r], chan=w_c)
            P.dma("sp", scr, w_t[:], reads=[w_r], writes=[r_wsc[key]], chan=pool_slots.c2[k])
        else:
            P.dma("pool", w_t[:], scr, reads=[r_wsc[key]], writes=[w_r], chan=w_c)

    def norm_stream(src, t0, gains, dst_fn, r_dst):
        bk = rot.next()
        for c in range(NCH):
            x_t, x_r, x_c = xch.next()
            P.dma("sp", x_t[:], fm(src)[:, c, t0:t0 + 512], writes=[x_r], chan=x_c)
            s_t, s_r, _ = sq.next()
            act(s_t[:], x_t[:], AF.Square, [x_r], [s_r])
            mm(pb[bk][:], ones_bf[:], s_t[:], c == 0, c == NCH - 1, [s_r, r_const], [r_pb[bk]])
        rsqrt_chain(rstd[:], pb[bk][:], 1.0 / D, tmp512[:], [r_pb[bk]], [r_rstd], r_tmp512)
        for c in range(NCH):
            x_t, x_r, x_c = xch.next()
            P.dma("sp", x_t[:], fm(src)[:, c, t0:t0 + 512], writes=[x_r], chan=x_c)
            stt(dst_fn(c), x_t[:], gains[:, c:c + 1], rstd[:], ALU.mult, ALU.mult, [x_r, r_rstd, r_const], [r_dst])

    QK_REQS = [(banks, 1), (st8, 1), (qn, 1), (rbs, 1), (qr, 1)]

    def load_rope(jt):
        P.dma("sp", ropet[:, :, 0, :], cc1[4 * jt:4 * jt + 4].rearrange("b p d -> p b d"), writes=[r_rope], chan=c_rope)
        P.dma("sp", ropet[:, :, 1, :], ss1[4 * jt:4 * jt + 4].rearrange("b p d -> p b d"), writes=[r_rope], chan=c_rope)

    def qk_body(nheads, g_bc, blk, proj_fn, dst_fn):
        def body(bk_, s8_, qn_, rb_, qr_):
            bk = bk_[0]
            rc_t, rc_r = ropet[:, blk % 4], r_rope
            _, s8, s8_r, _ = s8_[0]
            _, q_n, qn_r, _ = qn_[0]
            jk, jk_r = q_n, qn_r
            _, rb, rb_r, _ = rb_[0]
            _, q_t, q_r, _ = qr_[0]
            W = nheads * 128
            proj_fn(bk)
            yield
            for h in range(nheads):
                act(jk[:, h * 128:(h + 1) * 128], pb[bk][:, h * 128:(h + 1) * 128], AF.Square,
                    [r_pb[bk]], [jk_r, s8_r], accum_out=s8[:, h:h + 1])
            yield
            act(s8[:, 4:4 + nheads], s8[:, 0:nheads], AF.Sqrt, [s8_r, r_const], [s8_r], bias=eps_t[:, 0:1], scale=1.0 / HD)
            yield
            recip(s8[:, 8:8 + nheads], s8[:, 4:4 + nheads], [s8_r], [s8_r])
            yield
            for h in range(nheads):
                stt(q_n[:, h * 128:(h + 1) * 128], pb[bk][:, h * 128:(h + 1) * 128], s8[:, 8 + h:9 + h], g_bc[:],
                    ALU.mult, ALU.mult, [r_pb[bk], s8_r, r_const], [qn_r])
            yield
            qn3 = q_n[:, 0:W].rearrange("p (h d) -> p h d", d=128)
            rb3 = rb[:, 0:W].rearrange("p (h d) -> p h d", d=128)
            cc3 = rc_t[:, 0:1, :].broadcast_to([128, nheads, 128])
            ssa = rc_t[:, 1:2, 0:64].broadcast_to([128, nheads, 64])
            ssb = rc_t[:, 1:2, 64:128].broadcast_to([128, nheads, 64])
            tt(rb3[:, :, 0:64], qn3[:, :, 64:128], ssa, ALU.mult, [qn_r, rc_r], [rb_r])
            tt(rb3[:, :, 64:128], qn3[:, :, 0:64], ssb, ALU.mult, [qn_r, rc_r], [rb_r])
            yield
            tt(qn3, qn3, cc3, ALU.mult, [qn_r, rc_r], [qn_r])
            yield
            tt(q_t[:, 0:W], q_n[:, 0:W], rb[:, 0:W], ALU.add, [qn_r, rb_r], [q_r])
            yield
            for h in range(nheads):
                P.op("pe", lambda e, h=h: e.transpose(pbb[bk][:, h * 128:(h + 1) * 128], q_t[:, h * 128:(h + 1) * 128], id_bf[:]),
                     [q_r, r_const], [r_pb[bk]])
            yield
            dst_fn(bk)
        return body


    cur, nxt = xa, xb
    first_src = xT_in
    prev_res = []

    for l in range(NL):
        fence([], [r_const])
        r_ci = []
        for dst, src in ((n1g[:], n1g_d[l]), (n2g[:], n2g_d[l]), (aosog[:, 0:8], aog_d[l]), (aosog[:, 8:16], sog_d[l]),
                         (cw[:], cw_d[l]), (cb[:], cb_d[l]),
                         (qg_bc[:], qg_d[l:l + 1, :].partition_broadcast(128)),
                         (kg_bc[:], kg_d[l:l + 1, :].partition_broadcast(128)),
                         (esink[:], sink_d[l:l + 1, :].partition_broadcast(128)),
                         (lng_bc[:], lng_d[l:l + 1, :].partition_broadcast(128)),
                         (lnb_bc[:], lnb_d[l:l + 1, :].partition_broadcast(128)),
                         (bs_bc[:], bs_d[l:l + 1, :].partition_broadcast(128))):
            r_ci.append(Res("ci"))
            P.dma("sp", dst, src, reads=[r_const], writes=[r_ci[-1]], chan=c_const)
        r_ci.append(Res("ciw"))
        P.dma("pool", wsT[:], ws_d[l], reads=[r_const], writes=[r_ci[-1]], chan=c_ws)
        fence(r_ci, [r_const])
        act(esink[:], esink[:], AF.Exp, [r_const], [r_const])

        src_x = first_src if l == 0 else cur

        fence([], prev_res + mixer_res)

        wkv_t, wkv_r, wkv_c = wbig.next()
        P.dma("pool", wkv_t[:], w_in_d[l][:, :, 1024:1536], writes=[wkv_r], chan=wkv_c)

        def kv_chain(j, bi):
            blk = j * 4 + bi

            def proj(bk):
                for kc in range(NCH):
                    mm(pb[bk][:], xn1[:, kc, bi * 128:(bi + 1) * 128], wkv_t[:, kc, :], kc == 0, kc == NCH - 1,
                       [r_xn1, wkv_r], [r_pb[bk]])
                P.op("act", lambda e: e.activation(Vt[:, blk, :], pb[bk][:, 256:512], AF.Copy), [r_pb[bk]], [r_V])

            def kdst(tb):
                for h in range(NKV):
                    P.op("act", lambda e, h=h: e.activation(KT[:, h, blk * 128:(blk + 1) * 128],
                                                            pbb[tb][:, h * 128:(h + 1) * 128], AF.Copy),
                         [r_pb[tb]], [r_KT])
            return with_res(QK_REQS, qk_body(NKV, kg_bc, blk, proj, kdst))

        r_xns = [Res(f"xns{j}") for j in range(NT)]
        P.dma("sp", xt_m[:], fm(src_x)[:, :, 0:512], writes=[r_xtm], chan=c_xtm)
        for j in range(NT):
            load_rope(j)
            norm_tile(xt_m, r_xtm, 512, n1g, lambda c: xn1[:, c, :], r_xn1, rot)
            P.dma("sp", xn_scr[:, j], xn1[:], reads=[r_xn1], writes=[r_xns[j]], chan=c_xns)
            if j + 1 < NT:
                P.dma("sp", xt_m[:], fm(src_x)[:, :, (j + 1) * 512:(j + 2) * 512], writes=[r_xtm], chan=c_xtm)
            run_chains([kv_chain(j, bi) for bi in range(4)], 4)
        fence([], [r_xtm] + r_QT + r_vn + r_mix)
        P.dma("sp", xn1[:], xn_scr[:, 0], reads=r_xns, writes=[r_xn1], chan=c_xns)

        if dbg and l == 0:
            for h in range(NKV):
                P.op("dve", lambda e, h=h: e.tensor_copy(d_t[:, 0:S], KT[:, h, :]), [r_KT], [d_r])
                P.dma("sp", dbg_out["d_kt"][:, h, :], d_t[:, 0:S], reads=[d_r], chan=d_c)
            for blk in range(NB):
                P.op("dve", lambda e, blk=blk: e.tensor_copy(d_t[:, 0:256], Vt[:, blk, :]), [r_V], [d_r])
                P.dma("sp", dbg_out["d_v"][:, blk, :], d_t[:, 0:256], reads=[d_r], chan=d_c)

        for j in range(NT):
            t0 = j * 512
            P.op("dve", lambda e: e.memset(statA[:], 0.0), [], [r_statA])
            P.op("dve", lambda e: e.memset(statS[:], 0.0), [], [r_statS])

            def q_chain(g, bi, wq_t, wq_r, j=j):
                blk = j * 4 + bi

                def proj(bk):
                    for kc in range(NCH):
                        mm(pb[bk][:], xn1[:, kc, bi * 128:(bi + 1) * 128], wq_t[:, kc, :], kc == 0, kc == NCH - 1,
                           [r_xn1, wq_r], [r_pb[bk]])

                def qdst(tb):
                    P.op("act", lambda e: e.activation(QT[:, bi, g * 512:(g + 1) * 512], pbb[tb][:, 0:512], AF.Copy),
                         [r_pb[tb]], [r_QT[bi]])
                return with_res(QK_REQS, qk_body(4, qg_bc, blk, proj, qdst))

            def att_chain(g, bi, j=j):
                blk = j * 4 + bi
                kbs = [kb for kb in (blk - 1, blk, blk + 1) if 0 <= kb < NB]

                def body(bk_, pt_, dn_, us_, sq_):
                    sbs = bk_[0:3]
                    ob, db = bk_[0], bk_[1]
                    _, d_n, dn_r, _ = dn_[0]
                    _, u_t, u_r, _ = us_[0]
                    _, s_t, s_r, _ = sq_[0]
                    for kb, sb_ in zip(kbs, sbs):
                        mm(pb[sb_][:], KT[:, g, kb * 128:(kb + 1) * 128], QT[:, bi, g * 512:(g + 1) * 512], True, True,
                           [r_KT, r_QT[bi]], [r_pb[sb_]])
                    yield
                    pts = []
                    for i, (kb, sb_) in enumerate(zip(kbs, sbs)):
                        _, p_t, p_r, _ = pt_[i]
                        act(p_t[:], pb[sb_][:], AF.Exp, [r_pb[sb_]], [p_r], scale=float(HD) ** -0.5)
                        pts.append((p_t, p_r, kb))
                    yield
                    for p_t, p_r, kb in pts:
                        if kb != blk:
                            mi = 0 if kb < blk else 1
                            tt(p_t[:], p_t[:], mask4[:, mi, :], ALU.mult, [p_r, r_const], [p_r])
                    yield
                    for i, (p_t, p_r, kb) in enumerate(pts):
                        mm(pb[ob][:], Vt[:, kb, g * 128:(g + 1) * 128], p_t[:], i == 0, i == len(pts) - 1,
                           [r_V, p_r], [r_pb[ob]])
                    for i, (p_t, p_r, kb) in enumerate(pts):
                        mm(pb[db][:], ones_bf[:], p_t[:], i == 0, i == len(pts) - 1, [r_const, p_r], [r_pb[db]])
                    yield
                    for h in range(4):
                        ts(d_n[:, h * 128:(h + 1) * 128], pb[db][:, h * 128:(h + 1) * 128],
                           esink[:, g * 4 + h:g * 4 + h + 1], None, ALU.add, ALU.bypass, [r_pb[db], r_const], [dn_r])
                    yield
                    recip(d_n[:], d_n[:], [dn_r], [dn_r])
                    yield
                    tt(u_t[:], pb[ob][:], d_n[:], ALU.mult, [r_pb[ob], dn_r], [u_r])
                    yield
                    u3 = u_t[:].rearrange("p (h t) -> p h t", t=128)
                    P.op("act", lambda e: e.activation(mixT[:, g * 4:(g + 1) * 4, bi * 128:(bi + 1) * 128], u3, AF.Copy),
                         [u_r], [r_mix[g * 4 + h] for h in range(4)])
                    act(s_t[:], u_t[:], AF.Square, [u_r], [s_r])
                    yield
                    sb0 = sbs[2]
                    for h in range(4):
                        mm(pb[sb0][:, 0:128], ones_bf[:], s_t[:, h * 128:(h + 1) * 128], h == 0, h == 3,
                           [s_r, r_const], [r_pb[sb0]])
                    yield
                    tt(statA[:, bi * 128:(bi + 1) * 128], statA[:, bi * 128:(bi + 1) * 128], pb[sb0][:, 0:128], ALU.add,
                       [r_statA, r_pb[sb0]], [r_statA])
                return with_res([(banks, 3), (pt, 3), (den, 1), (usb, 1), (sq, 1)], body)

            def gv_chain(bi, wgv):
                def body(bk_, gl_, s8_, jk_):
                    bk = bk_[0]
                    _, g_t, g_r, _ = gl_[0]
                    _, s8, s8_r, _ = s8_[0]
                    _, jk, jk_r, _ = jk_[0]
                    for gg in range(2):
                        w_t, w_r = wgv[gg]
                        for kc in range(NCH):
                            mm(pb[bk][:], xn1[:, kc, bi * 128:(bi + 1) * 128], w_t[:, kc, :], kc == 0, kc == NCH - 1,
                               [r_xn1, w_r], [r_pb[bk]])
                        yield
                        act(g_t[:, gg * 512:(gg + 1) * 512], pb[bk][:], AF.Gelu_apprx_tanh, [r_pb[bk]], [g_r, s8_r],
                            accum_out=s8[:, gg:gg + 1])
                        yield
                    act(jk[:], g_t[:], AF.Square, [g_r], [jk_r, s8_r], accum_out=s8[:, 2:3])
                    yield
                    tt(s8[:, 3:4], s8[:, 0:1], s8[:, 1:2], ALU.add, [s8_r], [s8_r])
                    yield
                    ts(s8[:, 4:5], s8[:, 3:4], 1.0 / 1024, None, ALU.mult, ALU.bypass, [s8_r], [s8_r])
                    yield
                    tt(s8[:, 5:6], s8[:, 4:5], s8[:, 4:5], ALU.mult, [s8_r], [s8_r])
                    yield
                    stt(s8[:, 6:7], s8[:, 2:3], 1.0 / 1024, s8[:, 5:6], ALU.mult, ALU.subtract, [s8_r], [s8_r])
                    yield
                    act(s8[:, 9:10], s8[:, 6:7], AF.Sqrt, [s8_r, r_const], [s8_r], bias=eps_t[:, 0:1], scale=1.0)
                    yield
                    recip(s8[:, 7:8], s8[:, 9:10], [s8_r], [s8_r])
                    yield
                    ts(g_t[:], g_t[:], s8[:, 4:5], s8[:, 7:8], ALU.subtract, ALU.mult, [g_r, s8_r], [g_r])
                    yield
                    tt(g_t[:], g_t[:], lng_bc[:], ALU.mult, [g_r, r_const], [g_r])
                    yield
                    tt(vn[:, bi, :], g_t[:], lnb_bc[:], ALU.add, [g_r, r_const], [r_vn[bi]])
                return with_res([(banks, 1), (gel, 1), (st8, 1), (junk, 1)], body)

            def head_chain(h):
                def body(bk_, ws_, us_, fs_, sq_):
                    bk = bk_[0]
                    _, wu_t, wu_r, wu_c = ws_[0]
                    _, u_t, u_r, _ = us_[0]
                    _, f_t, f_r, _ = fs_[0]
                    _, s_t, s_r, _ = sq_[0]
                    load_w(wsm, ws_[0], ("gu", h), w_in_d[l][:, :, 1536 + h * 128:1536 + (h + 1) * 128],
                           wsc_in[:, :, 1536 + h * 128:1536 + (h + 1) * 128], j == 0)
                    for kc in range(NCH):
                        mm(pb[bk][:], wu_t[:, kc, :], xn1[:, kc, :], kc == 0, kc == NCH - 1, [r_xn1, wu_r], [r_pb[bk]])
                    yield
                    act(u_t[:], pb[bk][:], AF.Gelu_apprx_tanh, [r_pb[bk]], [u_r])
                    yield
                    for bi in range(4):
                        mm(pb[bk][:, bi * 128:(bi + 1) * 128], vn[:, bi, h * 128:(h + 1) * 128], wsT[:, h, :], True, True,
                           [r_vn[bi], r_const], [r_pb[bk]])
                    yield
                    f3 = f_t[:].rearrange("p (b t) -> p b t", t=128)
                    p3 = pb[bk][:].rearrange("p (b t) -> p b t", t=128)
                    b3 = bs_bc[:, h * 128:(h + 1) * 128].rearrange("p (o t) -> p o t", o=1).broadcast_to([128, 4, 128])
                    tt(f3, p3, b3, ALU.add, [r_pb[bk], r_const], [f_r])
                    yield
                    tt(f_t[:], f_t[:], u_t[:], ALU.mult, [f_r, u_r], [f_r])
                    yield
                    act(mixT[:, 8 + h, :], f_t[:], AF.Copy, [f_r], [r_mix[8 + h]])
                    act(s_t[:], f_t[:], AF.Square, [f_r], [s_r])
                    yield
                    mm(pb[bk][:], ones_bf[:], s_t[:], True, True, [s_r, r_const], [r_pb[bk]])
                    yield
                    tt(statS[:], statS[:], pb[bk][:], ALU.add, [r_statS, r_pb[bk]], [r_statS])
                return with_res([(banks, 1), (wsm, 1), (usb, 1), (fsb, 1), (sq, 1)], body)

            def wo_chain(f, t0=t0):
                def body(bk_, ws_, xc_):
                    bk = bk_[0]
                    _, wo_t, wo_r, wo_c = ws_[0]
                    _, x_t, x_r, x_c = xc_[0]
                    load_w(wsm, ws_[0], ("wo", f), w_o_d[l, f], wsc_o[f], j == 0)
                    P.dma("sp", x_t[:], fm(src_x)[:, f, t0:t0 + 512], writes=[x_r], chan=x_c)
                    for kc in range(NCH):
                        mm(pb[bk][:], wo_t[:, kc, :], mixT[:, kc, :], kc == 0, kc == NCH - 1, [r_mix[kc], wo_r], [r_pb[bk]])
                    yield
                    tt(x_t[:], pb[bk][:], x_t[:], ALU.add, [r_pb[bk], x_r], [x_r])
                    yield
                    P.dma("sp", fm(nxt)[:, f, t0:t0 + 512], x_t[:], reads=[x_r], chan=x_c)
                    if dbg and l == 0:
                        P.dma("sp", fm(dbg_out["d_xb"])[:, f, t0:t0 + 512], x_t[:], reads=[x_r], chan=x_c)
                return with_res([(banks, 1), (wsm, 1), (xch, 1)], body)

            def make_q(jq):
                load_rope(jq)
                qch = []
                wqs = []
                for g in range(NKV):
                    k = wbig.i % wbig.n
                    wq_t, wq_r, wq_c = wbig.next()
                    load_w(wbig, (k, wq_t, wq_r, wq_c), ("q", g), w_in_d[l][:, :, g * 512:(g + 1) * 512],
                           wsc_in[:, :, g * 512:(g + 1) * 512], jq == 0)
                    wqs.append((wq_t, wq_r))
                for g in range(NKV):
                    qch += [q_chain(g, bi, *wqs[g], j=jq) for bi in range(4)]
                return qch
            if j == 0:
                run_chains(make_q(0), 4)
            wgv = []
            for gg in range(2):
                k = wbig.i % wbig.n
                w_t, w_r, w_c = wbig.next()
                load_w(wbig, (k, w_t, w_r, w_c), ("gv", gg), w_in_d[l][:, :, 2560 + gg * 512:2560 + (gg + 1) * 512],
                       wsc_in[:, :, 2560 + gg * 512:2560 + (gg + 1) * 512], j == 0)
                wgv.append((w_t, w_r))
            chains = []
            atts = [att_chain(g, bi) for g in range(NKV) for bi in range(4)]
            gvs = [gv_chain(bi, wgv) for bi in range(4)]
            for i in range(4):
                chains += [gvs[i], atts[i]]
            chains += atts[4:]
            run_chains(chains, 3)
            heads = [head_chain(h) for h in range(8)]
            run_chains(heads, 3)
            if j + 1 < NT:
                P.dma("sp", xn1[:], xn_scr[:, j + 1], reads=r_xns, writes=[r_xn1], chan=c_xns)
            for half, stt_t, stt_r in ((0, statA, r_statA), (1, statS, r_statS)):
                rsqrt_chain(rstd[:], stt_t[:], 1.0 / 1024, tmp512[:], [stt_r], [r_rstd], r_tmp512)
                for c in range(8):
                    cc = half * 8 + c
                    stt(mixT[:, cc, :], mixT[:, cc, :], aosog[:, cc:cc + 1], rstd[:], ALU.mult, ALU.mult,
                        [r_mix[cc], r_rstd, r_const], [r_mix[cc]])
            if dbg and l == 0 and j == DBGJ:
                for c in range(NCH):
                    P.op("dve", lambda e, c=c: e.tensor_copy(d_t[:, 0:512], mixT[:, c, :]), [r_mix[c]], [d_r])
                    P.dma("sp", dbg_out["d_mix"][:, c, :], d_t[:, 0:512], reads=[d_r], chan=d_c)
            wos = [wo_chain(f) for f in range(NCH)]
            if j + 1 < NT:
                nq = make_q(j + 1)
                mix = []
                for i in range(8):
                    mix += [wos[2 * i], nq[i], wos[2 * i + 1]]
                run_chains(mix, 5)
            else:
                run_chains(wos, 3)

        r_scr = Res("scr")
        bar = P.op("sp", None, reads=[], writes=[r_scr])
        bar.deps = set(o for o in P.ops if o.chan is not None and o.chan in xch.c)

        fence([], mixer_res + ffn_res)
        prev_res = ffn_res
        last_layer = (l == NL - 1)
        dst_x = outT if last_layer else cur
        store_ops = []
        for js in range(S // 1024):
            t0 = js * 1024
            P.dma("sp", xt[:], fm(nxt)[:, :, t0:t0 + 1024], reads=[r_scr], writes=[r_xt], chan=c_xt)
            P.op("dve", lambda e: e.memset(xh[:], 0.0), [], [r_xh])
            if t0 > 0:
                P.op("sp", lambda e, t0=t0: e.dma_start(out=xh[:, 0, :], in_=fm(nxt)[:, :, t0 - 1], allow_slow_non_contiguous=True), [r_scr], [r_xh], c_xh)
            if t0 + 1024 < S:
                P.op("sp", lambda e, t0=t0: e.dma_start(out=xh[:, 1, :], in_=fm(nxt)[:, :, t0 + 1024], allow_slow_non_contiguous=True), [r_scr], [r_xh], c_xh)
            for sub in range(2):
                norm_tile(xt, r_xt, 512, n2g, lambda c, sub=sub: xn2[:, c, 1 + sub * 512:513 + sub * 512], r_xn2, rotf,
                          src_fn=lambda c, sub=sub: xt[:, c, sub * 512:(sub + 1) * 512])
            hb = rotf.next()
            for c in range(NCH):
                s_t, s_r, _ = sq.next()
                act(s_t[:, 0:2], xh[:, :, c], AF.Square, [r_xh], [s_r])
                mm(pb[hb][:, 0:2], ones_bf[:], s_t[:, 0:2], c == 0, c == NCH - 1, [s_r, r_const], [r_pb[hb]])
            rsqrt_chain(rstd[:, 0:2], pb[hb][:, 0:2], 1.0 / D, tmp512[:, 0:2], [r_pb[hb]], [r_rstd], r_tmp512)
            for c in range(NCH):
                stt(xn2[:, c, 0:1026:1025], xh[:, :, c], n2g[:, c:c + 1], rstd[:, 0:2], ALU.mult, ALU.mult,
                    [r_xh, r_rstd, r_const], [r_xn2])
            for (p0, p1) in PARTS:
                for i in range(p0, p1):
                    w_t, w_r, w_c = wup.next()
                    P.dma("pool", w_t[:], w_up_d[l, i], writes=[w_r], chan=w_c)
                    for sub in range(2):
                        c0 = sub * 512
                        a_t, a_r, _ = acc.next()
                        for gu in range(2):
                            b0, b1 = rotf.next(), rotf.next()
                            for kc in range(NCH):
                                mm(pb[b0][:, 0:258], w_t[:, kc, gu, :], xn2[:, kc, c0:c0 + 258], kc == 0, kc == NCH - 1,
                                   [r_xn2, w_r], [r_pb[b0]])
                                mm(pb[b1][:, 0:258], w_t[:, kc, gu, :], xn2[:, kc, c0 + 256:c0 + 514], kc == 0, kc == NCH - 1,
                                   [r_xn2, w_r], [r_pb[b1]])
                            ch = gu * NPAIR + i
                            for hf, bk in ((0, b0), (1, b1)):
                                dst = a_t[:, gu, hf * 256:(hf + 1) * 256]
                                act(dst, pb[bk][:, 1:257], AF.Identity, [r_pb[bk], r_const], [a_r],
                                    scale=cw[:, 1, ch:ch + 1], bias=cb[:, ch:ch + 1])
                                stt(dst, pb[bk][:, 0:256], cw[:, 0, ch:ch + 1], dst, ALU.mult, ALU.add,
                                    [r_pb[bk], r_const, a_r], [a_r])
                                stt(dst, pb[bk][:, 2:258], cw[:, 2, ch:ch + 1], dst, ALU.mult, ALU.add,
                                    [r_pb[bk], r_const, a_r], [a_r])
                        g_t, g_r, _ = sg.next()
                        act(g_t[:], a_t[:, 0, :], AF.Silu, [a_r], [g_r])
                        tt(hT[:, i - p0, c0:c0 + 512], g_t[:], a_t[:, 1, :], ALU.mult, [g_r, a_r], [r_hT])
                npp = p1 - p0
                for fp in range(8):
                    w_t, w_r, w_c = wdn.next()
                    P.dma("pool", w_t[:, 0:npp, :], w_dn_d[l, fp][:, p0:p1, :], writes=[w_r], chan=w_c)
                    for fi in range(2):
                        f = fp * 2 + fi
                        for sub in range(2):
                            c0 = sub * 512
                            bk = rotf.next()
                            for kc in range(npp):
                                mm(pb[bk][:], w_t[:, kc, fi * 128:(fi + 1) * 128], hT[:, kc, c0:c0 + 512], kc == 0, kc == npp - 1,
                                   [r_hT, w_r], [r_pb[bk]])
                            tt(xt[:, f, c0:c0 + 512], pb[bk][:], xt[:, f, c0:c0 + 512], ALU.add, [r_pb[bk], r_xt], [r_xt])
            store_ops.append(P.dma("sp", fm(dst_x)[:, :, t0:t0 + 1024], xt[:], reads=[r_xt], chan=c_xt))
        bar2 = P.op("sp", None, reads=[], writes=[])
        bar2.deps = set(store_ops)
        first_src = None

    if dbg:
        fin = P.op("sp", None, reads=[], writes=[])
        fin.deps = set(o for o in P.ops if o.chan is not None and o.eng == "sp")
    P.emit()
    return nc


def _consts(S):
    NB = S // 128
    inv = (10000.0 ** (-np.arange(0, HD, 2, dtype=np.float32) / HD)).astype(np.float32)
    ang = np.arange(S, dtype=np.float32)[:, None] * inv[None, :]
    cos, sin = np.cos(ang).astype(np.float32), np.sin(ang).astype(np.float32)
    cc = np.concatenate([cos, cos], axis=1)
    ss = np.concatenate([-sin, sin], axis=1)
    cc = np.ascontiguousarray(np.broadcast_to(cc.reshape(NB, 128, 1, 128), (NB, 128, 4, 128)))
    ss = np.ascontiguousarray(np.broadcast_to(ss.reshape(NB, 128, 1, 128), (NB, 128, 4, 128)))
    kk = np.arange(128)[:, None]
    qq = np.arange(128)[None, :]
    mprev = (kk >= qq).astype(np.float32)
    mnext = (kk <= qq).astype(np.float32)
    mask = np.stack([np.tile(mprev, (1, 4)), np.tile(mnext, (1, 4))]).astype(ml_dtypes.bfloat16)
    ones = np.ones((128, 128), ml_dtypes.bfloat16)
    idb = np.eye(128, dtype=np.float32).astype(ml_dtypes.bfloat16)
    return dict(c_cc=cc, c_ss=ss, c_mask=mask, c_ones=ones, c_idb=idb)


def _layout_weights(inp, NL):
    f = np.float32
    a = lambda v: np.ascontiguousarray(np.asarray(v, dtype=f))
    w = {}
    w["w_in"] = a(np.asarray(inp["w_in"]).reshape(NL, NCH, 128, INW).transpose(0, 2, 1, 3))
    w["w_o"] = a(np.asarray(inp["w_o"]).reshape(NL, NCH, 128, NCH, 128).transpose(0, 3, 2, 1, 4))
    wu = np.asarray(inp["w_up"]).reshape(NL, NCH, 128, 2, NPAIR, 128)
    w["w_up"] = a(wu.transpose(0, 4, 2, 1, 3, 5))
    wd = np.asarray(inp["w_down"]).reshape(NL, NPAIR, 128, 8, 256)
    w["w_dn"] = a(wd.transpose(0, 3, 2, 1, 4))
    w["n1g"] = a(np.asarray(inp["norm1_g"]).reshape(NL, NCH, 128).transpose(0, 2, 1))
    w["n2g"] = a(np.asarray(inp["norm2_g"]).reshape(NL, NCH, 128).transpose(0, 2, 1))
    w["aog"] = a(np.asarray(inp["attn_out_g"]).reshape(NL, 8, 128).transpose(0, 2, 1))
    w["sog"] = a(np.asarray(inp["sgu_out_g"]).reshape(NL, 8, 128).transpose(0, 2, 1))
    w["cw"] = a(np.asarray(inp["conv_w"]).reshape(NL, 3, 88, 128).transpose(0, 3, 1, 2))
    w["cb"] = a(np.asarray(inp["conv_b"]).reshape(NL, 88, 128).transpose(0, 2, 1))
    w["qg"] = a(inp["q_norm_g"])
    w["kg"] = a(inp["k_norm_g"])
    w["sink"] = a(inp["sink"])
    w["lng"] = a(inp["sgu_ln_g"])
    w["lnb"] = a(inp["sgu_ln_b"])
    w["bs"] = a(np.asarray(inp["b_s"]).reshape(NL, 1024))
    w["wsT"] = a(np.asarray(inp["w_s"]).transpose(0, 3, 1, 2))
    return w


_CACHE = {}


def _get_nc(S, NL, dbg=False):
    key = (S, NL, dbg)
    if key not in _CACHE:
        _CACHE[key] = build(S, NL, dbg)
    return _CACHE[key]


def run_layers(xs, inp, NL, dbg=False):
    S = xs[0].shape[0]
    w = _layout_weights(inp, NL)
    w.update(_consts(S))
    nc = build(S, NL, dbg)
    in_maps = []
    for x in xs:
        m = dict(w)
        m["xT"] = np.ascontiguousarray(np.asarray(x, dtype=np.float32).T)
        in_maps.append(m)
    res = run_bass_kernel_spmd(nc, in_maps, core_ids=list(range(len(xs))))
    return res


def kernel(**inputs):
    x = np.asarray(inputs["x"], dtype=np.float32)
    B = x.shape[0]
    NL = np.asarray(inputs["w_in"]).shape[0]
    res = run_layers([x[b] for b in range(B)], inputs, NL)
    out = np.stack([np.ascontiguousarray(r["outT"].T) for r in res.results]).astype(np.float32)
    return out
```

```python
import contextlib
import numpy as np
import ml_dtypes
import concourse.bass as bass
import concourse.mybir as mybir
from concourse.bass_utils import run_bass_kernel_spmd

F32 = mybir.dt.float32
BF16 = mybir.dt.bfloat16
ALU = mybir.AluOpType
AF = mybir.ActivationFunctionType

D = 2048
NCH = 16
DFF = 5632
NPAIR = 44
HD = 128
NQ = 8
NKV = 2
INW = 3584
EPS = 1e-6
SB_BASE = 16512
SB_CAP = 212863

ENGS = ("pe", "act", "dve", "pool", "sp")
SAME_ENG_DIST = 3
import os
DBGJ = int(os.environ.get("DBGJ", "0"))


class Res:
    __slots__ = ("name", "w", "r")

    def __init__(self, name):
        self.name = name
        self.w = None
        self.r = []


class Chan:
    __slots__ = ("name", "sem", "count")

    def __init__(self, name):
        self.name = name
        self.sem = None
        self.count = 0


class Op:
    __slots__ = ("eng", "fn", "deps", "raw", "pos", "sig", "signo", "chan", "chan_val", "gidx")


class Prog:
    def __init__(self, nc):
        self.nc = nc
        self.ops = []
        self.streams = {e: [] for e in ENGS}
        self.chans = []

    def chan(self, name):
        c = Chan(name)
        self.chans.append(c)
        return c

    def op(self, eng, fn, reads=(), writes=(), chan=None):
        o = Op()
        o.eng = eng
        o.fn = fn
        o.chan = chan
        o.sig = False
        o.signo = None
        o.chan_val = None
        deps = set()
        for r in reads:
            if r.w is not None:
                deps.add(r.w)
        o.raw = set(deps)
        for r in writes:
            if r.w is not None:
                deps.add(r.w)
            deps.update(r.r)
        deps.discard(o)
        while any(d.fn is None for d in deps):
            nd = set()
            for d in deps:
                if d.fn is None:
                    nd.update(d.deps)
                else:
                    nd.add(d)
            deps = nd
        o.deps = deps
        for r in reads:
            r.r.append(o)
        for r in writes:
            r.w = o
            r.r = []
        o.pos = len(self.streams[eng])
        o.gidx = len(self.ops)
        self.streams[eng].append(o)
        self.ops.append(o)
        if chan is not None:
            chan.count += 16
            o.chan_val = chan.count
        return o

    def dma(self, eng, out, in_, reads=(), writes=(), chan=None):
        assert chan is not None
        return self.op(eng, lambda e: e.dma_start(out=out, in_=in_), reads, writes, chan)

    def _needs_wait(self, x, c):
        if c.chan is not None:
            return True
        if c.eng != x.eng:
            return True
        if c.eng == "pe":
            return False
        return True

    def emit(self):
        nc = self.nc
        for x in self.ops:
            for c in x.deps:
                if c.chan is None and self._needs_wait(x, c):
                    c.sig = True
        for e in ENGS:
            n = 0
            for o in self.streams[e]:
                if o.sig:
                    n += 1
                    o.signo = n
        with contextlib.ExitStack() as st:
            esem = {e: st.enter_context(nc.semaphore("s_" + e)) for e in ENGS}
            for ci, c in enumerate(self.chans):
                if c.count:
                    c.sem = st.enter_context(nc.semaphore(f"c{ci}_" + c.name))
            block = st.enter_context(nc.Block())

            def run(ename):
                def body(eng):
                    seen_e = {e: 0 for e in ENGS}
                    seen_c = {}
                    for o in self.streams[ename]:
                        need_c = {}
                        need_e = {}
                        for c in o.deps:
                            if not self._needs_wait(o, c):
                                continue
                            if c.chan is not None:
                                if c.chan_val > need_c.get(c.chan, 0):
                                    need_c[c.chan] = c.chan_val
                            else:
                                if c.signo > need_e.get(c.eng, 0):
                                    need_e[c.eng] = c.signo
                        for ch, v in need_c.items():
                            if seen_c.get(ch, 0) >= v:
                                continue
                            eng.wait_ge(ch.sem, v)
                            seen_c[ch] = v
                        for en, v in need_e.items():
                            if seen_e[en] >= v:
                                continue
                            eng.wait_ge(esem[en], v)
                            seen_e[en] = v
                        if o.fn is None:
                            continue
                        ins = o.fn(eng)
                        if o.chan is not None:
                            ins.then_inc(o.chan.sem, 16)
                        elif o.sig:
                            ins.then_inc(esem[ename], 1)

                return body

            block.tensor(run("pe"))
            block.scalar(run("act"))
            block.vector(run("dve"))
            block.gpsimd(run("pool"))
            block.sync(run("sp"))


class Slots:
    def __init__(self, P, alloc, name, n, shape, dtype):
        self.t = [alloc(f"{name}{i}", shape, dtype) for i in range(n)]
        self.r = [Res(f"{name}{i}") for i in range(n)]
        self.c = [P.chan(f"{name}{i}") for i in range(n)]
        self.c2 = [P.chan(f"{name}s{i}") for i in range(n)]
        self.i = 0
        self.n = n
        self.free = list(range(n))

    def next(self):
        k = self.i % self.n
        self.i += 1
        return self.t[k], self.r[k], self.c[k]

    def can(self, m):
        return len(self.free) >= m

    def take(self, m):
        ks = [self.free.pop(0) for _ in range(m)]
        return [(k, self.t[k], self.r[k], self.c[k]) for k in ks]

    def give(self, items):
        for it in items:
            self.free.append(it[0])


def build(S, NL, dbg=False):
    assert S % 512 == 0
    NT = S // 512
    NB = S // 128
    nc = bass.Bass("TRN2", target_bir_lowering=False)
    P = Prog(nc)

    def din(name, shape, dt=F32):
        return nc.dram_tensor(name, list(shape), dt, kind="ExternalInput").ap()

    xT_in = din("xT", [D, S])
    w_in_d = din("w_in", [NL, 128, NCH, INW])
    w_o_d = din("w_o", [NL, NCH, 128, NCH, 128])
    w_up_d = din("w_up", [NL, NPAIR, 128, NCH, 2, 128])
    w_dn_d = din("w_dn", [NL, 8, 128, NPAIR, 256])
    n1g_d = din("n1g", [NL, 128, NCH])
    n2g_d = din("n2g", [NL, 128, NCH])
    aog_d = din("aog", [NL, 128, 8])
    sog_d = din("sog", [NL, 128, 8])
    cw_d = din("cw", [NL, 128, 3, 88])
    cb_d = din("cb", [NL, 128, 88])
    qg_d = din("qg", [NL, 128])
    kg_d = din("kg", [NL, 128])
    sink_d = din("sink", [NL, 8])
    lng_d = din("lng", [NL, 1024])
    lnb_d = din("lnb", [NL, 1024])
    bs_d = din("bs", [NL, 1024])
    ws_d = din("wsT", [NL, 128, 8, 128])
    c_cc = din("c_cc", [NB, 128, 4, 128])
    c_ss = din("c_ss", [NB, 128, 4, 128])
    c_mask = din("c_mask", [2, 128, 512], BF16)
    c_ones = din("c_ones", [128, 128], BF16)
    c_idb = din("c_idb", [128, 128], BF16)
    outT = nc.dram_tensor("outT", [D, S], F32, kind="ExternalOutput").ap()
    xa = nc.dram_tensor("xa_scr", [D, S], F32).ap()
    xb = nc.dram_tensor("xb_scr", [D, S], F32).ap()
    xn_scr = nc.dram_tensor("xn_scr", [128, S // 512, NCH, 512], BF16).ap()
    wsc_in = nc.dram_tensor("wsc_in", [128, NCH, INW], BF16).ap()
    wsc_o = nc.dram_tensor("wsc_o", [NCH, 128, NCH, 128], BF16).ap()
    dbg_out = {}
    if dbg:
        for nm, shp in (("d_kt", [128, NKV, S]), ("d_v", [128, NB, 256]),
                        ("d_mix", [128, NCH, 512]), ("d_xb", [D, S])):
            dbg_out[nm] = nc.dram_tensor(nm, shp, F32, kind="ExternalOutput").ap()

    def fm(ap):
        return ap.rearrange("(c p) s -> p c s", p=128)

    class Arena:
        def __init__(self, base):
            self.off = base
            self.hi = base

        def __call__(self, name, shape, dtype):
            nb = int(np.prod(shape[1:])) * (2 if dtype == BF16 else 4)
            nb = (nb + 63) // 64 * 64
            t = nc.alloc_sbuf_tensor_at(name, list(shape), dtype, offset=self.off)
            self.off += nb
            self.hi = max(self.hi, self.off)
            assert self.off <= SB_BASE + SB_CAP, (name, self.off - SB_BASE)
            return t

    A = Arena(SB_BASE)
    fence_t = [None]
    ones_bf = A("ones_bf", [128, 128], BF16)
    id_bf = A("id_bf", [128, 128], BF16)
    mask4 = A("mask4", [128, 2, 512], BF16)
    eps_t = A("eps_t", [128, 1], F32)
    fence_t[0] = A("fence_t", [128, 1], F32)
    n1g = A("n1g", [128, NCH], F32)
    n2g = A("n2g", [128, NCH], F32)
    aosog = A("aosog", [128, 16], F32)
    cw = A("cw", [128, 3, 88], F32)
    cb = A("cb", [128, 88], F32)
    qg_bc = A("qg_bc", [128, 128], F32)
    kg_bc = A("kg_bc", [128, 128], F32)
    esink = A("esink", [128, 8], F32)
    lng_bc = A("lng_bc", [128, 1024], F32)
    lnb_bc = A("lnb_bc", [128, 1024], F32)
    bs_bc = A("bs_bc", [128, 1024], F32)
    wsT = A("wsT", [128, 8, 128], BF16)
    r_const = Res("const")
    c_const = P.chan("const")
    c_ws = P.chan("ws")
    rstd = A("rstd", [128, 512], F32)
    r_rstd = Res("rstd")
    tmp512 = A("tmp512", [128, 512], F32)
    r_tmp512 = Res("tmp512")
    sq = Slots(P, A, "sq", 5, [128, 512], BF16)
    xo = Slots(P, A, "xo", 2, [128, 512], F32)
    phase_base = A.off

    pb = [nc.alloc_psum_tensor(f"pb{i}", [128, 512], F32) for i in range(8)]
    pbb = [t.bitcast(BF16) for t in pb]
    r_pb = [Res(f"pb{i}") for i in range(8)]

    class Rot:
        def __init__(self, ids):
            self.ids = ids
            self.i = 0

        def next(self):
            k = self.ids[self.i % len(self.ids)]
            self.i += 1
            return k

    def mm(out, lhsT, rhs, start, stop, reads, writes, skip=False):
        if skip:
            P.op("pe", lambda e: e.matmul(out, lhsT, rhs, start=start, stop=stop, skip_group_check=True), reads, writes)
        else:
            P.op("pe", lambda e: e.matmul(out, lhsT, rhs, start=start, stop=stop), reads, writes)

    r_fence = Res("fence")

    def fence(reads, writes):
        P.op("dve", lambda e: e.memset(fence_t[0][:], 0.0), reads, list(writes) + [r_fence])

    def act(out, in_, func, reads, writes, **kw):
        P.op("act", lambda e: e.activation(out, in_, func, **kw), reads, writes)

    def tt(out, a, b, op, reads, writes, eng="dve"):
        P.op(eng, lambda e: e.tensor_tensor(out, a, b, op), reads, writes)

    def stt(out, in0, scalar, in1, op0, op1, reads, writes):
        P.op("dve", lambda e: e.scalar_tensor_tensor(out, in0, scalar, in1, op0, op1), reads, writes)

    def ts(out, in0, s1, s2, op0, op1, reads, writes):
        P.op("dve", lambda e: e.tensor_scalar(out, in0, s1, s2, op0, op1), reads, writes)

    def recip(out, in_, reads, writes):
        P.op("dve", lambda e: e.reciprocal(out, in_), reads, writes)

    def rsqrt_chain(out, in_, scale, tmp, reads, writes, r_tmp):
        act(tmp, in_, AF.Sqrt, reads + [r_const], [r_tmp], bias=eps_t[:, 0:1], scale=scale)
        recip(out, tmp, [r_tmp], writes)

    P.dma("sp", ones_bf[:], c_ones, writes=[r_const], chan=c_const)
    P.dma("sp", id_bf[:], c_idb, writes=[r_const], chan=c_const)
    P.dma("sp", mask4[:, 0, :], c_mask[0], writes=[r_const], chan=c_const)
    P.dma("sp", mask4[:, 1, :], c_mask[1], writes=[r_const], chan=c_const)
    P.op("dve", lambda e: e.memset(eps_t[:], EPS), [], [r_const])

    def norm_tile(src_t, r_src, ncols, gains, dst_fn, r_dst, rot, src_fn=None):
        if src_fn is None:
            src_fn = lambda c: src_t[:, c, 0:ncols]
        bk = rot.next()
        for c in range(NCH):
            s_t, s_r, _ = sq.next()
            act(s_t[:, 0:ncols], src_fn(c), AF.Square, [r_src], [s_r])
            mm(pb[bk][:, 0:ncols], ones_bf[:], s_t[:, 0:ncols], c == 0, c == NCH - 1,
               [s_r, r_const], [r_pb[bk]])
        rsqrt_chain(rstd[:, 0:ncols], pb[bk][:, 0:ncols], 1.0 / D, tmp512[:, 0:ncols],
                    [r_pb[bk]], [r_rstd], r_tmp512)
        for c in range(NCH):
            stt(dst_fn(c), src_fn(c), gains[:, c:c + 1], rstd[:, 0:ncols], ALU.mult, ALU.mult,
                [r_src, r_rstd, r_const], [r_dst])


    def run_chains(gens, width):
        active = []
        it = iter(gens)
        while True:
            while len(active) < width:
                g = next(it, None)
                if g is None:
                    break
                active.append(g)
            if not active:
                break
            for g in list(active):
                try:
                    next(g)
                except StopIteration:
                    active.remove(g)

    A.off = phase_base
    xn1 = A("xn1", [128, NCH, 512], BF16)
    r_xn1 = Res("xn1")
    KT = A("KT", [128, NKV, S], BF16)
    r_KT = Res("KT")
    Vt = A("Vt", [128, NB, 256], BF16)
    r_V = Res("V")
    _o = A.off
    xt_m = A("xt_m", [128, NCH, 512], F32)
    r_xtm = Res("xt_m")
    c_xtm = P.chan("xt_m")
    c_xns = P.chan("xns")
    A.off = _o
    QT = A("QT", [128, 4, 1024], BF16)
    r_QT = [Res(f"QT{i}") for i in range(4)]
    vn = A("vn", [128, 4, 1024], BF16)
    r_vn = [Res(f"vn{i}") for i in range(4)]
    mixT = A("mixT", [128, NCH, 512], BF16)
    r_mix = [Res(f"mix{i}") for i in range(NCH)]
    wbig = Slots(P, A, "wbig", 2, [128, NCH, 512], BF16)
    wsm = Slots(P, A, "wsm", 3, [128, NCH, 128], BF16)
    xch = Slots(P, A, "xch", 3, [128, 512], F32)
    ropet = A("ropet", [128, 4, 2, 128], F32)
    r_rope = Res("rope")
    c_rope = P.chan("rope")
    qn = Slots(P, A, "qn", 4, [128, 512], F32)
    rbs = Slots(P, A, "rb", 4, [128, 512], F32)
    qr = Slots(P, A, "qr", 4, [128, 512], BF16)
    st8 = Slots(P, A, "st8", 4, [128, 16], F32)
    junk = Slots(P, A, "junk", 2, [128, 1024], BF16)
    pt = Slots(P, A, "pt", 6, [128, 512], BF16)
    den = Slots(P, A, "den", 2, [128, 512], F32)
    usb = Slots(P, A, "usb", 3, [128, 512], F32)
    gel = Slots(P, A, "gel", 2, [128, 1024], F32)
    fsb = Slots(P, A, "fsb", 2, [128, 512], F32)
    statA = A("statA", [128, 512], F32)
    r_statA = Res("statA")
    statS = A("statS", [128, 512], F32)
    r_statS = Res("statS")
    mixer_res = [r_xn1, r_KT, r_V, r_statA, r_statS, r_xtm, r_rope] + r_QT + r_vn + r_mix
    for sl in (wbig, wsm, xch, qn, rbs, qr, st8, junk, pt, den, usb, gel, fsb):
        mixer_res += sl.r
    if dbg:
        dtmp = Slots(P, A, "dtmp", 1, [128, 2048], F32)
        d_t, d_r, d_c = dtmp.t[0], dtmp.r[0], dtmp.c[0]
        mixer_res.append(d_r)
    mixer_hi = A.off


    A.off = phase_base
    xt = A("xt", [128, NCH, 1024], F32)
    r_xt = Res("xt")
    c_xt = P.chan("xt")
    xn2 = A("xn2", [128, NCH, 1026], BF16)
    r_xn2 = Res("xn2")
    xh = A("xh", [128, 2, NCH], F32)
    r_xh = Res("xh")
    c_xh = P.chan("xh")
    PARTS = [(0, 15), (15, 30), (30, 44)]
    hT = A("hT", [128, 15, 1024], BF16)
    r_hT = Res("hT")
    wup = Slots(P, A, "wup", 2, [128, NCH, 2, 128], BF16)
    wdn = Slots(P, A, "wdn", 2, [128, 15, 256], BF16)
    acc = Slots(P, A, "acc", 2, [128, 2, 512], F32)
    sg = Slots(P, A, "sg", 2, [128, 512], F32)
    ffn_res = [r_xt, r_xn2, r_xh, r_hT] + wup.r + wdn.r + acc.r + sg.r


    rot = Rot([0, 1, 2, 3, 4, 5, 6, 7])
    rotf = Rot([0, 1, 2, 3, 4, 5, 6, 7])
    cc1 = c_cc[:, :, 0, :]
    ss1 = c_ss[:, :, 0, :]

    class BankPool:
        def __init__(self, ids):
            self.free = list(ids)

        def can(self, m):
            return len(self.free) >= m

        def take(self, m):
            return [self.free.pop(0) for _ in range(m)]

        def give(self, ks):
            self.free.extend(ks)

    banks = BankPool(range(8))

    def acquire(reqs):
        while not all(p.can(m) for p, m in reqs):
            yield None
        yield [p.take(m) for p, m in reqs]

    def with_res(reqs, body):
        def gen():
            got = None
            for got in acquire(reqs):
                if got is None:
                    yield
            try:
                yield from body(*got)
            finally:
                for (p, m), g_ in zip(reqs, got):
                    p.give(g_)
        return gen()

    r_wsc = {}

    def load_w(pool_slots, slot, key, src_f32, scr, first):
        k, w_t, w_r, w_c = slot
        if key not in r_wsc:
            r_wsc[key] = Res("wsc" + str(key))
        if first:
            P.dma("pool", w_t[:], src_f32, writes=[w_r], chan=w_c)
            P.dma("sp", scr, w_t[:], reads=[w_r], writes=[r_wsc[key]], chan=pool_slots.c2[k])
        else:
            P.dma("pool", w_t[:], scr, reads=[r_wsc[key]], writes=[w_r], chan=w_c)

    def norm_stream(src, t0, gains, dst_fn, r_dst):
        bk = rot.next()
        for c in range(NCH):
            x_t, x_r, x_c = xch.next()
            P.dma("sp", x_t[:], fm(src)[:, c, t0:t0 + 512], writes=[x_r], chan=x_c)
            s_t, s_r, _ = sq.next()
            act(s_t[:], x_t[:], AF.Square, [x_r], [s_r])
            mm(pb[bk][:], ones_bf[:], s_t[:], c == 0, c == NCH - 1, [s_r, r_const], [r_pb[bk]])
        rsqrt_chain(rstd[:], pb[bk][:], 1.0 / D, tmp512[:], [r_pb[bk]], [r_rstd], r_tmp512)
        for c in range(NCH):
            x_t, x_r, x_c = xch.next()
            P.dma("sp", x_t[:], fm(src)[:, c, t0:t0 + 512], writes=[x_r], chan=x_c)
            stt(dst_fn(c), x_t[:], gains[:, c:c + 1], rstd[:], ALU.mult, ALU.mult, [x_r, r_rstd, r_const], [r_dst])

    QK_REQS = [(banks, 1), (st8, 1), (qn, 1), (rbs, 1), (qr, 1)]

    def load_rope(jt):
        P.dma("sp", ropet[:, :, 0, :], cc1[4 * jt:4 * jt + 4].rearrange("b p d -> p b d"), writes=[r_rope], chan=c_rope)
        P.dma("sp", ropet[:, :, 1, :], ss1[4 * jt:4 * jt + 4].rearrange("b p d -> p b d"), writes=[r_rope], chan=c_rope)

    def qk_body(nheads, g_bc, blk, proj_fn, dst_fn):
        def body(bk_, s8_, qn_, rb_, qr_):
            bk = bk_[0]
            rc_t, rc_r = ropet[:, blk % 4], r_rope
            _, s8, s8_r, _ = s8_[0]
            _, q_n, qn_r, _ = qn_[0]
            jk, jk_r = q_n, qn_r
            _, rb, rb_r, _ = rb_[0]
            _, q_t, q_r, _ = qr_[0]
            W = nheads * 128
            proj_fn(bk)
            yield
            for h in range(nheads):
                act(jk[:, h * 128:(h + 1) * 128], pb[bk][:, h * 128:(h + 1) * 128], AF.Square,
                    [r_pb[bk]], [jk_r, s8_r], accum_out=s8[:, h:h + 1])
            yield
            act(s8[:, 4:4 + nheads], s8[:, 0:nheads], AF.Sqrt, [s8_r, r_const], [s8_r], bias=eps_t[:, 0:1], scale=1.0 / HD)
            yield
            recip(s8[:, 8:8 + nheads], s8[:, 4:4 + nheads], [s8_r], [s8_r])
            yield
            for h in range(nheads):
                stt(q_n[:, h * 128:(h + 1) * 128], pb[bk][:, h * 128:(h + 1) * 128], s8[:, 8 + h:9 + h], g_bc[:],
                    ALU.mult, ALU.mult, [r_pb[bk], s8_r, r_const], [qn_r])
            yield
            qn3 = q_n[:, 0:W].rearrange("p (h d) -> p h d", d=128)
            rb3 = rb[:, 0:W].rearrange("p (h d) -> p h d", d=128)
            cc3 = rc_t[:, 0:1, :].broadcast_to([128, nheads, 128])
            ssa = rc_t[:, 1:2, 0:64].broadcast_to([128, nheads, 64])
            ssb = rc_t[:, 1:2, 64:128].broadcast_to([128, nheads, 64])
            tt(rb3[:, :, 0:64], qn3[:, :, 64:128], ssa, ALU.mult, [qn_r, rc_r], [rb_r])
            tt(rb3[:, :, 64:128], qn3[:, :, 0:64], ssb, ALU.mult, [qn_r, rc_r], [rb_r])
            yield
            tt(qn3, qn3, cc3, ALU.mult, [qn_r, rc_r], [qn_r])
            yield
            tt(q_t[:, 0:W], q_n[:, 0:W], rb[:, 0:W], ALU.add, [qn_r, rb_r], [q_r])
            yield
            for h in range(nheads):
                P.op("pe", lambda e, h=h: e.transpose(pbb[bk][:, h * 128:(h + 1) * 128], q_t[:, h * 128:(h + 1) * 128], id_bf[:]),
                     [q_r, r_const], [r_pb[bk]])
            yield
            dst_fn(bk)
        return body


    cur, nxt = xa, xb
    first_src = xT_in
    prev_res = []

    for l in range(NL):
        fence([], [r_const])
        r_ci = []
        for dst, src in ((n1g[:], n1g_d[l]), (n2g[:], n2g_d[l]), (aosog[:, 0:8], aog_d[l]), (aosog[:, 8:16], sog_d[l]),
                         (cw[:], cw_d[l]), (cb[:], cb_d[l]),
                         (qg_bc[:], qg_d[l:l + 1, :].partition_broadcast(128)),
                         (kg_bc[:], kg_d[l:l + 1, :].partition_broadcast(128)),
                         (esink[:], sink_d[l:l + 1, :].partition_broadcast(128)),
                         (lng_bc[:], lng_d[l:l + 1, :].partition_broadcast(128)),
                         (lnb_bc[:], lnb_d[l:l + 1, :].partition_broadcast(128)),
                         (bs_bc[:], bs_d[l:l + 1, :].partition_broadcast(128))):
            r_ci.append(Res("ci"))
            P.dma("sp", dst, src, reads=[r_const], writes=[r_ci[-1]], chan=c_const)
        r_ci.append(Res("ciw"))
        P.dma("pool", wsT[:], ws_d[l], reads=[r_const], writes=[r_ci[-1]], chan=c_ws)
        fence(r_ci, [r_const])
        act(esink[:], esink[:], AF.Exp, [r_const], [r_const])

        src_x = first_src if l == 0 else cur

        fence([], prev_res + mixer_res)

        wkv_t, wkv_r, wkv_c = wbig.next()
        P.dma("pool", wkv_t[:], w_in_d[l][:, :, 1024:1536], writes=[wkv_r], chan=wkv_c)

        def kv_chain(j, bi):
            blk = j * 4 + bi

            def proj(bk):
                for kc in range(NCH):
                    mm(pb[bk][:], xn1[:, kc, bi * 128:(bi + 1) * 128], wkv_t[:, kc, :], kc == 0, kc == NCH - 1,
                       [r_xn1, wkv_r], [r_pb[bk]])
                P.op("act", lambda e: e.activation(Vt[:, blk, :], pb[bk][:, 256:512], AF.Copy), [r_pb[bk]], [r_V])

            def kdst(tb):
                for h in range(NKV):
                    P.op("act", lambda e, h=h: e.activation(KT[:, h, blk * 128:(blk + 1) * 128],
                                                            pbb[tb][:, h * 128:(h + 1) * 128], AF.Copy),
                         [r_pb[tb]], [r_KT])
            return with_res(QK_REQS, qk_body(NKV, kg_bc, blk, proj, kdst))

        r_xns = [Res(f"xns{j}") for j in range(NT)]
        P.dma("sp", xt_m[:], fm(src_x)[:, :, 0:512], writes=[r_xtm], chan=c_xtm)
        for j in range(NT):
            load_rope(j)
            norm_tile(xt_m, r_xtm, 512, n1g, lambda c: xn1[:, c, :], r_xn1, rot)
            P.dma("sp", xn_scr[:, j], xn1[:], reads=[r_xn1], writes=[r_xns[j]], chan=c_xns)
            if j + 1 < NT:
                P.dma("sp", xt_m[:], fm(src_x)[:, :, (j + 1) * 512:(j + 2) * 512], writes=[r_xtm], chan=c_xtm)
            run_chains([kv_chain(j, bi) for bi in range(4)], 4)
        fence([], [r_xtm] + r_QT + r_vn + r_mix)
        P.dma("sp", xn1[:], xn_scr[:, 0], reads=r_xns, writes=[r_xn1], chan=c_xns)

        if dbg and l == 0:
            for h in range(NKV):
                P.op("dve", lambda e, h=h: e.tensor_copy(d_t[:, 0:S], KT[:, h, :]), [r_KT], [d_r])
                P.dma("sp", dbg_out["d_kt"][:, h, :], d_t[:, 0:S], reads=[d_r], chan=d_c)
            for blk in range(NB):
                P.op("dve", lambda e, blk=blk: e.tensor_copy(d_t[:, 0:256], Vt[:, blk, :]), [r_V], [d_r])
                P.dma("sp", dbg_out["d_v"][:, blk, :], d_t[:, 0:256], reads=[d_r], chan=d_c)

        for j in range(NT):
            t0 = j * 512
            P.op("dve", lambda e: e.memset(statA[:], 0.0), [], [r_statA])
            P.op("dve", lambda e: e.memset(statS[:], 0.0), [], [r_statS])

            def q_chain(g, bi, wq_t, wq_r, j=j):
                blk = j * 4 + bi

                def proj(bk):
                    for kc in range(NCH):
                        mm(pb[bk][:], xn1[:, kc, bi * 128:(bi + 1) * 128], wq_t[:, kc, :], kc == 0, kc == NCH - 1,
                           [r_xn1, wq_r], [r_pb[bk]])

                def qdst(tb):
                    P.op("act", lambda e: e.activation(QT[:, bi, g * 512:(g + 1) * 512], pbb[tb][:, 0:512], AF.Copy),
                         [r_pb[tb]], [r_QT[bi]])
                return with_res(QK_REQS, qk_body(4, qg_bc, blk, proj, qdst))

            def att_chain(g, bi, j=j):
                blk = j * 4 + bi
                kbs = [kb for kb in (blk - 1, blk, blk + 1) if 0 <= kb < NB]

                def body(bk_, pt_, dn_, us_, sq_):
                    sbs = bk_[0:3]
                    ob, db = bk_[0], bk_[1]
                    _, d_n, dn_r, _ = dn_[0]
                    _, u_t, u_r, _ = us_[0]
                    _, s_t, s_r, _ = sq_[0]
                    for kb, sb_ in zip(kbs, sbs):
                        mm(pb[sb_][:], KT[:, g, kb * 128:(kb + 1) * 128], QT[:, bi, g * 512:(g + 1) * 512], True, True,
                           [r_KT, r_QT[bi]], [r_pb[sb_]])
                    yield
                    pts = []
                    for i, (kb, sb_) in enumerate(zip(kbs, sbs)):
                        _, p_t, p_r, _ = pt_[i]
                        act(p_t[:], pb[sb_][:], AF.Exp, [r_pb[sb_]], [p_r], scale=float(HD) ** -0.5)
                        pts.append((p_t, p_r, kb))
                    yield
                    for p_t, p_r, kb in pts:
                        if kb != blk:
                            mi = 0 if kb < blk else 1
                            tt(p_t[:], p_t[:], mask4[:, mi, :], ALU.mult, [p_r, r_const], [p_r])
                    yield
                    for i, (p_t, p_r, kb) in enumerate(pts):
                        mm(pb[ob][:], Vt[:, kb, g * 128:(g + 1) * 128], p_t[:], i == 0, i == len(pts) - 1,
                           [r_V, p_r], [r_pb[ob]])
                    for i, (p_t, p_r, kb) in enumerate(pts):
                        mm(pb[db][:], ones_bf[:], p_t[:], i == 0, i == len(pts) - 1, [r_const, p_r], [r_pb[db]])
                    yield
                    for h in range(4):
                        ts(d_n[:, h * 128:(h + 1) * 128], pb[db][:, h * 128:(h + 1) * 128],
                           esink[:, g * 4 + h:g * 4 + h + 1], None, ALU.add, ALU.bypass, [r_pb[db], r_const], [dn_r])
                    yield
                    recip(d_n[:], d_n[:], [dn_r], [dn_r])
                    yield
                    tt(u_t[:], pb[ob][:], d_n[:], ALU.mult, [r_pb[ob], dn_r], [u_r])
                    yield
                    u3 = u_t[:].rearrange("p (h t) -> p h t", t=128)
                    P.op("act", lambda e: e.activation(mixT[:, g * 4:(g + 1) * 4, bi * 128:(bi + 1) * 128], u3, AF.Copy),
                         [u_r], [r_mix[g * 4 + h] for h in range(4)])
                    act(s_t[:], u_t[:], AF.Square, [u_r], [s_r])
                    yield
                    sb0 = sbs[2]
                    for h in range(4):
                        mm(pb[sb0][:, 0:128], ones_bf[:], s_t[:, h * 128:(h + 1) * 128], h == 0, h == 3,
                           [s_r, r_const], [r_pb[sb0]])
                    yield
                    tt(statA[:, bi * 128:(bi + 1) * 128], statA[:, bi * 128:(bi + 1) * 128], pb[sb0][:, 0:128], ALU.add,
                       [r_statA, r_pb[sb0]], [r_statA])
                return with_res([(banks, 3), (pt, 3), (den, 1), (usb, 1), (sq, 1)], body)

            def gv_chain(bi, wgv):
                def body(bk_, gl_, s8_, jk_):
                    bk = bk_[0]
                    _, g_t, g_r, _ = gl_[0]
                    _, s8, s8_r, _ = s8_[0]
                    _, jk, jk_r, _ = jk_[0]
                    for gg in range(2):
                        w_t, w_r = wgv[gg]
                        for kc in range(NCH):
                            mm(pb[bk][:], xn1[:, kc, bi * 128:(bi + 1) * 128], w_t[:, kc, :], kc == 0, kc == NCH - 1,
                               [r_xn1, w_r], [r_pb[bk]])
                        yield
                        act(g_t[:, gg * 512:(gg + 1) * 512], pb[bk][:], AF.Gelu_apprx_tanh, [r_pb[bk]], [g_r, s8_r],
                            accum_out=s8[:, gg:gg + 1])
                        yield
                    act(jk[:], g_t[:], AF.Square, [g_r], [jk_r, s8_r], accum_out=s8[:, 2:3])
                    yield
                    tt(s8[:, 3:4], s8[:, 0:1], s8[:, 1:2], ALU.add, [s8_r], [s8_r])
                    yield
                    ts(s8[:, 4:5], s8[:, 3:4], 1.0 / 1024, None, ALU.mult, ALU.bypass, [s8_r], [s8_r])
                    yield
                    tt(s8[:, 5:6], s8[:, 4:5], s8[:, 4:5], ALU.mult, [s8_r], [s8_r])
                    yield
                    stt(s8[:, 6:7], s8[:, 2:3], 1.0 / 1024, s8[:, 5:6], ALU.mult, ALU.subtract, [s8_r], [s8_r])
                    yield
                    act(s8[:, 9:10], s8[:, 6:7], AF.Sqrt, [s8_r, r_const], [s8_r], bias=eps_t[:, 0:1], scale=1.0)
                    yield
                    recip(s8[:, 7:8], s8[:, 9:10], [s8_r], [s8_r])
                    yield
                    ts(g_t[:], g_t[:], s8[:, 4:5], s8[:, 7:8], ALU.subtract, ALU.mult, [g_r, s8_r], [g_r])
                    yield
                    tt(g_t[:], g_t[:], lng_bc[:], ALU.mult, [g_r, r_const], [g_r])
                    yield
                    tt(vn[:, bi, :], g_t[:], lnb_bc[:], ALU.add, [g_r, r_const], [r_vn[bi]])
                return with_res([(banks, 1), (gel, 1), (st8, 1), (junk, 1)], body)

            def head_chain(h):
                def body(bk_, ws_, us_, fs_, sq_):
                    bk = bk_[0]
                    _, wu_t, wu_r, wu_c = ws_[0]
                    _, u_t, u_r, _ = us_[0]
                    _, f_t, f_r, _ = fs_[0]
                    _, s_t, s_r, _ = sq_[0]
                    load_w(wsm, ws_[0], ("gu", h), w_in_d[l][:, :, 1536 + h * 128:1536 + (h + 1) * 128],
                           wsc_in[:, :, 1536 + h * 128:1536 + (h + 1) * 128], j == 0)
                    for kc in range(NCH):
                        mm(pb[bk][:], wu_t[:, kc, :], xn1[:, kc, :], kc == 0, kc == NCH - 1, [r_xn1, wu_r], [r_pb[bk]])
                    yield
                    act(u_t[:], pb[bk][:], AF.Gelu_apprx_tanh, [r_pb[bk]], [u_r])
                    yield
                    for bi in range(4):
                        mm(pb[bk][:, bi * 128:(bi + 1) * 128], vn[:, bi, h * 128:(h + 1) * 128], wsT[:, h, :], True, True,
                           [r_vn[bi], r_const], [r_pb[bk]])
                    yield
                    f3 = f_t[:].rearrange("p (b t) -> p b t", t=128)
                    p3 = pb[bk][:].rearrange("p (b t) -> p b t", t=128)
                    b3 = bs_bc[:, h * 128:(h + 1) * 128].rearrange("p (o t) -> p o t", o=1).broadcast_to([128, 4, 128])
                    tt(f3, p3, b3, ALU.add, [r_pb[bk], r_const], [f_r])
                    yield
                    tt(f_t[:], f_t[:], u_t[:], ALU.mult, [f_r, u_r], [f_r])
                    yield
                    act(mixT[:, 8 + h, :], f_t[:], AF.Copy, [f_r], [r_mix[8 + h]])
                    act(s_t[:], f_t[:], AF.Square, [f_r], [s_r])
                    yield
                    mm(pb[bk][:], ones_bf[:], s_t[:], True, True, [s_r, r_const], [r_pb[bk]])
                    yield
                    tt(statS[:], statS[:], pb[bk][:], ALU.add, [r_statS, r_pb[bk]], [r_statS])
                return with_res([(banks, 1), (wsm, 1), (usb, 1), (fsb, 1), (sq, 1)], body)

            def wo_chain(f, t0=t0):
                def body(bk_, ws_, xc_):
                    bk = bk_[0]
                    _, wo_t, wo_r, wo_c = ws_[0]
                    _, x_t, x_r, x_c = xc_[0]
                    load_w(wsm, ws_[0], ("wo", f), w_o_d[l, f], wsc_o[f], j == 0)
                    P.dma("sp", x_t[:], fm(src_x)[:, f, t0:t0 + 512], writes=[x_r], chan=x_c)
                    for kc in range(NCH):
                        mm(pb[bk][:], wo_t[:, kc, :], mixT[:, kc, :], kc == 0, kc == NCH - 1, [r_mix[kc], wo_r], [r_pb[bk]])
                    yield
                    tt(x_t[:], pb[bk][:], x_t[:], ALU.add, [r_pb[bk], x_r], [x_r])
                    yield
                    P.dma("sp", fm(nxt)[:, f, t0:t0 + 512], x_t[:], reads=[x_r], chan=x_c)
                    if dbg and l == 0:
                        P.dma("sp", fm(dbg_out["d_xb"])[:, f, t0:t0 + 512], x_t[:], reads=[x_r], chan=x_c)
                return with_res([(banks, 1), (wsm, 1), (xch, 1)], body)

            flags = {"gv": 0, "att": 0, "head": 0, "mn": 0, "xn": 0, "wq": 0}
            wq_hold = []

            def counted(gen, key):
                yield from gen
                flags[key] += 1

            def gated(cond, gen_fn):
                while not cond():
                    yield
                yield from gen_fn()

            def load_q_weights(jq):
                load_rope(jq)
                for g in range(NKV):
                    k = wbig.i % wbig.n
                    wq_t, wq_r, wq_c = wbig.next()
                    load_w(wbig, (k, wq_t, wq_r, wq_c), ("q", g), w_in_d[l][:, :, g * 512:(g + 1) * 512],
                           wsc_in[:, :, g * 512:(g + 1) * 512], jq == 0)
                    wq_hold.append((wq_t, wq_r))

            if j == 0:
                load_q_weights(0)
                run_chains([q_chain(g, bi, *wq_hold[g], j=0) for g in range(NKV) for bi in range(4)], 4)
                wq_hold.clear()
            wgv = []
            for gg in range(2):
                k = wbig.i % wbig.n
                w_t, w_r, w_c = wbig.next()
                load_w(wbig, (k, w_t, w_r, w_c), ("gv", gg), w_in_d[l][:, :, 2560 + gg * 512:2560 + (gg + 1) * 512],
                       wsc_in[:, :, 2560 + gg * 512:2560 + (gg + 1) * 512], j == 0)
                wgv.append((w_t, w_r))
            atts = [counted(att_chain(g, bi), "att") for g in range(NKV) for bi in range(4)]
            gvs = [counted(gv_chain(bi, wgv), "gv") for bi in range(4)]
            chains = []
            for i in range(4):
                chains += [gvs[i], atts[i]]
            chains += atts[4:]
            chains += [counted(gated(lambda: flags["gv"] == 4, lambda h=h: head_chain(h)), "head") for h in range(8)]

            def mnorm_body():
                for half, stt_t, stt_r in ((0, statA, r_statA), (1, statS, r_statS)):
                    rsqrt_chain(rstd[:], stt_t[:], 1.0 / 1024, tmp512[:], [stt_r], [r_rstd], r_tmp512)
                    yield
                    for c in range(8):
                        cc = half * 8 + c
                        stt(mixT[:, cc, :], mixT[:, cc, :], aosog[:, cc:cc + 1], rstd[:], ALU.mult, ALU.mult,
                            [r_mix[cc], r_rstd, r_const], [r_mix[cc]])
                        if c % 4 == 3:
                            yield
                if dbg and l == 0 and j == DBGJ:
                    for c in range(NCH):
                        P.op("dve", lambda e, c=c: e.tensor_copy(d_t[:, 0:512], mixT[:, c, :]), [r_mix[c]], [d_r])
                        P.dma("sp", dbg_out["d_mix"][:, c, :], d_t[:, 0:512], reads=[d_r], chan=d_c)
                flags["mn"] = 1

            def wq_body(jq=j + 1):
                load_q_weights(jq)
                flags["wq"] = 1
                yield

            def xn_body(jq=j + 1):
                P.dma("sp", xn1[:], xn_scr[:, jq], reads=r_xns, writes=[r_xn1], chan=c_xns)
                flags["xn"] = 1
                yield

            if j + 1 < NT:
                chains.append(gated(lambda: flags["gv"] == 4, wq_body))
                chains.append(gated(lambda: flags["gv"] == 4 and flags["head"] == 8, xn_body))
            chains.append(gated(lambda: flags["att"] == 8 and flags["head"] == 8, mnorm_body))
            wos = [gated(lambda: flags["mn"] == 1, lambda f=f: wo_chain(f)) for f in range(NCH)]
            if j + 1 < NT:
                nq = [gated(lambda: flags["att"] == 8 and flags["xn"] == 1 and flags["wq"] == 1,
                            lambda g=g, bi=bi: q_chain(g, bi, *wq_hold[g], j=j + 1))
                      for g in range(NKV) for bi in range(4)]
                for i in range(8):
                    chains += [wos[2 * i], nq[i], wos[2 * i + 1]]
            else:
                chains += wos
            run_chains(chains, 5)

        r_scr = Res("scr")
        bar = P.op("sp", None, reads=[], writes=[r_scr])
        bar.deps = set(o for o in P.ops if o.chan is not None and o.chan in xch.c)

        fence([], mixer_res + ffn_res)
        prev_res = ffn_res
        last_layer = (l == NL - 1)
        dst_x = outT if last_layer else cur
        store_ops = []
        for js in range(S // 1024):
            t0 = js * 1024
            P.dma("sp", xt[:], fm(nxt)[:, :, t0:t0 + 1024], reads=[r_scr], writes=[r_xt], chan=c_xt)
            P.op("dve", lambda e: e.memset(xh[:], 0.0), [], [r_xh])
            if t0 > 0:
                P.op("sp", lambda e, t0=t0: e.dma_start(out=xh[:, 0, :], in_=fm(nxt)[:, :, t0 - 1], allow_slow_non_contiguous=True), [r_scr], [r_xh], c_xh)
            if t0 + 1024 < S:
                P.op("sp", lambda e, t0=t0: e.dma_start(out=xh[:, 1, :], in_=fm(nxt)[:, :, t0 + 1024], allow_slow_non_contiguous=True), [r_scr], [r_xh], c_xh)
            for sub in range(2):
                norm_tile(xt, r_xt, 512, n2g, lambda c, sub=sub: xn2[:, c, 1 + sub * 512:513 + sub * 512], r_xn2, rotf,
                          src_fn=lambda c, sub=sub: xt[:, c, sub * 512:(sub + 1) * 512])
            hb = rotf.next()
            for c in range(NCH):
                s_t, s_r, _ = sq.next()
                act(s_t[:, 0:2], xh[:, :, c], AF.Square, [r_xh], [s_r])
                mm(pb[hb][:, 0:2], ones_bf[:], s_t[:, 0:2], c == 0, c == NCH - 1, [s_r, r_const], [r_pb[hb]])
            rsqrt_chain(rstd[:, 0:2], pb[hb][:, 0:2], 1.0 / D, tmp512[:, 0:2], [r_pb[hb]], [r_rstd], r_tmp512)
            for c in range(NCH):
                stt(xn2[:, c, 0:1026:1025], xh[:, :, c], n2g[:, c:c + 1], rstd[:, 0:2], ALU.mult, ALU.mult,
                    [r_xh, r_rstd, r_const], [r_xn2])
            for (p0, p1) in PARTS:
                for i in range(p0, p1):
                    w_t, w_r, w_c = wup.next()
                    P.dma("pool", w_t[:], w_up_d[l, i], writes=[w_r], chan=w_c)
                    for sub in range(2):
                        c0 = sub * 512
                        a_t, a_r, _ = acc.next()
                        for gu in range(2):
                            b0, b1 = rotf.next(), rotf.next()
                            for kc in range(NCH):
                                mm(pb[b0][:, 0:258], w_t[:, kc, gu, :], xn2[:, kc, c0:c0 + 258], kc == 0, kc == NCH - 1,
                                   [r_xn2, w_r], [r_pb[b0]])
                                mm(pb[b1][:, 0:258], w_t[:, kc, gu, :], xn2[:, kc, c0 + 256:c0 + 514], kc == 0, kc == NCH - 1,
                                   [r_xn2, w_r], [r_pb[b1]])
                            ch = gu * NPAIR + i
                            for hf, bk in ((0, b0), (1, b1)):
                                dst = a_t[:, gu, hf * 256:(hf + 1) * 256]
                                act(dst, pb[bk][:, 1:257], AF.Identity, [r_pb[bk], r_const], [a_r],
                                    scale=cw[:, 1, ch:ch + 1], bias=cb[:, ch:ch + 1])
                                stt(dst, pb[bk][:, 0:256], cw[:, 0, ch:ch + 1], dst, ALU.mult, ALU.add,
                                    [r_pb[bk], r_const, a_r], [a_r])
                                stt(dst, pb[bk][:, 2:258], cw[:, 2, ch:ch + 1], dst, ALU.mult, ALU.add,
                                    [r_pb[bk], r_const, a_r], [a_r])
                        g_t, g_r, _ = sg.next()
                        act(g_t[:], a_t[:, 0, :], AF.Silu, [a_r], [g_r])
                        tt(hT[:, i - p0, c0:c0 + 512], g_t[:], a_t[:, 1, :], ALU.mult, [g_r, a_r], [r_hT])
                npp = p1 - p0
                for fp in range(8):
                    w_t, w_r, w_c = wdn.next()
                    P.dma("pool", w_t[:, 0:npp, :], w_dn_d[l, fp][:, p0:p1, :], writes=[w_r], chan=w_c)
                    for fi in range(2):
                        f = fp * 2 + fi
                        for sub in range(2):
                            c0 = sub * 512
                            bk = rotf.next()
                            for kc in range(npp):
                                mm(pb[bk][:], w_t[:, kc, fi * 128:(fi + 1) * 128], hT[:, kc, c0:c0 + 512], kc == 0, kc == npp - 1,
                                   [r_hT, w_r], [r_pb[bk]])
                            tt(xt[:, f, c0:c0 + 512], pb[bk][:], xt[:, f, c0:c0 + 512], ALU.add, [r_pb[bk], r_xt], [r_xt])
            store_ops.append(P.dma("sp", fm(dst_x)[:, :, t0:t0 + 1024], xt[:], reads=[r_xt], chan=c_xt))
        bar2 = P.op("sp", None, reads=[], writes=[])
        bar2.deps = set(store_ops)
        first_src = None

    if dbg:
        fin = P.op("sp", None, reads=[], writes=[])
        fin.deps = set(o for o in P.ops if o.chan is not None and o.eng == "sp")
    P.emit()
    return nc


def _consts(S):
    NB = S // 128
    inv = (10000.0 ** (-np.arange(0, HD, 2, dtype=np.float32) / HD)).astype(np.float32)
    ang = np.arange(S, dtype=np.float32)[:, None] * inv[None, :]
    cos, sin = np.cos(ang).astype(np.float32), np.sin(ang).astype(np.float32)
    cc = np.concatenate([cos, cos], axis=1)
    ss = np.concatenate([-sin, sin], axis=1)
    cc = np.ascontiguousarray(np.broadcast_to(cc.reshape(NB, 128, 1, 128), (NB, 128, 4, 128)))
    ss = np.ascontiguousarray(np.broadcast_to(ss.reshape(NB, 128, 1, 128), (NB, 128, 4, 128)))
    kk = np.arange(128)[:, None]
    qq = np.arange(128)[None, :]
    mprev = (kk >= qq).astype(np.float32)
    mnext = (kk <= qq).astype(np.float32)
    mask = np.stack([np.tile(mprev, (1, 4)), np.tile(mnext, (1, 4))]).astype(ml_dtypes.bfloat16)
    ones = np.ones((128, 128), ml_dtypes.bfloat16)
    idb = np.eye(128, dtype=np.float32).astype(ml_dtypes.bfloat16)
    return dict(c_cc=cc, c_ss=ss, c_mask=mask, c_ones=ones, c_idb=idb)


def _layout_weights(inp, NL):
    f = np.float32
    a = lambda v: np.ascontiguousarray(np.asarray(v, dtype=f))
    w = {}
    w["w_in"] = a(np.asarray(inp["w_in"]).reshape(NL, NCH, 128, INW).transpose(0, 2, 1, 3))
    w["w_o"] = a(np.asarray(inp["w_o"]).reshape(NL, NCH, 128, NCH, 128).transpose(0, 3, 2, 1, 4))
    wu = np.asarray(inp["w_up"]).reshape(NL, NCH, 128, 2, NPAIR, 128)
    w["w_up"] = a(wu.transpose(0, 4, 2, 1, 3, 5))
    wd = np.asarray(inp["w_down"]).reshape(NL, NPAIR, 128, 8, 256)
    w["w_dn"] = a(wd.transpose(0, 3, 2, 1, 4))
    w["n1g"] = a(np.asarray(inp["norm1_g"]).reshape(NL, NCH, 128).transpose(0, 2, 1))
    w["n2g"] = a(np.asarray(inp["norm2_g"]).reshape(NL, NCH, 128).transpose(0, 2, 1))
    w["aog"] = a(np.asarray(inp["attn_out_g"]).reshape(NL, 8, 128).transpose(0, 2, 1))
    w["sog"] = a(np.asarray(inp["sgu_out_g"]).reshape(NL, 8, 128).transpose(0, 2, 1))
    w["cw"] = a(np.asarray(inp["conv_w"]).reshape(NL, 3, 88, 128).transpose(0, 3, 1, 2))
    w["cb"] = a(np.asarray(inp["conv_b"]).reshape(NL, 88, 128).transpose(0, 2, 1))
    w["qg"] = a(inp["q_norm_g"])
    w["kg"] = a(inp["k_norm_g"])
    w["sink"] = a(inp["sink"])
    w["lng"] = a(inp["sgu_ln_g"])
    w["lnb"] = a(inp["sgu_ln_b"])
    w["bs"] = a(np.asarray(inp["b_s"]).reshape(NL, 1024))
    w["wsT"] = a(np.asarray(inp["w_s"]).transpose(0, 3, 1, 2))
    return w


_CACHE = {}


def _get_nc(S, NL, dbg=False):
    key = (S, NL, dbg)
    if key not in _CACHE:
        _CACHE[key] = build(S, NL, dbg)
    return _CACHE[key]


def run_layers(xs, inp, NL, dbg=False):
    S = xs[0].shape[0]
    w = _layout_weights(inp, NL)
    w.update(_consts(S))
    nc = build(S, NL, dbg)
    in_maps = []
    for x in xs:
        m = dict(w)
        m["xT"] = np.ascontiguousarray(np.asarray(x, dtype=np.float32).T)
        in_maps.append(m)
    res = run_bass_kernel_spmd(nc, in_maps, core_ids=list(range(len(xs))))
    return res


def kernel(**inputs):
    x = np.asarray(inputs["x"], dtype=np.float32)
    B = x.shape[0]
    NL = np.asarray(inputs["w_in"]).shape[0]
    res = run_layers([x[b] for b in range(B)], inputs, NL)
    out = np.stack([np.ascontiguousarray(r["outT"].T) for r in res.results]).astype(np.float32)
    return out
```

```python
import contextlib
import numpy as np
import ml_dtypes
import concourse.bass as bass
import concourse.mybir as mybir
from concourse.bass_utils import run_bass_kernel_spmd

F32 = mybir.dt.float32
BF16 = mybir.dt.bfloat16
ALU = mybir.AluOpType
AF = mybir.ActivationFunctionType

D = 2048
NCH = 16
DFF = 5632
NPAIR = 44
HD = 128
NQ = 8
NKV = 2
INW = 3584
EPS = 1e-6
SB_BASE = 16512
SB_CAP = 212863

ENGS = ("pe", "act", "dve", "pool", "sp")
SAME_ENG_DIST = 3
import os
DBGJ = int(os.environ.get("DBGJ", "0"))


class Res:
    __slots__ = ("name", "w", "r")

    def __init__(self, name):
        self.name = name
        self.w = None
        self.r = []


class Chan:
    __slots__ = ("name", "sem", "count")

    def __init__(self, name):
        self.name = name
        self.sem = None
        self.count = 0


class Op:
    __slots__ = ("eng", "fn", "deps", "raw", "pos", "sig", "signo", "chan", "chan_val", "gidx")


class Prog:
    def __init__(self, nc):
        self.nc = nc
        self.ops = []
        self.streams = {e: [] for e in ENGS}
        self.chans = []

    def chan(self, name):
        c = Chan(name)
        self.chans.append(c)
        return c

    def op(self, eng, fn, reads=(), writes=(), chan=None):
        o = Op()
        o.eng = eng
        o.fn = fn
        o.chan = chan
        o.sig = False
        o.signo = None
        o.chan_val = None
        deps = set()
        for r in reads:
            if r.w is not None:
                deps.add(r.w)
        o.raw = set(deps)
        for r in writes:
            if r.w is not None:
                deps.add(r.w)
            deps.update(r.r)
        deps.discard(o)
        while any(d.fn is None for d in deps):
            nd = set()
            for d in deps:
                if d.fn is None:
                    nd.update(d.deps)
                else:
                    nd.add(d)
            deps = nd
        o.deps = deps
        for r in reads:
            r.r.append(o)
        for r in writes:
            r.w = o
            r.r = []
        o.pos = len(self.streams[eng])
        o.gidx = len(self.ops)
        self.streams[eng].append(o)
        self.ops.append(o)
        if chan is not None:
            chan.count += 16
            o.chan_val = chan.count
        return o

    def dma(self, eng, out, in_, reads=(), writes=(), chan=None):
        assert chan is not None
        return self.op(eng, lambda e: e.dma_start(out=out, in_=in_), reads, writes, chan)

    def _needs_wait(self, x, c):
        if c.chan is not None:
            return True
        if c.eng != x.eng:
            return True
        if c.eng == "pe":
            return False
        return True

    def emit(self):
        nc = self.nc
        for x in self.ops:
            for c in x.deps:
                if c.chan is None and self._needs_wait(x, c):
                    c.sig = True
        for e in ENGS:
            n = 0
            for o in self.streams[e]:
                if o.sig:
                    n += 1
                    o.signo = n
        with contextlib.ExitStack() as st:
            esem = {e: st.enter_context(nc.semaphore("s_" + e)) for e in ENGS}
            for ci, c in enumerate(self.chans):
                if c.count:
                    c.sem = st.enter_context(nc.semaphore(f"c{ci}_" + c.name))
            block = st.enter_context(nc.Block())

            def run(ename):
                def body(eng):
                    seen_e = {e: 0 for e in ENGS}
                    seen_c = {}
                    for o in self.streams[ename]:
                        need_c = {}
                        need_e = {}
                        for c in o.deps:
                            if not self._needs_wait(o, c):
                                continue
                            if c.chan is not None:
                                if c.chan_val > need_c.get(c.chan, 0):
                                    need_c[c.chan] = c.chan_val
                            else:
                                if c.signo > need_e.get(c.eng, 0):
                                    need_e[c.eng] = c.signo
                        for ch, v in need_c.items():
                            if seen_c.get(ch, 0) >= v:
                                continue
                            eng.wait_ge(ch.sem, v)
                            seen_c[ch] = v
                        for en, v in need_e.items():
                            if seen_e[en] >= v:
                                continue
                            eng.wait_ge(esem[en], v)
                            seen_e[en] = v
                        if o.fn is None:
                            continue
                        ins = o.fn(eng)
                        if o.chan is not None:
                            ins.then_inc(o.chan.sem, 16)
                        elif o.sig:
                            ins.then_inc(esem[ename], 1)

                return body

            block.tensor(run("pe"))
            block.scalar(run("act"))
            block.vector(run("dve"))
            block.gpsimd(run("pool"))
            block.sync(run("sp"))


class Slots:
    def __init__(self, P, alloc, name, n, shape, dtype):
        self.t = [alloc(f"{name}{i}", shape, dtype) for i in range(n)]
        self.r = [Res(f"{name}{i}") for i in range(n)]
        self.c = [P.chan(f"{name}{i}") for i in range(n)]
        self.c2 = [P.chan(f"{name}s{i}") for i in range(n)]
        self.i = 0
        self.n = n
        self.free = list(range(n))

    def next(self):
        k = self.i % self.n
        self.i += 1
        return self.t[k], self.r[k], self.c[k]

    def can(self, m):
        return len(self.free) >= m

    def take(self, m):
        ks = [self.free.pop(0) for _ in range(m)]
        return [(k, self.t[k], self.r[k], self.c[k]) for k in ks]

    def give(self, items):
        for it in items:
            self.free.append(it[0])


def build(S, NL, dbg=False):
    assert S % 512 == 0
    NT = S // 512
    NB = S // 128
    nc = bass.Bass("TRN2", target_bir_lowering=False)
    P = Prog(nc)

    def din(name, shape, dt=F32):
        return nc.dram_tensor(name, list(shape), dt, kind="ExternalInput").ap()

    xT_in = din("xT", [D, S])
    w_in_d = din("w_in", [NL, 128, NCH, INW])
    w_o_d = din("w_o", [NL, NCH, 128, NCH, 128])
    w_up_d = din("w_up", [NL, NPAIR, 128, NCH, 2, 128])
    w_dn_d = din("w_dn", [NL, 8, 128, NPAIR, 256])
    n1g_d = din("n1g", [NL, 128, NCH])
    n2g_d = din("n2g", [NL, 128, NCH])
    aog_d = din("aog", [NL, 128, 8])
    sog_d = din("sog", [NL, 128, 8])
    cw_d = din("cw", [NL, 128, 3, 88])
    cb_d = din("cb", [NL, 128, 88])
    qg_d = din("qg", [NL, 128])
    kg_d = din("kg", [NL, 128])
    sink_d = din("sink", [NL, 8])
    lng_d = din("lng", [NL, 1024])
    lnb_d = din("lnb", [NL, 1024])
    bs_d = din("bs", [NL, 1024])
    ws_d = din("wsT", [NL, 128, 8, 128])
    c_cc = din("c_cc", [NB, 128, 4, 128])
    c_ss = din("c_ss", [NB, 128, 4, 128])
    c_mask = din("c_mask", [2, 128, 512], BF16)
    c_ones = din("c_ones", [128, 128], BF16)
    c_idb = din("c_idb", [128, 128], BF16)
    outT = nc.dram_tensor("outT", [D, S], F32, kind="ExternalOutput").ap()
    xa = nc.dram_tensor("xa_scr", [D, S], F32).ap()
    xb = nc.dram_tensor("xb_scr", [D, S], F32).ap()
    xn_scr = nc.dram_tensor("xn_scr", [128, S // 512, NCH, 512], BF16).ap()
    wsc_in = nc.dram_tensor("wsc_in", [128, NCH, INW], BF16).ap()
    wsc_o = nc.dram_tensor("wsc_o", [NCH, 128, NCH, 128], BF16).ap()
    dbg_out = {}
    if dbg:
        for nm, shp in (("d_kt", [128, NKV, S]), ("d_v", [128, NB, 256]),
                        ("d_mix", [128, NCH, 512]), ("d_xb", [D, S])):
            dbg_out[nm] = nc.dram_tensor(nm, shp, F32, kind="ExternalOutput").ap()

    def fm(ap):
        return ap.rearrange("(c p) s -> p c s", p=128)

    class Arena:
        def __init__(self, base):
            self.off = base
            self.hi = base

        def __call__(self, name, shape, dtype):
            nb = int(np.prod(shape[1:])) * (2 if dtype == BF16 else 4)
            nb = (nb + 63) // 64 * 64
            t = nc.alloc_sbuf_tensor_at(name, list(shape), dtype, offset=self.off)
            self.off += nb
            self.hi = max(self.hi, self.off)
            assert self.off <= SB_BASE + SB_CAP, (name, self.off - SB_BASE)
            return t

    A = Arena(SB_BASE)
    fence_t = [None]
    ones_bf = A("ones_bf", [128, 128], BF16)
    id_bf = A("id_bf", [128, 128], BF16)
    mask4 = A("mask4", [128, 2, 512], BF16)
    eps_t = A("eps_t", [128, 1], F32)
    fence_t[0] = A("fence_t", [128, 1], F32)
    n1g = A("n1g", [128, NCH], F32)
    n2g = A("n2g", [128, NCH], F32)
    aosog = A("aosog", [128, 16], F32)
    cw = A("cw", [128, 3, 88], F32)
    cb = A("cb", [128, 88], F32)
    qg_bc = A("qg_bc", [128, 128], F32)
    kg_bc = A("kg_bc", [128, 128], F32)
    esink = A("esink", [128, 8], F32)
    lng_bc = A("lng_bc", [128, 1024], F32)
    lnb_bc = A("lnb_bc", [128, 1024], F32)
    bs_bc = A("bs_bc", [128, 1024], F32)
    wsT = A("wsT", [128, 8, 128], BF16)
    r_const = Res("const")
    c_const = P.chan("const")
    c_ws = P.chan("ws")
    rstd = A("rstd", [128, 512], F32)
    r_rstd = Res("rstd")
    tmp512 = A("tmp512", [128, 512], F32)
    r_tmp512 = Res("tmp512")
    sq = Slots(P, A, "sq", 5, [128, 512], BF16)
    xo = Slots(P, A, "xo", 2, [128, 512], F32)
    phase_base = A.off

    pb = [nc.alloc_psum_tensor(f"pb{i}", [128, 512], F32) for i in range(8)]
    pbb = [t.bitcast(BF16) for t in pb]
    r_pb = [Res(f"pb{i}") for i in range(8)]

    class Rot:
        def __init__(self, ids):
            self.ids = ids
            self.i = 0

        def next(self):
            k = self.ids[self.i % len(self.ids)]
            self.i += 1
            return k

    def mm(out, lhsT, rhs, start, stop, reads, writes, skip=False):
        if skip:
            P.op("pe", lambda e: e.matmul(out, lhsT, rhs, start=start, stop=stop, skip_group_check=True), reads, writes)
        else:
            P.op("pe", lambda e: e.matmul(out, lhsT, rhs, start=start, stop=stop), reads, writes)

    r_fence = Res("fence")

    def fence(reads, writes):
        P.op("dve", lambda e: e.memset(fence_t[0][:], 0.0), reads, list(writes) + [r_fence])

    def act(out, in_, func, reads, writes, **kw):
        P.op("act", lambda e: e.activation(out, in_, func, **kw), reads, writes)

    def tt(out, a, b, op, reads, writes, eng="dve"):
        P.op(eng, lambda e: e.tensor_tensor(out, a, b, op), reads, writes)

    def stt(out, in0, scalar, in1, op0, op1, reads, writes):
        P.op("dve", lambda e: e.scalar_tensor_tensor(out, in0, scalar, in1, op0, op1), reads, writes)

    def ts(out, in0, s1, s2, op0, op1, reads, writes):
        P.op("dve", lambda e: e.tensor_scalar(out, in0, s1, s2, op0, op1), reads, writes)

    def recip(out, in_, reads, writes):
        P.op("dve", lambda e: e.reciprocal(out, in_), reads, writes)

    def rsqrt_chain(out, in_, scale, tmp, reads, writes, r_tmp):
        act(tmp, in_, AF.Sqrt, reads + [r_const], [r_tmp], bias=eps_t[:, 0:1], scale=scale)
        recip(out, tmp, [r_tmp], writes)

    P.dma("sp", ones_bf[:], c_ones, writes=[r_const], chan=c_const)
    P.dma("sp", id_bf[:], c_idb, writes=[r_const], chan=c_const)
    P.dma("sp", mask4[:, 0, :], c_mask[0], writes=[r_const], chan=c_const)
    P.dma("sp", mask4[:, 1, :], c_mask[1], writes=[r_const], chan=c_const)
    P.op("dve", lambda e: e.memset(eps_t[:], EPS), [], [r_const])

    def norm_tile(src_t, r_src, ncols, gains, dst_fn, r_dst, rot, src_fn=None):
        if src_fn is None:
            src_fn = lambda c: src_t[:, c, 0:ncols]
        bk = rot.next()
        for c in range(NCH):
            s_t, s_r, _ = sq.next()
            act(s_t[:, 0:ncols], src_fn(c), AF.Square, [r_src], [s_r])
            mm(pb[bk][:, 0:ncols], ones_bf[:], s_t[:, 0:ncols], c == 0, c == NCH - 1,
               [s_r, r_const], [r_pb[bk]])
        rsqrt_chain(rstd[:, 0:ncols], pb[bk][:, 0:ncols], 1.0 / D, tmp512[:, 0:ncols],
                    [r_pb[bk]], [r_rstd], r_tmp512)
        for c in range(NCH):
            stt(dst_fn(c), src_fn(c), gains[:, c:c + 1], rstd[:, 0:ncols], ALU.mult, ALU.mult,
                [r_src, r_rstd, r_const], [r_dst])


    def run_chains(gens, width):
        active = []
        it = iter(gens)
        while True:
            while len(active) < width:
                g = next(it, None)
                if g is None:
                    break
                active.append(g)
            if not active:
                break
            for g in list(active):
                try:
                    next(g)
                except StopIteration:
                    active.remove(g)

    A.off = phase_base
    xn1 = A("xn1", [128, NCH, 512], BF16)
    r_xn1 = Res("xn1")
    KT = A("KT", [128, NKV, S], BF16)
    r_KT = Res("KT")
    Vt = A("Vt", [128, NB, 256], BF16)
    r_V = Res("V")
    _o = A.off
    xt_m = A("xt_m", [128, NCH, 512], F32)
    r_xtm = Res("xt_m")
    c_xtm = P.chan("xt_m")
    c_xns = P.chan("xns")
    A.off = _o
    QT = A("QT", [128, 4, 1024], BF16)
    r_QT = [Res(f"QT{i}") for i in range(4)]
    vn = A("vn", [128, 4, 1024], BF16)
    r_vn = [Res(f"vn{i}") for i in range(4)]
    mixT = A("mixT", [128, NCH, 512], BF16)
    r_mix = [Res(f"mix{i}") for i in range(NCH)]
    wbig = Slots(P, A, "wbig", 2, [128, NCH, 512], BF16)
    wsm = Slots(P, A, "wsm", 3, [128, NCH, 128], BF16)
    xch = Slots(P, A, "xch", 3, [128, 512], F32)
    ropet = A("ropet", [128, 4, 2, 128], F32)
    r_rope = Res("rope")
    c_rope = P.chan("rope")
    qn = Slots(P, A, "qn", 4, [128, 512], F32)
    rbs = Slots(P, A, "rb", 4, [128, 512], F32)
    qr = Slots(P, A, "qr", 4, [128, 512], BF16)
    st8 = Slots(P, A, "st8", 4, [128, 16], F32)
    junk = Slots(P, A, "junk", 2, [128, 1024], BF16)
    pt = Slots(P, A, "pt", 6, [128, 512], BF16)
    den = Slots(P, A, "den", 2, [128, 512], F32)
    usb = Slots(P, A, "usb", 3, [128, 512], F32)
    gel = Slots(P, A, "gel", 2, [128, 1024], F32)
    fsb = Slots(P, A, "fsb", 2, [128, 512], F32)
    statA = A("statA", [128, 512], F32)
    r_statA = Res("statA")
    statS = A("statS", [128, 512], F32)
    r_statS = Res("statS")
    mixer_res = [r_xn1, r_KT, r_V, r_statA, r_statS, r_xtm, r_rope] + r_QT + r_vn + r_mix
    for sl in (wbig, wsm, xch, qn, rbs, qr, st8, junk, pt, den, usb, gel, fsb):
        mixer_res += sl.r
    if dbg:
        dtmp = Slots(P, A, "dtmp", 1, [128, 2048], F32)
        d_t, d_r, d_c = dtmp.t[0], dtmp.r[0], dtmp.c[0]
        mixer_res.append(d_r)
    mixer_hi = A.off


    A.off = phase_base
    xt = A("xt", [128, NCH, 1024], F32)
    r_xt = Res("xt")
    c_xt = P.chan("xt")
    xn2 = A("xn2", [128, NCH, 1026], BF16)
    r_xn2 = Res("xn2")
    xh = A("xh", [128, 2, NCH], F32)
    r_xh = Res("xh")
    c_xh = P.chan("xh")
    PARTS = [(0, 15), (15, 30), (30, 44)]
    hT = A("hT", [128, 15, 1024], BF16)
    r_hT = Res("hT")
    wup = Slots(P, A, "wup", 2, [128, NCH, 2, 128], BF16)
    wdn = Slots(P, A, "wdn", 2, [128, 15, 256], BF16)
    acc = Slots(P, A, "acc", 2, [128, 2, 512], F32)
    sg = Slots(P, A, "sg", 2, [128, 512], F32)
    ffn_res = [r_xt, r_xn2, r_xh, r_hT] + wup.r + wdn.r + acc.r + sg.r


    rot = Rot([0, 1, 2, 3, 4, 5, 6, 7])
    rotf = Rot([0, 1, 2, 3, 4, 5, 6, 7])
    cc1 = c_cc[:, :, 0, :]
    ss1 = c_ss[:, :, 0, :]

    class BankPool:
        def __init__(self, ids):
            self.free = list(ids)

        def can(self, m):
            return len(self.free) >= m

        def take(self, m):
            return [self.free.pop(0) for _ in range(m)]

        def give(self, ks):
            self.free.extend(ks)

    banks = BankPool(range(8))

    def acquire(reqs):
        while not all(p.can(m) for p, m in reqs):
            yield None
        yield [p.take(m) for p, m in reqs]

    def with_res(reqs, body):
        def gen():
            got = None
            for got in acquire(reqs):
                if got is None:
                    yield
            try:
                yield from body(*got)
            finally:
                for (p, m), g_ in zip(reqs, got):
                    p.give(g_)
        return gen()

    r_wsc = {}

    def load_w(pool_slots, slot, key, src_f32, scr, first):
        k, w_t, w_r, w_c = slot
        if key not in r_wsc:
            r_wsc[key] = Res("wsc" + str(key))
        if first:
            P.dma("pool", w_t[:], src_f32, writes=[w_r], chan=w_c)
            P.dma("sp", scr, w_t[:], reads=[w_r], writes=[r_wsc[key]], chan=pool_slots.c2[k])
        else:
            P.dma("pool", w_t[:], scr, reads=[r_wsc[key]], writes=[w_r], chan=w_c)

    def norm_stream(src, t0, gains, dst_fn, r_dst):
        bk = rot.next()
        for c in range(NCH):
            x_t, x_r, x_c = xch.next()
            P.dma("sp", x_t[:], fm(src)[:, c, t0:t0 + 512], writes=[x_r], chan=x_c)
            s_t, s_r, _ = sq.next()
            act(s_t[:], x_t[:], AF.Square, [x_r], [s_r])
            mm(pb[bk][:], ones_bf[:], s_t[:], c == 0, c == NCH - 1, [s_r, r_const], [r_pb[bk]])
        rsqrt_chain(rstd[:], pb[bk][:], 1.0 / D, tmp512[:], [r_pb[bk]], [r_rstd], r_tmp512)
        for c in range(NCH):
            x_t, x_r, x_c = xch.next()
            P.dma("sp", x_t[:], fm(src)[:, c, t0:t0 + 512], writes=[x_r], chan=x_c)
            stt(dst_fn(c), x_t[:], gains[:, c:c + 1], rstd[:], ALU.mult, ALU.mult, [x_r, r_rstd, r_const], [r_dst])

    QK_REQS = [(banks, 1), (st8, 1), (qn, 1), (rbs, 1), (qr, 1)]

    def load_rope(jt):
        P.dma("sp", ropet[:, :, 0, :], cc1[4 * jt:4 * jt + 4].rearrange("b p d -> p b d"), writes=[r_rope], chan=c_rope)
        P.dma("sp", ropet[:, :, 1, :], ss1[4 * jt:4 * jt + 4].rearrange("b p d -> p b d"), writes=[r_rope], chan=c_rope)

    def qk_body(nheads, g_bc, blk, proj_fn, dst_fn):
        def body(bk_, s8_, qn_, rb_, qr_):
            bk = bk_[0]
            rc_t, rc_r = ropet[:, blk % 4], r_rope
            _, s8, s8_r, _ = s8_[0]
            _, q_n, qn_r, _ = qn_[0]
            jk, jk_r = q_n, qn_r
            _, rb, rb_r, _ = rb_[0]
            _, q_t, q_r, _ = qr_[0]
            W = nheads * 128
            proj_fn(bk)
            yield
            for h in range(nheads):
                act(jk[:, h * 128:(h + 1) * 128], pb[bk][:, h * 128:(h + 1) * 128], AF.Square,
                    [r_pb[bk]], [jk_r, s8_r], accum_out=s8[:, h:h + 1])
            yield
            act(s8[:, 4:4 + nheads], s8[:, 0:nheads], AF.Sqrt, [s8_r, r_const], [s8_r], bias=eps_t[:, 0:1], scale=1.0 / HD)
            yield
            recip(s8[:, 8:8 + nheads], s8[:, 4:4 + nheads], [s8_r], [s8_r])
            yield
            for h in range(nheads):
                stt(q_n[:, h * 128:(h + 1) * 128], pb[bk][:, h * 128:(h + 1) * 128], s8[:, 8 + h:9 + h], g_bc[:],
                    ALU.mult, ALU.mult, [r_pb[bk], s8_r, r_const], [qn_r])
            yield
            qn3 = q_n[:, 0:W].rearrange("p (h d) -> p h d", d=128)
            rb3 = rb[:, 0:W].rearrange("p (h d) -> p h d", d=128)
            cc3 = rc_t[:, 0:1, :].broadcast_to([128, nheads, 128])
            ssa = rc_t[:, 1:2, 0:64].broadcast_to([128, nheads, 64])
            ssb = rc_t[:, 1:2, 64:128].broadcast_to([128, nheads, 64])
            tt(rb3[:, :, 0:64], qn3[:, :, 64:128], ssa, ALU.mult, [qn_r, rc_r], [rb_r])
            tt(rb3[:, :, 64:128], qn3[:, :, 0:64], ssb, ALU.mult, [qn_r, rc_r], [rb_r])
            yield
            tt(qn3, qn3, cc3, ALU.mult, [qn_r, rc_r], [qn_r])
            yield
            tt(q_t[:, 0:W], q_n[:, 0:W], rb[:, 0:W], ALU.add, [qn_r, rb_r], [q_r])
            yield
            for h in range(nheads):
                P.op("pe", lambda e, h=h: e.transpose(pbb[bk][:, h * 128:(h + 1) * 128], q_t[:, h * 128:(h + 1) * 128], id_bf[:]),
                     [q_r, r_const], [r_pb[bk]])
            yield
            dst_fn(bk)
        return body


    cur, nxt = xa, xb
    first_src = xT_in
    prev_res = []

    for l in range(NL):
        fence([], [r_const])
        r_ci = []
        for dst, src in ((n1g[:], n1g_d[l]), (n2g[:], n2g_d[l]), (aosog[:, 0:8], aog_d[l]), (aosog[:, 8:16], sog_d[l]),
                         (cw[:], cw_d[l]), (cb[:], cb_d[l]),
                         (qg_bc[:], qg_d[l:l + 1, :].partition_broadcast(128)),
                         (kg_bc[:], kg_d[l:l + 1, :].partition_broadcast(128)),
                         (esink[:], sink_d[l:l + 1, :].partition_broadcast(128)),
                         (lng_bc[:], lng_d[l:l + 1, :].partition_broadcast(128)),
                         (lnb_bc[:], lnb_d[l:l + 1, :].partition_broadcast(128)),
                         (bs_bc[:], bs_d[l:l + 1, :].partition_broadcast(128))):
            r_ci.append(Res("ci"))
            P.dma("sp", dst, src, reads=[r_const], writes=[r_ci[-1]], chan=c_const)
        r_ci.append(Res("ciw"))
        P.dma("pool", wsT[:], ws_d[l], reads=[r_const], writes=[r_ci[-1]], chan=c_ws)
        fence(r_ci, [r_const])
        act(esink[:], esink[:], AF.Exp, [r_const], [r_const])

        src_x = first_src if l == 0 else cur

        fence([], prev_res + mixer_res)

        wkv_t, wkv_r, wkv_c = wbig.next()
        P.dma("pool", wkv_t[:], w_in_d[l][:, :, 1024:1536], writes=[wkv_r], chan=wkv_c)

        def kv_chain(j, bi):
            blk = j * 4 + bi

            def proj(bk):
                for kc in range(NCH):
                    mm(pb[bk][:], xn1[:, kc, bi * 128:(bi + 1) * 128], wkv_t[:, kc, :], kc == 0, kc == NCH - 1,
                       [r_xn1, wkv_r], [r_pb[bk]])
                P.op("act", lambda e: e.activation(Vt[:, blk, :], pb[bk][:, 256:512], AF.Copy), [r_pb[bk]], [r_V])

            def kdst(tb):
                for h in range(NKV):
                    P.op("act", lambda e, h=h: e.activation(KT[:, h, blk * 128:(blk + 1) * 128],
                                                            pbb[tb][:, h * 128:(h + 1) * 128], AF.Copy),
                         [r_pb[tb]], [r_KT])
            return with_res(QK_REQS, qk_body(NKV, kg_bc, blk, proj, kdst))

        r_xns = [Res(f"xns{j}") for j in range(NT)]
        P.dma("sp", xt_m[:], fm(src_x)[:, :, 0:512], writes=[r_xtm], chan=c_xtm)
        for j in range(NT):
            load_rope(j)
            norm_tile(xt_m, r_xtm, 512, n1g, lambda c: xn1[:, c, :], r_xn1, rot)
            P.dma("sp", xn_scr[:, j], xn1[:], reads=[r_xn1], writes=[r_xns[j]], chan=c_xns)
            if j + 1 < NT:
                P.dma("sp", xt_m[:], fm(src_x)[:, :, (j + 1) * 512:(j + 2) * 512], writes=[r_xtm], chan=c_xtm)
            run_chains([kv_chain(j, bi) for bi in range(4)], 4)
        fence([], [r_xtm] + r_QT + r_vn + r_mix)
        P.dma("sp", xn1[:], xn_scr[:, 0], reads=r_xns, writes=[r_xn1], chan=c_xns)

        if dbg and l == 0:
            for h in range(NKV):
                P.op("dve", lambda e, h=h: e.tensor_copy(d_t[:, 0:S], KT[:, h, :]), [r_KT], [d_r])
                P.dma("sp", dbg_out["d_kt"][:, h, :], d_t[:, 0:S], reads=[d_r], chan=d_c)
            for blk in range(NB):
                P.op("dve", lambda e, blk=blk: e.tensor_copy(d_t[:, 0:256], Vt[:, blk, :]), [r_V], [d_r])
                P.dma("sp", dbg_out["d_v"][:, blk, :], d_t[:, 0:256], reads=[d_r], chan=d_c)

        for j in range(NT):
            t0 = j * 512
            P.op("dve", lambda e: e.memset(statA[:], 0.0), [], [r_statA])
            P.op("dve", lambda e: e.memset(statS[:], 0.0), [], [r_statS])

            def q_chain(g, bi, wq_t, wq_r, j=j):
                blk = j * 4 + bi

                def proj(bk):
                    for kc in range(NCH):
                        mm(pb[bk][:], xn1[:, kc, bi * 128:(bi + 1) * 128], wq_t[:, kc, :], kc == 0, kc == NCH - 1,
                           [r_xn1, wq_r], [r_pb[bk]])

                def qdst(tb):
                    P.op("act", lambda e: e.activation(QT[:, bi, g * 512:(g + 1) * 512], pbb[tb][:, 0:512], AF.Copy),
                         [r_pb[tb]], [r_QT[bi]])
                return with_res(QK_REQS, qk_body(4, qg_bc, blk, proj, qdst))

            def att_chain(g, bi, j=j):
                blk = j * 4 + bi
                kbs = [kb for kb in (blk - 1, blk, blk + 1) if 0 <= kb < NB]

                def body(bk_, pt_, dn_, us_, sq_):
                    sbs = bk_[0:3]
                    ob, db = bk_[0], bk_[1]
                    _, d_n, dn_r, _ = dn_[0]
                    _, u_t, u_r, _ = us_[0]
                    _, s_t, s_r, _ = sq_[0]
                    for kb, sb_ in zip(kbs, sbs):
                        mm(pb[sb_][:], KT[:, g, kb * 128:(kb + 1) * 128], QT[:, bi, g * 512:(g + 1) * 512], True, True,
                           [r_KT, r_QT[bi]], [r_pb[sb_]])
                    yield
                    pts = []
                    for i, (kb, sb_) in enumerate(zip(kbs, sbs)):
                        _, p_t, p_r, _ = pt_[i]
                        act(p_t[:], pb[sb_][:], AF.Exp, [r_pb[sb_]], [p_r], scale=float(HD) ** -0.5)
                        pts.append((p_t, p_r, kb))
                    yield
                    for p_t, p_r, kb in pts:
                        if kb != blk:
                            mi = 0 if kb < blk else 1
                            tt(p_t[:], p_t[:], mask4[:, mi, :], ALU.mult, [p_r, r_const], [p_r])
                    yield
                    for i, (p_t, p_r, kb) in enumerate(pts):
                        mm(pb[ob][:], Vt[:, kb, g * 128:(g + 1) * 128], p_t[:], i == 0, i == len(pts) - 1,
                           [r_V, p_r], [r_pb[ob]])
                    for i, (p_t, p_r, kb) in enumerate(pts):
                        mm(pb[db][:], ones_bf[:], p_t[:], i == 0, i == len(pts) - 1, [r_const, p_r], [r_pb[db]])
                    yield
                    for h in range(4):
                        ts(d_n[:, h * 128:(h + 1) * 128], pb[db][:, h * 128:(h + 1) * 128],
                           esink[:, g * 4 + h:g * 4 + h + 1], None, ALU.add, ALU.bypass, [r_pb[db], r_const], [dn_r])
                    yield
                    recip(d_n[:], d_n[:], [dn_r], [dn_r])
                    yield
                    tt(u_t[:], pb[ob][:], d_n[:], ALU.mult, [r_pb[ob], dn_r], [u_r])
                    yield
                    u3 = u_t[:].rearrange("p (h t) -> p h t", t=128)
                    P.op("act", lambda e: e.activation(mixT[:, g * 4:(g + 1) * 4, bi * 128:(bi + 1) * 128], u3, AF.Copy),
                         [u_r], [r_mix[g * 4 + h] for h in range(4)])
                    act(s_t[:], u_t[:], AF.Square, [u_r], [s_r])
                    yield
                    sb0 = sbs[2]
                    for h in range(4):
                        mm(pb[sb0][:, 0:128], ones_bf[:], s_t[:, h * 128:(h + 1) * 128], h == 0, h == 3,
                           [s_r, r_const], [r_pb[sb0]])
                    yield
                    tt(statA[:, bi * 128:(bi + 1) * 128], statA[:, bi * 128:(bi + 1) * 128], pb[sb0][:, 0:128], ALU.add,
                       [r_statA, r_pb[sb0]], [r_statA])
                return with_res([(banks, 3), (pt, 3), (den, 1), (usb, 1), (sq, 1)], body)

            def gv_chain(bi, wgv):
                def body(bk_, gl_, s8_, jk_):
                    bk = bk_[0]
                    _, g_t, g_r, _ = gl_[0]
                    _, s8, s8_r, _ = s8_[0]
                    _, jk, jk_r, _ = jk_[0]
                    for gg in range(2):
                        w_t, w_r = wgv[gg]
                        for kc in range(NCH):
                            mm(pb[bk][:], xn1[:, kc, bi * 128:(bi + 1) * 128], w_t[:, kc, :], kc == 0, kc == NCH - 1,
                               [r_xn1, w_r], [r_pb[bk]])
                        yield
                        act(g_t[:, gg * 512:(gg + 1) * 512], pb[bk][:], AF.Gelu_apprx_tanh, [r_pb[bk]], [g_r, s8_r],
                            accum_out=s8[:, gg:gg + 1])
                        yield
                    act(jk[:], g_t[:], AF.Square, [g_r], [jk_r, s8_r], accum_out=s8[:, 2:3])
                    yield
                    tt(s8[:, 3:4], s8[:, 0:1], s8[:, 1:2], ALU.add, [s8_r], [s8_r])
                    yield
                    ts(s8[:, 4:5], s8[:, 3:4], 1.0 / 1024, None, ALU.mult, ALU.bypass, [s8_r], [s8_r])
                    yield
                    tt(s8[:, 5:6], s8[:, 4:5], s8[:, 4:5], ALU.mult, [s8_r], [s8_r])
                    yield
                    stt(s8[:, 6:7], s8[:, 2:3], 1.0 / 1024, s8[:, 5:6], ALU.mult, ALU.subtract, [s8_r], [s8_r])
                    yield
                    act(s8[:, 9:10], s8[:, 6:7], AF.Sqrt, [s8_r, r_const], [s8_r], bias=eps_t[:, 0:1], scale=1.0)
                    yield
                    recip(s8[:, 7:8], s8[:, 9:10], [s8_r], [s8_r])
                    yield
                    ts(g_t[:], g_t[:], s8[:, 4:5], s8[:, 7:8], ALU.subtract, ALU.mult, [g_r, s8_r], [g_r])
                    yield
                    tt(g_t[:], g_t[:], lng_bc[:], ALU.mult, [g_r, r_const], [g_r])
                    yield
                    tt(vn[:, bi, :], g_t[:], lnb_bc[:], ALU.add, [g_r, r_const], [r_vn[bi]])
                return with_res([(banks, 1), (gel, 1), (st8, 1), (junk, 1)], body)

            def head_chain(h):
                def body(bk_, ws_, us_, fs_, sq_):
                    bk = bk_[0]
                    _, wu_t, wu_r, wu_c = ws_[0]
                    _, u_t, u_r, _ = us_[0]
                    _, f_t, f_r, _ = fs_[0]
                    _, s_t, s_r, _ = sq_[0]
                    load_w(wsm, ws_[0], ("gu", h), w_in_d[l][:, :, 1536 + h * 128:1536 + (h + 1) * 128],
                           wsc_in[:, :, 1536 + h * 128:1536 + (h + 1) * 128], j == 0)
                    for kc in range(NCH):
                        mm(pb[bk][:], wu_t[:, kc, :], xn1[:, kc, :], kc == 0, kc == NCH - 1, [r_xn1, wu_r], [r_pb[bk]])
                    yield
                    act(u_t[:], pb[bk][:], AF.Gelu_apprx_tanh, [r_pb[bk]], [u_r])
                    yield
                    for bi in range(4):
                        mm(pb[bk][:, bi * 128:(bi + 1) * 128], vn[:, bi, h * 128:(h + 1) * 128], wsT[:, h, :], True, True,
                           [r_vn[bi], r_const], [r_pb[bk]])
                    yield
                    f3 = f_t[:].rearrange("p (b t) -> p b t", t=128)
                    p3 = pb[bk][:].rearrange("p (b t) -> p b t", t=128)
                    b3 = bs_bc[:, h * 128:(h + 1) * 128].rearrange("p (o t) -> p o t", o=1).broadcast_to([128, 4, 128])
                    tt(f3, p3, b3, ALU.add, [r_pb[bk], r_const], [f_r])
                    yield
                    tt(f_t[:], f_t[:], u_t[:], ALU.mult, [f_r, u_r], [f_r])
                    yield
                    act(mixT[:, 8 + h, :], f_t[:], AF.Copy, [f_r], [r_mix[8 + h]])
                    act(s_t[:], f_t[:], AF.Square, [f_r], [s_r])
                    yield
                    mm(pb[bk][:], ones_bf[:], s_t[:], True, True, [s_r, r_const], [r_pb[bk]])
                    yield
                    tt(statS[:], statS[:], pb[bk][:], ALU.add, [r_statS, r_pb[bk]], [r_statS])
                return with_res([(banks, 1), (wsm, 1), (usb, 1), (fsb, 1), (sq, 1)], body)

            def wo_chain(f, t0=t0):
                def body(bk_, ws_, xc_):
                    bk = bk_[0]
                    _, wo_t, wo_r, wo_c = ws_[0]
                    _, x_t, x_r, x_c = xc_[0]
                    load_w(wsm, ws_[0], ("wo", f), w_o_d[l, f], wsc_o[f], j == 0)
                    P.dma("sp", x_t[:], fm(src_x)[:, f, t0:t0 + 512], writes=[x_r], chan=x_c)
                    for kc in range(NCH):
                        mm(pb[bk][:], wo_t[:, kc, :], mixT[:, kc, :], kc == 0, kc == NCH - 1, [r_mix[kc], wo_r], [r_pb[bk]])
                    yield
                    tt(x_t[:], pb[bk][:], x_t[:], ALU.add, [r_pb[bk], x_r], [x_r])
                    yield
                    P.dma("sp", fm(nxt)[:, f, t0:t0 + 512], x_t[:], reads=[x_r], chan=x_c)
                    if dbg and l == 0:
                        P.dma("sp", fm(dbg_out["d_xb"])[:, f, t0:t0 + 512], x_t[:], reads=[x_r], chan=x_c)
                return with_res([(banks, 1), (wsm, 1), (xch, 1)], body)

            flags = {"gv": 0, "att": 0, "head": 0, "mn": 0, "xn": 0, "wq": 0}
            wq_hold = []

            def counted(gen, key):
                yield from gen
                flags[key] += 1

            def gated(cond, gen_fn):
                while not cond():
                    yield
                yield from gen_fn()

            def load_q_weights(jq):
                load_rope(jq)
                for g in range(NKV):
                    k = wbig.i % wbig.n
                    wq_t, wq_r, wq_c = wbig.next()
                    load_w(wbig, (k, wq_t, wq_r, wq_c), ("q", g), w_in_d[l][:, :, g * 512:(g + 1) * 512],
                           wsc_in[:, :, g * 512:(g + 1) * 512], jq == 0)
                    wq_hold.append((wq_t, wq_r))

            if j == 0:
                load_q_weights(0)
                run_chains([q_chain(g, bi, *wq_hold[g], j=0) for g in range(NKV) for bi in range(4)], 4)
                wq_hold.clear()
            wgv = []
            for gg in range(2):
                k = wbig.i % wbig.n
                w_t, w_r, w_c = wbig.next()
                load_w(wbig, (k, w_t, w_r, w_c), ("gv", gg), w_in_d[l][:, :, 2560 + gg * 512:2560 + (gg + 1) * 512],
                       wsc_in[:, :, 2560 + gg * 512:2560 + (gg + 1) * 512], j == 0)
                wgv.append((w_t, w_r))
            atts = [counted(att_chain(g, bi), "att") for g in range(NKV) for bi in range(4)]
            gvs = [counted(gv_chain(bi, wgv), "gv") for bi in range(4)]
            chains = []
            for i in range(4):
                chains += [gvs[i], atts[i]]
            chains += atts[4:]
            chains += [counted(gated(lambda: flags["gv"] == 4, lambda h=h: head_chain(h)), "head") for h in range(8)]

            def mnorm_body():
                for half, stt_t, stt_r in ((0, statA, r_statA), (1, statS, r_statS)):
                    rsqrt_chain(rstd[:], stt_t[:], 1.0 / 1024, tmp512[:], [stt_r], [r_rstd], r_tmp512)
                    yield
                    for c in range(8):
                        cc = half * 8 + c
                        stt(mixT[:, cc, :], mixT[:, cc, :], aosog[:, cc:cc + 1], rstd[:], ALU.mult, ALU.mult,
                            [r_mix[cc], r_rstd, r_const], [r_mix[cc]])
                        if c % 4 == 3:
                            yield
                if dbg and l == 0 and j == DBGJ:
                    for c in range(NCH):
                        P.op("dve", lambda e, c=c: e.tensor_copy(d_t[:, 0:512], mixT[:, c, :]), [r_mix[c]], [d_r])
                        P.dma("sp", dbg_out["d_mix"][:, c, :], d_t[:, 0:512], reads=[d_r], chan=d_c)
                flags["mn"] = 1

            def wq_body(jq=j + 1):
                load_q_weights(jq)
                flags["wq"] = 1
                yield

            def xn_body(jq=j + 1):
                P.dma("sp", xn1[:], xn_scr[:, jq], reads=r_xns, writes=[r_xn1], chan=c_xns)
                flags["xn"] = 1
                yield

            if j + 1 < NT:
                chains.append(gated(lambda: flags["gv"] == 4, wq_body))
                chains.append(gated(lambda: flags["gv"] == 4 and flags["head"] == 8, xn_body))
            chains.append(gated(lambda: flags["att"] == 8 and flags["head"] == 8, mnorm_body))
            wos = [gated(lambda: flags["mn"] == 1, lambda f=f: wo_chain(f)) for f in range(NCH)]
            if j + 1 < NT:
                nq = [gated(lambda: flags["att"] == 8 and flags["xn"] == 1 and flags["wq"] == 1,
                            lambda g=g, bi=bi: q_chain(g, bi, *wq_hold[g], j=j + 1))
                      for g in range(NKV) for bi in range(4)]
                for i in range(8):
                    chains += [wos[2 * i], nq[i], wos[2 * i + 1]]
            else:
                chains += wos
            run_chains(chains, 7)

        r_scr = Res("scr")
        bar = P.op("sp", None, reads=[], writes=[r_scr])
        bar.deps = set(o for o in P.ops if o.chan is not None and o.chan in xch.c)

        fence([], mixer_res + ffn_res)
        prev_res = ffn_res
        last_layer = (l == NL - 1)
        dst_x = outT if last_layer else cur
        store_ops = []
        for js in range(S // 1024):
            t0 = js * 1024
            P.dma("sp", xt[:], fm(nxt)[:, :, t0:t0 + 1024], reads=[r_scr], writes=[r_xt], chan=c_xt)
            P.op("dve", lambda e: e.memset(xh[:], 0.0), [], [r_xh])
            if t0 > 0:
                P.op("sp", lambda e, t0=t0: e.dma_start(out=xh[:, 0, :], in_=fm(nxt)[:, :, t0 - 1], allow_slow_non_contiguous=True), [r_scr], [r_xh], c_xh)
            if t0 + 1024 < S:
                P.op("sp", lambda e, t0=t0: e.dma_start(out=xh[:, 1, :], in_=fm(nxt)[:, :, t0 + 1024], allow_slow_non_contiguous=True), [r_scr], [r_xh], c_xh)
            for sub in range(2):
                norm_tile(xt, r_xt, 512, n2g, lambda c, sub=sub: xn2[:, c, 1 + sub * 512:513 + sub * 512], r_xn2, rotf,
                          src_fn=lambda c, sub=sub: xt[:, c, sub * 512:(sub + 1) * 512])
            hb = rotf.next()
            for c in range(NCH):
                s_t, s_r, _ = sq.next()
                act(s_t[:, 0:2], xh[:, :, c], AF.Square, [r_xh], [s_r])
                mm(pb[hb][:, 0:2], ones_bf[:], s_t[:, 0:2], c == 0, c == NCH - 1, [s_r, r_const], [r_pb[hb]])
            rsqrt_chain(rstd[:, 0:2], pb[hb][:, 0:2], 1.0 / D, tmp512[:, 0:2], [r_pb[hb]], [r_rstd], r_tmp512)
            for c in range(NCH):
                stt(xn2[:, c, 0:1026:1025], xh[:, :, c], n2g[:, c:c + 1], rstd[:, 0:2], ALU.mult, ALU.mult,
                    [r_xh, r_rstd, r_const], [r_xn2])
            for (p0, p1) in PARTS:
                for i in range(p0, p1):
                    w_t, w_r, w_c = wup.next()
                    P.dma("pool", w_t[:], w_up_d[l, i], writes=[w_r], chan=w_c)
                    for sub in range(2):
                        c0 = sub * 512
                        a_t, a_r, _ = acc.next()
                        for gu in range(2):
                            b0, b1 = rotf.next(), rotf.next()
                            for kc in range(NCH):
                                mm(pb[b0][:, 0:258], w_t[:, kc, gu, :], xn2[:, kc, c0:c0 + 258], kc == 0, kc == NCH - 1,
                                   [r_xn2, w_r], [r_pb[b0]])
                                mm(pb[b1][:, 0:258], w_t[:, kc, gu, :], xn2[:, kc, c0 + 256:c0 + 514], kc == 0, kc == NCH - 1,
                                   [r_xn2, w_r], [r_pb[b1]])
                            ch = gu * NPAIR + i
                            for hf, bk in ((0, b0), (1, b1)):
                                dst = a_t[:, gu, hf * 256:(hf + 1) * 256]
                                act(dst, pb[bk][:, 1:257], AF.Identity, [r_pb[bk], r_const], [a_r],
                                    scale=cw[:, 1, ch:ch + 1], bias=cb[:, ch:ch + 1])
                                stt(dst, pb[bk][:, 0:256], cw[:, 0, ch:ch + 1], dst, ALU.mult, ALU.add,
                                    [r_pb[bk], r_const, a_r], [a_r])
                                stt(dst, pb[bk][:, 2:258], cw[:, 2, ch:ch + 1], dst, ALU.mult, ALU.add,
                                    [r_pb[bk], r_const, a_r], [a_r])
                        g_t, g_r, _ = sg.next()
                        act(g_t[:], a_t[:, 0, :], AF.Silu, [a_r], [g_r])
                        tt(hT[:, i - p0, c0:c0 + 512], g_t[:], a_t[:, 1, :], ALU.mult, [g_r, a_r], [r_hT])
                npp = p1 - p0
                for fp in range(8):
                    w_t, w_r, w_c = wdn.next()
                    P.dma("pool", w_t[:, 0:npp, :], w_dn_d[l, fp][:, p0:p1, :], writes=[w_r], chan=w_c)
                    for fi in range(2):
                        f = fp * 2 + fi
                        for sub in range(2):
                            c0 = sub * 512
                            bk = rotf.next()
                            for kc in range(npp):
                                mm(pb[bk][:], w_t[:, kc, fi * 128:(fi + 1) * 128], hT[:, kc, c0:c0 + 512], kc == 0, kc == npp - 1,
                                   [r_hT, w_r], [r_pb[bk]])
                            tt(xt[:, f, c0:c0 + 512], pb[bk][:], xt[:, f, c0:c0 + 512], ALU.add, [r_pb[bk], r_xt], [r_xt])
            store_ops.append(P.dma("sp", fm(dst_x)[:, :, t0:t0 + 1024], xt[:], reads=[r_xt], chan=c_xt))
        bar2 = P.op("sp", None, reads=[], writes=[])
        bar2.deps = set(store_ops)
        first_src = None

    if dbg:
        fin = P.op("sp", None, reads=[], writes=[])
        fin.deps = set(o for o in P.ops if o.chan is not None and o.eng == "sp")
    P.emit()
    return nc


def _consts(S):
    NB = S // 128
    inv = (10000.0 ** (-np.arange(0, HD, 2, dtype=np.float32) / HD)).astype(np.float32)
    ang = np.arange(S, dtype=np.float32)[:, None] * inv[None, :]
    cos, sin = np.cos(ang).astype(np.float32), np.sin(ang).astype(np.float32)
    cc = np.concatenate([cos, cos], axis=1)
    ss = np.concatenate([-sin, sin], axis=1)
    cc = np.ascontiguousarray(np.broadcast_to(cc.reshape(NB, 128, 1, 128), (NB, 128, 4, 128)))
    ss = np.ascontiguousarray(np.broadcast_to(ss.reshape(NB, 128, 1, 128), (NB, 128, 4, 128)))
    kk = np.arange(128)[:, None]
    qq = np.arange(128)[None, :]
    mprev = (kk >= qq).astype(np.float32)
    mnext = (kk <= qq).astype(np.float32)
    mask = np.stack([np.tile(mprev, (1, 4)), np.tile(mnext, (1, 4))]).astype(ml_dtypes.bfloat16)
    ones = np.ones((128, 128), ml_dtypes.bfloat16)
    idb = np.eye(128, dtype=np.float32).astype(ml_dtypes.bfloat16)
    return dict(c_cc=cc, c_ss=ss, c_mask=mask, c_ones=ones, c_idb=idb)


def _layout_weights(inp, NL):
    f = np.float32
    a = lambda v: np.ascontiguousarray(np.asarray(v, dtype=f))
    w = {}
    w["w_in"] = a(np.asarray(inp["w_in"]).reshape(NL, NCH, 128, INW).transpose(0, 2, 1, 3))
    w["w_o"] = a(np.asarray(inp["w_o"]).reshape(NL, NCH, 128, NCH, 128).transpose(0, 3, 2, 1, 4))
    wu = np.asarray(inp["w_up"]).reshape(NL, NCH, 128, 2, NPAIR, 128)
    w["w_up"] = a(wu.transpose(0, 4, 2, 1, 3, 5))
    wd = np.asarray(inp["w_down"]).reshape(NL, NPAIR, 128, 8, 256)
    w["w_dn"] = a(wd.transpose(0, 3, 2, 1, 4))
    w["n1g"] = a(np.asarray(inp["norm1_g"]).reshape(NL, NCH, 128).transpose(0, 2, 1))
    w["n2g"] = a(np.asarray(inp["norm2_g"]).reshape(NL, NCH, 128).transpose(0, 2, 1))
    w["aog"] = a(np.asarray(inp["attn_out_g"]).reshape(NL, 8, 128).transpose(0, 2, 1))
    w["sog"] = a(np.asarray(inp["sgu_out_g"]).reshape(NL, 8, 128).transpose(0, 2, 1))
    w["cw"] = a(np.asarray(inp["conv_w"]).reshape(NL, 3, 88, 128).transpose(0, 3, 1, 2))
    w["cb"] = a(np.asarray(inp["conv_b"]).reshape(NL, 88, 128).transpose(0, 2, 1))
    w["qg"] = a(inp["q_norm_g"])
    w["kg"] = a(inp["k_norm_g"])
    w["sink"] = a(inp["sink"])
    w["lng"] = a(inp["sgu_ln_g"])
    w["lnb"] = a(inp["sgu_ln_b"])
    w["bs"] = a(np.asarray(inp["b_s"]).reshape(NL, 1024))
    w["wsT"] = a(np.asarray(inp["w_s"]).transpose(0, 3, 1, 2))
    return w


_CACHE = {}


def _get_nc(S, NL, dbg=False):
    key = (S, NL, dbg)
    if key not in _CACHE:
        _CACHE[key] = build(S, NL, dbg)
    return _CACHE[key]


def run_layers(xs, inp, NL, dbg=False):
    S = xs[0].shape[0]
    w = _layout_weights(inp, NL)
    w.update(_consts(S))
    nc = build(S, NL, dbg)
    in_maps = []
    for x in xs:
        m = dict(w)
        m["xT"] = np.ascontiguousarray(np.asarray(x, dtype=np.float32).T)
        in_maps.append(m)
    res = run_bass_kernel_spmd(nc, in_maps, core_ids=list(range(len(xs))))
    return res


def kernel(**inputs):
    x = np.asarray(inputs["x"], dtype=np.float32)
    B = x.shape[0]
    NL = np.asarray(inputs["w_in"]).shape[0]
    res = run_layers([x[b] for b in range(B)], inputs, NL)
    out = np.stack([np.ascontiguousarray(r["outT"].T) for r in res.results]).astype(np.float32)
    return out
```
